# Optimizing a Trainium2 kernel written in Bass

```python
import jax, jax.numpy as jnp
from jax import lax
import numpy as np

D_MODEL = 1024
BATCH = 1
SEQ = 16384
DEPTH = 4

HEAD_DIM = 64
ROPE_THETA = 10000.0
LN_EPS = 1e-5
NEG_INF = -1e30
Q_BLOCK = 128

NSA_HEADS = 8
NSA_GROUPS = 2
NSA_HPG = NSA_HEADS // NSA_GROUPS
NSA_Q_W = NSA_HEADS * HEAD_DIM
NSA_KV_W = NSA_GROUPS * HEAD_DIM
CMP_LEN = 32
CMP_STRIDE = 16
CMP_HIDDEN = 128
SLC_BLOCK = 64
SLC_RATIO = SLC_BLOCK // CMP_STRIDE
SLC_TOPK = 16
NSA_WINDOW = 512
FORCE_SCORE = 1e6

DIL_PAIRS = ((128, 1), (512, 4), (2048, 16))
DIL_HPG = 4
DIL_HEADS = DIL_HPG * 3
DIL_W = DIL_HEADS * HEAD_DIM
DIL_OUT_W = DIL_HPG * HEAD_DIM

SGU_CHUNK = 128
SGU_GROUPS = 4
SGU_GROUP_CH = 128
SGU_WIDTH = SGU_GROUPS * SGU_GROUP_CH

FFN_DIM = 2816

N_BRANCH = 3
IN_COLS = NSA_Q_W + 6 * NSA_KV_W + N_BRANCH * NSA_HEADS + 3 * DIL_W + 2 * SGU_WIDTH + N_BRANCH * D_MODEL

DEEPNORM_ALPHA = (2 * DEPTH) ** 0.25
DEEPNORM_BETA = (8 * DEPTH) ** -0.25

kernel_name = 'hybrid_nsa_dilated_sgu_macaron_deepnorm'


def layer_norm(x, g, b):
    xf = x.astype(jnp.float32)
    mu = jnp.mean(xf, axis=-1, keepdims=True)
    var = jnp.mean(jnp.square(xf - mu), axis=-1, keepdims=True)
    return ((xf - mu) * lax.rsqrt(var + LN_EPS) * g + b).astype(x.dtype)


def swiglu(x, w_gate, w_up, w_down):
    return (jax.nn.silu(x @ w_gate) * (x @ w_up)) @ w_down


def rope_tables(seq):
    inv = 1.0 / (ROPE_THETA ** (jnp.arange(0, HEAD_DIM, 2, dtype=jnp.float32) / HEAD_DIM))
    ang = jnp.arange(seq, dtype=jnp.float32)[:, None] * inv[None, :]
    return jnp.cos(ang), jnp.sin(ang)


def apply_rope(x, cos, sin):
    half = HEAD_DIM // 2
    shape = (1, x.shape[1]) + (1,) * (x.ndim - 3) + (half,)
    c = cos.reshape(shape).astype(x.dtype)
    s = sin.reshape(shape).astype(x.dtype)
    x1, x2 = x[..., :half], x[..., half:]
    return jnp.concatenate([x1 * c - x2 * s, x2 * c + x1 * s], axis=-1)


def masked_softmax(scores, mask):
    s = jnp.where(mask, scores.astype(jnp.float32), NEG_INF)
    m = jnp.max(s, axis=-1, keepdims=True)
    e = jnp.exp(s - m) * mask
    den = jnp.sum(e, axis=-1, keepdims=True)
    p = e / jnp.maximum(den, 1e-30)
    lse = m[..., 0] + jnp.log(jnp.maximum(den[..., 0], 1e-30))
    return p, lse


def compress_blocks(x, pos, w1, w2):
    B, S, G, Dh = x.shape
    n_cmp = S // CMP_STRIDE
    xp = jnp.pad(x, ((0, 0), (0, CMP_STRIDE), (0, 0), (0, 0)))
    sub = xp.reshape(B, n_cmp + 1, CMP_STRIDE, G, Dh)
    blk = jnp.concatenate([sub[:, :-1], sub[:, 1:]], axis=2) + pos[None, None, :, None, :]
    flat = blk.transpose(0, 1, 3, 2, 4).reshape(B, n_cmp, G, CMP_LEN * Dh)
    return jax.nn.gelu(flat @ w1) @ w2


def nsa_mixer(q, k_c, v_c, k_s, v_s, k_w, v_w, gate, pk_pos, pk_w1, pk_w2, pv_pos, pv_w1, pv_w2, cos, sin):
    B, S = q.shape[:2]
    G, HPG, Dh = NSA_GROUPS, NSA_HPG, HEAD_DIM
    scale = Dh ** -0.5
    n_cmp = S // CMP_STRIDE
    n_slc = S // SLC_BLOCK
    n_blk = S // Q_BLOCK
    top_k = min(SLC_TOPK, n_slc)
    q = q.reshape(B, S, G, HPG, Dh)
    q_rot = apply_rope(q, cos, sin)
    k_s = apply_rope(k_s, cos, sin)
    k_w = apply_rope(k_w, cos, sin)
    kc = compress_blocks(k_c, pk_pos, pk_w1, pk_w2)
    vc = compress_blocks(v_c, pv_pos, pv_w1, pv_w2)
    c_end = jnp.arange(n_cmp) * CMP_STRIDE + (CMP_LEN - 1)
    ks_b = k_s.reshape(B, n_slc, SLC_BLOCK, G, Dh).transpose(0, 3, 1, 2, 4)
    vs_b = v_s.reshape(B, n_slc, SLC_BLOCK, G, Dh).transpose(0, 3, 1, 2, 4)
    kw_p = jnp.pad(k_w, ((0, 0), (NSA_WINDOW, 0), (0, 0), (0, 0)))
    vw_p = jnp.pad(v_w, ((0, 0), (NSA_WINDOW, 0), (0, 0), (0, 0)))
    slc_idx = jnp.arange(n_slc)
    bi = jnp.arange(B)[:, None, None, None]
    gi = jnp.arange(G)[None, :, None, None]

    def block_fn(args):
        blk, qr, qp = args
        t = blk * Q_BLOCK + jnp.arange(Q_BLOCK)
        s_c = jnp.einsum('bqghd,bngd->bghqn', qr, kc) * scale
        p_c, _ = masked_softmax(s_c, c_end[None, :] <= t[:, None])
        o_c = jnp.einsum('bghqn,bngd->bqghd', p_c.astype(vc.dtype), vc)
        imp = jnp.pad(jnp.sum(p_c, axis=2), ((0, 0), (0, 0), (0, 0), (1, SLC_RATIO - 1)))
        p_slc = imp[..., :n_cmp].reshape(B, G, Q_BLOCK, n_slc, SLC_RATIO).sum(-1) + imp[..., SLC_RATIO::SLC_RATIO]
        cur = (t // SLC_BLOCK)[:, None]
        forced = (slc_idx[None] == 0) | (slc_idx[None] == cur) | (slc_idx[None] == cur - 1)
        future = slc_idx[None] * SLC_BLOCK > t[:, None]
        sel_score = jnp.where(forced, FORCE_SCORE, jnp.where(future, -1.0, p_slc))
        _, idx = lax.top_k(sel_score, top_k)
        kg = ks_b[bi, gi, idx].reshape(B, G, Q_BLOCK, top_k * SLC_BLOCK, Dh)
        vg = vs_b[bi, gi, idx].reshape(B, G, Q_BLOCK, top_k * SLC_BLOCK, Dh)
        kpos = (idx[..., None] * SLC_BLOCK + jnp.arange(SLC_BLOCK)).reshape(B, G, Q_BLOCK, top_k * SLC_BLOCK)
        s_s = jnp.einsum('bqghd,bgqkd->bghqk', qp, kg) * scale
        p_s, _ = masked_softmax(s_s, (kpos <= t[None, None, :, None])[:, :, None])
        o_s = jnp.einsum('bghqk,bgqkd->bqghd', p_s.astype(vg.dtype), vg)
        kw = lax.dynamic_slice_in_dim(kw_p, blk * Q_BLOCK, Q_BLOCK + NSA_WINDOW, axis=1)
        vw = lax.dynamic_slice_in_dim(vw_p, blk * Q_BLOCK, Q_BLOCK + NSA_WINDOW, axis=1)
        kpos_w = blk * Q_BLOCK - NSA_WINDOW + jnp.arange(Q_BLOCK + NSA_WINDOW)
        diff = t[:, None] - kpos_w[None, :]
        m_w = (diff >= 0) & (diff < NSA_WINDOW) & (kpos_w[None, :] >= 0)
        s_w = jnp.einsum('bqghd,bkgd->bghqk', qp, kw) * scale
        p_w, _ = masked_softmax(s_w, m_w)
        o_w = jnp.einsum('bghqk,bkgd->bqghd', p_w.astype(vw.dtype), vw)
        return o_c, o_s, o_w

    def to_blocks(x):
        return jnp.moveaxis(x.reshape(B, n_blk, Q_BLOCK, G, HPG, Dh), 1, 0)

    o_c, o_s, o_w = lax.map(block_fn, (jnp.arange(n_blk), to_blocks(q), to_blocks(q_rot)))
    back = lambda o: jnp.moveaxis(o, 0, 1).reshape(B, S, G, HPG, Dh)
    g = jax.nn.sigmoid(gate.reshape(B, S, G, HPG, N_BRANCH))[..., None]
    o = g[..., 0, :] * back(o_c) + g[..., 1, :] * back(o_s) + g[..., 2, :] * back(o_w)
    return o.reshape(B, S, NSA_Q_W)


def dilated_group(q, k, v, span, dil):
    B, S, Hg, Dh = q.shape
    unit = dil * Q_BLOCK
    s_pad = -(-S // unit) * unit
    m_len = s_pad // dil
    n_sub = m_len // Q_BLOCK

    def to_sub(x):
        x = jnp.pad(x, ((0, 0), (0, s_pad - S), (0, 0), (0, 0)))
        x = x.reshape(B, m_len, dil, Hg, Dh).transpose(0, 2, 1, 3, 4)
        return x.reshape(B, dil, n_sub, Q_BLOCK, Hg, Dh)

    def with_prev(x):
        prev = jnp.pad(x, ((0, 0), (0, 0), (1, 0), (0, 0), (0, 0), (0, 0)))[:, :, :-1]
        return jnp.concatenate([prev, x], axis=3)

    qs = to_sub(q)
    ks = with_prev(to_sub(k))
    vs = with_prev(to_sub(v))
    scores = jnp.einsum('brnqhd,brnkhd->brnhqk', qs, ks) * (Dh ** -0.5)
    qi = jnp.arange(Q_BLOCK)[:, None] + Q_BLOCK
    ki = jnp.arange(2 * Q_BLOCK)[None, :]
    delta = qi - ki
    band = (delta >= 0) & (delta <= span)
    first = (jnp.arange(n_sub) == 0)[:, None, None] & (ki < Q_BLOCK)[None]
    mask = (band[None] & ~first)[:, None]
    p, lse = masked_softmax(scores, mask)
    o = jnp.einsum('brnhqk,brnkhd->brnqhd', p.astype(vs.dtype), vs)
    o = o.reshape(B, dil, m_len, Hg, Dh).transpose(0, 2, 1, 3, 4).reshape(B, s_pad, Hg, Dh)[:, :S]
    lse = lse.transpose(0, 1, 2, 4, 3).reshape(B, dil, m_len, Hg).transpose(0, 2, 1, 3).reshape(B, s_pad, Hg)[:, :S]
    return o, lse


def dilated_mixer(q, k, v, cos, sin):
    B, S = q.shape[:2]
    q = apply_rope(q.reshape(B, S, DIL_HEADS, HEAD_DIM), cos, sin)
    k = apply_rope(k.reshape(B, S, DIL_HEADS, HEAD_DIM), cos, sin)
    v = v.reshape(B, S, DIL_HEADS, HEAD_DIM)
    outs, lses = [], []
    for gidx, (win, dil) in enumerate(DIL_PAIRS):
        hs = slice(gidx * DIL_HPG, (gidx + 1) * DIL_HPG)
        o, lse = dilated_group(q[:, :, hs], k[:, :, hs], v[:, :, hs], win // dil, dil)
        outs.append(o)
        lses.append(lse)
    w = jax.nn.softmax(jnp.stack(lses, axis=0), axis=0)
    o = jnp.sum(w[..., None].astype(v.dtype) * jnp.stack(outs, axis=0), axis=0)
    return o.reshape(B, S, DIL_OUT_W)


def sgu_mixer(uv, ln_g, ln_b, w_s, b_s):
    B, S, _ = uv.shape
    uv = jax.nn.gelu(uv)
    u, v = uv[..., :SGU_WIDTH], uv[..., SGU_WIDTH:]
    v = layer_norm(v, ln_g, ln_b)
    vc = v.reshape(B, S // SGU_CHUNK, SGU_CHUNK, SGU_GROUPS, SGU_GROUP_CH)
    causal = jnp.tril(jnp.ones((SGU_CHUNK, SGU_CHUNK), dtype=bool))
    ws = jnp.where(causal[None], w_s, 0)
    sv = jnp.einsum('gts,bnsgc->bntgc', ws, vc) + b_s.T[None, None, :, :, None]
    return u * sv.reshape(B, S, SGU_WIDTH)


def token_mixing(h, w_in, pk_pos, pk_w1, pk_w2, pv_pos, pv_w1, pv_w2, sgu_ln_g, sgu_ln_b, sgu_w, sgu_b,
                 w_branch_a, w_branch_b, w_branch_c, w_out, cos, sin):
    sizes = [NSA_Q_W] + [NSA_KV_W] * 6 + [N_BRANCH * NSA_HEADS, DIL_W, DIL_W, DIL_W, 2 * SGU_WIDTH, N_BRANCH * D_MODEL]
    z = h @ w_in
    parts = jnp.split(z, np.cumsum(sizes)[:-1].tolist(), axis=-1)
    q_a, kc, vc, ks, vs, kw, vw, g_a, q_b, k_b, v_b, uv, g_m = parts
    B, S, _ = h.shape
    kv = lambda t: t.reshape(B, S, NSA_GROUPS, HEAD_DIM)
    y_a = nsa_mixer(q_a, kv(kc), kv(vc), kv(ks), kv(vs), kv(kw), kv(vw), g_a,
                    pk_pos, pk_w1, pk_w2, pv_pos, pv_w1, pv_w2, cos, sin) @ w_branch_a
    y_b = dilated_mixer(q_b, k_b, v_b, cos, sin) @ w_branch_b
    y_c = sgu_mixer(uv, sgu_ln_g, sgu_ln_b, sgu_w, sgu_b) @ w_branch_c
    g = jax.nn.sigmoid(g_m)
    merged = g[..., :D_MODEL] * y_a + g[..., D_MODEL:2 * D_MODEL] * y_b + g[..., 2 * D_MODEL:] * y_c
    return merged @ w_out


def setup_inputs(seed: int = 0) -> dict:
    key = jax.random.key(seed)
    ks = jax.random.split(key, 24)
    L, D = DEPTH, D_MODEL

    def nrm(k, shape, scale):
        return jax.random.normal(k, shape, jnp.float32) * scale

    return {
        'x': nrm(ks[0], (BATCH, SEQ, D), 1.0),
        'ln_g': 1.0 + nrm(ks[1], (L, 3, D), 0.02),
        'ln_b': nrm(ks[2], (L, 3, D), 0.02),
        'ffn1_gate': nrm(ks[3], (L, D, FFN_DIM), D ** -0.5),
        'ffn1_up': nrm(ks[4], (L, D, FFN_DIM), D ** -0.5),
        'ffn1_down': nrm(ks[5], (L, FFN_DIM, D), FFN_DIM ** -0.5 * DEEPNORM_BETA),
        'ffn2_gate': nrm(ks[6], (L, D, FFN_DIM), D ** -0.5),
        'ffn2_up': nrm(ks[7], (L, D, FFN_DIM), D ** -0.5),
        'ffn2_down': nrm(ks[8], (L, FFN_DIM, D), FFN_DIM ** -0.5 * DEEPNORM_BETA),
        'w_in': nrm(ks[9], (L, D, IN_COLS), D ** -0.5),
        'phi_k_pos': nrm(ks[10], (L, CMP_LEN, HEAD_DIM), 0.5),
        'phi_k_w1': nrm(ks[11], (L, CMP_LEN * HEAD_DIM, CMP_HIDDEN), (CMP_LEN * HEAD_DIM) ** -0.5),
        'phi_k_w2': nrm(ks[12], (L, CMP_HIDDEN, HEAD_DIM), CMP_HIDDEN ** -0.5),
        'phi_v_pos': nrm(ks[13], (L, CMP_LEN, HEAD_DIM), 0.5),
        'phi_v_w1': nrm(ks[14], (L, CMP_LEN * HEAD_DIM, CMP_HIDDEN), (CMP_LEN * HEAD_DIM) ** -0.5),
        'phi_v_w2': nrm(ks[15], (L, CMP_HIDDEN, HEAD_DIM), CMP_HIDDEN ** -0.5),
        'sgu_ln_g': 1.0 + nrm(ks[16], (L, SGU_WIDTH), 0.02),
        'sgu_ln_b': nrm(ks[17], (L, SGU_WIDTH), 0.02),
        'sgu_w': nrm(ks[18], (L, SGU_GROUPS, SGU_CHUNK, SGU_CHUNK), 0.5 * SGU_CHUNK ** -0.5),
        'sgu_b': 1.0 + nrm(ks[19], (L, SGU_GROUPS, SGU_CHUNK), 0.02),
        'w_branch_a': nrm(ks[20], (L, NSA_Q_W, D), NSA_Q_W ** -0.5),
        'w_branch_b': nrm(ks[21], (L, DIL_OUT_W, D), DIL_OUT_W ** -0.5),
        'w_branch_c': nrm(ks[22], (L, SGU_WIDTH, D), SGU_WIDTH ** -0.5),
        'w_out': nrm(ks[23], (L, D, D), D ** -0.5 * DEEPNORM_BETA),
    }


def reference(x, ln_g, ln_b, ffn1_gate, ffn1_up, ffn1_down, ffn2_gate, ffn2_up, ffn2_down, w_in,
              phi_k_pos, phi_k_w1, phi_k_w2, phi_v_pos, phi_v_w1, phi_v_w2,
              sgu_ln_g, sgu_ln_b, sgu_w, sgu_b, w_branch_a, w_branch_b, w_branch_c, w_out):
    cos, sin = rope_tables(x.shape[1])
    for l in range(DEPTH):
        x = layer_norm(DEEPNORM_ALPHA * x + 0.5 * swiglu(x, ffn1_gate[l], ffn1_up[l], ffn1_down[l]),
                       ln_g[l, 0], ln_b[l, 0])
        mix = token_mixing(x, w_in[l], phi_k_pos[l], phi_k_w1[l], phi_k_w2[l], phi_v_pos[l], phi_v_w1[l], phi_v_w2[l],
                           sgu_ln_g[l], sgu_ln_b[l], sgu_w[l], sgu_b[l],
                           w_branch_a[l], w_branch_b[l], w_branch_c[l], w_out[l], cos, sin)
        x = layer_norm(DEEPNORM_ALPHA * x + mix, ln_g[l, 1], ln_b[l, 1])
        x = layer_norm(DEEPNORM_ALPHA * x + 0.5 * swiglu(x, ffn2_gate[l], ffn2_up[l], ffn2_down[l]),
                       ln_g[l, 2], ln_b[l, 2])
    return x
```

```python
import contextlib
import numpy as np
import ml_dtypes
import concourse.bass as bass
import concourse.mybir as mybir
from concourse.bass_utils import run_bass_kernel_spmd

F32 = mybir.dt.float32
BF16 = mybir.dt.bfloat16
AF = mybir.ActivationFunctionType
ALU = mybir.AluOpType
AX = mybir.AxisListType
NPBF = ml_dtypes.bfloat16

ENGS = ("tensor", "vector", "scalar", "gpsimd", "sync")


class Buf:
    def __init__(self, name, t):
        self.name = name
        self.t = t[:]
        self.w = {}
        self.r = {}
        self.dsem = None


class KB:
    def __init__(self):
        self.nc = bass.Bass("TRN2", target_bir_lowering=False)
        self.es = contextlib.ExitStack()
        self.q = {e: [] for e in ENGS}
        self.cnt = {e: 0 for e in ENGS}
        self.known = {e: {} for e in ENGS}
        self.sems = {}
        self.semcnt = {}
        self.out_events = {}
        for e in ENGS:
            self.sems[e] = self.es.enter_context(self.nc.semaphore("c_" + e))
        self.nd = 0
        global _LAST_KB
        _LAST_KB = self

    def dram(self, name, shape, dt, kind):
        return self.nc.dram_tensor(name, list(shape), dt, kind=kind).ap()

    def sb(self, name, shape, dt):
        t = self.es.enter_context(self.nc.sbuf_tensor(name, list(shape), dt))
        return Buf(name, t)

    def ps(self, name, shape, dt):
        t = self.es.enter_context(self.nc.psum_tensor(name, list(shape), dt))
        return Buf(name, t)

    def _newsem(self, name):
        s = self.es.enter_context(self.nc.semaphore(name))
        self.sems[name] = s
        self.semcnt[name] = 0
        return name

    def _waits(self, eng, reads, writes):
        need = {}
        for b in reads:
            for k, v in b.w.items():
                need[k] = max(need.get(k, 0), v)
        for b in writes:
            for k, v in b.w.items():
                if k == eng and eng != "sync" and eng != "gpsimd":
                    continue
                need[k] = max(need.get(k, 0), v)
            for k, v in b.r.items():
                if k == eng:
                    continue
                need[k] = max(need.get(k, 0), v)
        out = []
        kn = self.known[eng]
        for k, v in need.items():
            if eng == "tensor" and k == "tensor":
                continue
            if kn.get(k, 0) >= v:
                continue
            kn[k] = v
            out.append((k, v))
        return out

    def op(self, eng, fn, reads=(), writes=()):
        waits = self._waits(eng, reads, writes)
        self.cnt[eng] += 1
        n = self.cnt[eng]
        self.q[eng].append((waits, fn, (eng, 1)))
        for b in reads:
            b.r[eng] = n
        for b in writes:
            b.w = {eng: n}
            b.r = {}

    def I(self, eng, name, *args, reads=(), writes=(), **kw):
        self.op(eng, lambda e, a=args, k=kw, n=name: getattr(e, n)(*a, **k), reads=reads, writes=writes)

    def dma(self, eng, dst, out_ap, in_ap, reads=(), out_dram=False, **kw):
        writes = [dst] if dst is not None else []
        waits = self._waits(eng, reads, writes)
        if dst is not None:
            if dst.dsem is None:
                dst.dsem = self._newsem("d_%d_%s" % (self.nd, dst.name))
                self.nd += 1
            sk = dst.dsem
        else:
            b0 = reads[0]
            if getattr(b0, "ssem", None) is None:
                b0.ssem = self._newsem("s_%d_%s" % (self.nd, b0.name))
                self.nd += 1
            sk = b0.ssem
        self.semcnt[sk] += 16
        v = self.semcnt[sk]
        fn = lambda e, o=out_ap, i=in_ap, kw=kw: e.dma_start(out=o, in_=i, **kw)
        self.q[eng].append((waits, fn, (sk, 16)))
        for b in reads:
            b.r[sk] = v
        if dst is not None:
            neww = {k: val for k, val in dst.w.items() if k == sk}
            neww[sk] = v
            dst.w = neww
            dst.r = {}
        if out_dram:
            self.out_events[sk] = v

    def finish(self):
        nc = self.nc
        fin = [(k, v) for k, v in self.out_events.items()]
        with nc.Block() as block:
            def run(engname):
                def body(e):
                    for waits, fn, (sk, inc) in self.q[engname]:
                        for k, v in waits:
                            e.wait_ge(self.sems[k], v)
                        fn(e).then_inc(self.sems[sk], inc)
                    if engname == "sync":
                        for k, v in fin:
                            e.wait_ge(self.sems[k], v)
                return body
            block.tensor(run("tensor"))
            block.vector(run("vector"))
            block.scalar(run("scalar"))
            block.gpsimd(run("gpsimd"))
            block.sync(run("sync"))
        self.es.close()


D = 1024
FF = 2816
NT = 2048
DEPTH = 4
ALPHA = (2 * DEPTH) ** 0.25
LN_EPS = 1e-5
NFC = FF // 128
IN_COLS = 7704


def bcast_rows(ap_row, nparts):
    return ap_row.broadcast(0, nparts) if hasattr(ap_row, "broadcast") else ap_row


def emit_ln(kb, r, outb, G, Bt, stats, mv, rstd, eps):
    for hh in range(2):
        kb.I("vector", "bn_stats", stats.t[:, hh * 6:(hh + 1) * 6], r.t[:, hh * 512:(hh + 1) * 512], reads=[r], writes=[stats])
    kb.I("vector", "bn_aggr", mv.t[:, :], stats.t[:, :], reads=[stats], writes=[mv])
    kb.I("vector", "tensor_scalar", rstd.t[:, :], mv.t[:, 1:2], eps, None, ALU.add, reads=[mv], writes=[rstd])
    kb.I("scalar", "sqrt", rstd.t[:, :], rstd.t[:, :], reads=[rstd], writes=[rstd])
    kb.I("vector", "reciprocal", rstd.t[:, :], rstd.t[:, :], reads=[rstd], writes=[rstd])
    kb.I("vector", "tensor_scalar", outb.t[:, :], r.t[:, :], mv.t[:, 0:1], rstd.t[:, 0:1], ALU.subtract, ALU.mult,
         reads=[r, mv, rstd], writes=[outb])
    kb.I("gpsimd", "tensor_tensor", outb.t[:, :], outb.t[:, :], G.t[:, :], ALU.mult, reads=[outb, G], writes=[outb])
    kb.I("gpsimd", "tensor_tensor", outb.t[:, :], outb.t[:, :], Bt.t[:, :], ALU.add, reads=[outb, Bt], writes=[outb])


def build_ffn():
    kb = KB()
    x = kb.dram("x", [NT, D], F32, "ExternalInput")
    wg = kb.dram("wg", [D, FF], F32, "ExternalInput")
    wu = kb.dram("wu", [D, FF], F32, "ExternalInput")
    wd = kb.dram("wd", [FF, D], F32, "ExternalInput")
    lng = kb.dram("lng", [128, D], F32, "ExternalInput")
    lnb = kb.dram("lnb", [128, D], F32, "ExternalInput")
    idn = kb.dram("idn", [128, 128], F32, "ExternalInput")
    y = kb.dram("y", [NT, D], F32, "ExternalOutput")

    Wg = [kb.sb("Wg%d" % i, [128, FF], BF16) for i in range(8)]
    Wu = [kb.sb("Wu%d" % i, [128, FF], BF16) for i in range(8)]
    Wd = [kb.sb("Wd%d" % i, [128, 2 * D], BF16) for i in range(NFC // 2)]
    G = kb.sb("G", [128, D], F32)
    Bt = kb.sb("Bt", [128, D], F32)
    ident = kb.sb("ident", [128, 128], F32)
    TT = 256
    NSUB = TT // 128
    xs = [kb.sb("xs%d" % i, [128, D], F32) for i in range(2 * NSUB)]
    xT = [kb.sb("xT%d" % i, [128, 8, TT], BF16) for i in range(2)]
    hT = [kb.sb("hT%d" % i, [128, TT], BF16) for i in range(NFC)]
    sg = [kb.sb("sg%d" % i, [128, TT], F32) for i in range(2)]
    rr = [kb.sb("rr%d" % i, [128, D], F32) for i in range(2)]
    oo = [kb.sb("oo%d" % i, [128, D], F32) for i in range(2)]
    stats = kb.sb("stats", [128, 12], F32)
    mv = kb.sb("mv", [128, 2], F32)
    rstd = kb.sb("rstd", [128, 1], F32)
    pg = [kb.ps("pg%d" % i, [128, 512], F32) for i in range(2)]
    pu = [kb.ps("pu%d" % i, [128, 512], F32) for i in range(2)]
    pt = [kb.ps("pt%d" % i, [128, 512], F32) for i in range(2)]
    po = [kb.ps("po%d" % i, [128, 512], F32) for i in range(2)]

    kb.dma("sync", ident, ident.t[:, :], idn[:, :])
    kb.dma("sync", G, G.t[:, :], lng[:, :])
    kb.dma("sync", Bt, Bt.t[:, :], lnb[:, :])
    for s in range(NSUB):
        kb.dma("sync", xs[s], xs[s].t[:, :], x[s * 128:(s + 1) * 128, :])
    for i in range(8):
        kb.dma("gpsimd", Wg[i], Wg[i].t[:, :], wg[i * 128:(i + 1) * 128, :])
        kb.dma("gpsimd", Wu[i], Wu[i].t[:, :], wu[i * 128:(i + 1) * 128, :])
    for i in range(NFC // 2):
        kb.dma("gpsimd", Wd[i], Wd[i].t[:, :].rearrange("p (a d) -> p a d", a=2),
               wd[i * 256:(i + 1) * 256, :].rearrange("(a p) d -> p a d", p=128))

    eps2 = LN_EPS / (ALPHA * ALPHA)
    ntile = NT // TT
    for t in range(ntile):
        par = t % 2
        xcur = xs[par * NSUB:(par + 1) * NSUB]
        if t + 1 < ntile:
            nxt = xs[(1 - par) * NSUB:(2 - par) * NSUB]
            for s in range(NSUB):
                r0 = (t + 1) * TT + s * 128
                kb.dma("sync", nxt[s], nxt[s].t[:, :], x[r0:r0 + 128, :])
        for s in range(NSUB):
            for hh in range(2):
                for j in range(4):
                    dc = hh * 4 + j
                    kb.I("tensor", "transpose", pt[hh].t[:, j * 128:(j + 1) * 128],
                         xcur[s].t[:, dc * 128:(dc + 1) * 128], ident.t[:, :], reads=[xcur[s], ident], writes=[pt[hh]])
                kb.I("scalar", "copy", xT[par].t[:, hh * 4:(hh + 1) * 4, s * 128:(s + 1) * 128],
                     pt[hh].t[:, :].rearrange("p (a b) -> p a b", a=4), reads=[pt[hh]], writes=[xT[par]])
        for fc in range(NFC):
            pp = fc % 2
            for dc in range(8):
                kb.I("tensor", "matmul", pg[pp].t[:, :TT], Wg[dc].t[:, fc * 128:(fc + 1) * 128], xT[par].t[:, dc, :],
                     start=(dc == 0), stop=(dc == 7), reads=[Wg[dc], xT[par]], writes=[pg[pp]])
            for dc in range(8):
                kb.I("tensor", "matmul", pu[pp].t[:, :TT], Wu[dc].t[:, fc * 128:(fc + 1) * 128], xT[par].t[:, dc, :],
                     start=(dc == 0), stop=(dc == 7), reads=[Wu[dc], xT[par]], writes=[pu[pp]])
            kb.I("scalar", "activation", sg[pp].t[:, :], pg[pp].t[:, :TT], AF.Silu, reads=[pg[pp]], writes=[sg[pp]])
            kb.I("vector", "tensor_tensor", hT[fc].t[:, :], sg[pp].t[:, :], pu[pp].t[:, :TT], ALU.mult,
                 reads=[sg[pp], pu[pp]], writes=[hT[fc]])
        for s in range(NSUB):
            rb = rr[s % 2]
            ob = oo[s % 2]
            for hh in range(2):
                for fc in range(NFC):
                    kb.I("tensor", "matmul", po[hh].t[:, :], hT[fc].t[:, s * 128:(s + 1) * 128],
                         Wd[fc // 2].t[:, (fc % 2) * D + hh * 512:(fc % 2) * D + (hh + 1) * 512],
                         start=(fc == 0), stop=(fc == NFC - 1), reads=[hT[fc], Wd[fc // 2]], writes=[po[hh]])
                kb.I("vector", "scalar_tensor_tensor", rb.t[:, hh * 512:(hh + 1) * 512], po[hh].t[:, :], 0.5 / ALPHA,
                     xcur[s].t[:, hh * 512:(hh + 1) * 512], ALU.mult, ALU.add, reads=[po[hh], xcur[s]], writes=[rb])
            emit_ln(kb, rb, ob, G, Bt, stats, mv, rstd, eps2)
            r0 = t * TT + s * 128
            kb.dma("sync", None, y[r0:r0 + 128, :], ob.t[:, :], reads=[ob], out_dram=True)
    kb.finish()
    return kb.nc


ZC = 4632
ZW = ZC + 512
ZF = 536
C_QA, C_KC, C_VC, C_KS, C_VS, C_KW, C_VW, C_GA, C_QB, C_KB, C_VB, C_U, C_V, C_GM, C_QAR = (
    0, 512, 640, 768, 896, 1024, 1152, 1280, 1304, 2072, 2840, 3608, 4120, 4632, 4632)
GELU_K = 1.5957691216057308


def emit_gelu(kb, eng_v, zt, c0, n, tmp):
    xa = zt.t[:, c0:c0 + n]
    ta = tmp.t[:, 0:n]
    kb.I(eng_v, "tensor_tensor", ta, xa, xa, ALU.mult, reads=[zt], writes=[tmp])
    kb.I(eng_v, "tensor_scalar", ta, ta, 0.044715, 1.0, ALU.mult, ALU.add, reads=[tmp], writes=[tmp])
    kb.I(eng_v, "tensor_tensor", ta, ta, xa, ALU.mult, reads=[tmp, zt], writes=[tmp])
    kb.I("scalar", "activation", ta, ta, AF.Sigmoid, scale=GELU_K, reads=[tmp], writes=[tmp])
    kb.I(eng_v, "tensor_tensor", xa, xa, ta, ALU.mult, reads=[tmp, zt], writes=[zt])


def emit_rope(kb, zt, c_in, c_out, H, cc, ss, tmp):
    xin = zt.t[:, c_in:c_in + H * 64].rearrange("p (h d) -> p h d", d=64)
    xout = zt.t[:, c_out:c_out + H * 64].rearrange("p (h d) -> p h d", d=64)
    tt = tmp.t[:, 0:H * 64].rearrange("p (h d) -> p h d", d=64)
    ccb = cc.t[:, :].unsqueeze(1).broadcast_to([128, H, 64])
    s1 = ss.t[:, 0:32].unsqueeze(1).broadcast_to([128, H, 32])
    s2 = ss.t[:, 32:64].unsqueeze(1).broadcast_to([128, H, 32])
    kb.I("vector", "tensor_tensor", tt[:, :, 0:32], xin[:, :, 32:64], s1, ALU.mult, reads=[zt, ss], writes=[tmp])
    kb.I("vector", "tensor_tensor", tt[:, :, 32:64], xin[:, :, 0:32], s2, ALU.mult, reads=[zt, ss], writes=[tmp])
    kb.I("vector", "tensor_tensor", xout, xin, ccb, ALU.mult, reads=[zt, cc], writes=[zt])
    kb.I("vector", "tensor_tensor", xout, xout, tt, ALU.add, reads=[zt, tmp], writes=[zt])


def build_win():
    kb = KB()
    x = kb.dram("x", [NT, D], F32, "ExternalInput")
    win = kb.dram("win", [D, ZC], F32, "ExternalInput")
    idn = kb.dram("idn", [128, 128], F32, "ExternalInput")
    ccd = kb.dram("cc", [NT, 64], F32, "ExternalInput")
    ssd = kb.dram("ss", [NT, 64], F32, "ExternalInput")
    sgd = kb.dram("sg", [128, 512], F32, "ExternalInput")
    sbd = kb.dram("sb", [128, 512], F32, "ExternalInput")
    zbo = kb.dram("zb", [NT, ZW], BF16, "ExternalOutput")
    zfo = kb.dram("zf", [NT, ZF], F32, "ExternalOutput")

    W = [kb.sb("W%d" % i, [128, ZC], BF16) for i in range(8)]
    ident = kb.sb("ident", [128, 128], F32)
    SG = kb.sb("SG", [128, 512], F32)
    SB = kb.sb("SB", [128, 512], F32)
    xs = [kb.sb("xs%d" % i, [128, D], F32) for i in range(2)]
    cc = [kb.sb("cc%d" % i, [128, 64], F32) for i in range(2)]
    ss = [kb.sb("ss%d" % i, [128, 64], F32) for i in range(2)]
    xT = [kb.sb("xT%d" % i, [128, 8, 128], BF16) for i in range(2)]
    Z = [kb.sb("Z%d" % i, [128, ZW], F32) for i in range(2)]
    ZB = [kb.sb("ZB%d" % i, [128, ZW], BF16) for i in range(2)]
    ZFs = [kb.sb("ZF%d" % i, [128, ZF], F32) for i in range(2)]
    tmp = kb.sb("tmp", [128, 1024], F32)
    stats = kb.sb("stats", [128, 6], F32)
    mv = kb.sb("mv", [128, 2], F32)
    rstd = kb.sb("rstd", [128, 1], F32)
    pt = [kb.ps("pt%d" % i, [128, 512], F32) for i in range(2)]
    pz = [kb.ps("pz%d" % i, [128, 512], F32) for i in range(4)]

    kb.dma("sync", ident, ident.t, idn)
    kb.dma("sync", SG, SG.t, sgd)
    kb.dma("sync", SB, SB.t, sbd)

    def load(s):
        p = s % 2
        kb.dma("sync", xs[p], xs[p].t, x[s * 128:(s + 1) * 128, :])
        kb.dma("sync", cc[p], cc[p].t, ccd[s * 128:(s + 1) * 128, :])
        kb.dma("sync", ss[p], ss[p].t, ssd[s * 128:(s + 1) * 128, :])
    load(0)
    for i in range(8):
        kb.dma("gpsimd", W[i], W[i].t, win[i * 128:(i + 1) * 128, :])
    nsub = NT // 128
    nck = (ZC + 511) // 512
    for s in range(nsub):
        p = s % 2
        if s + 1 < nsub:
            load(s + 1)
        for hh in range(2):
            for j in range(4):
                dc = hh * 4 + j
                kb.I("tensor", "transpose", pt[hh].t[:, j * 128:(j + 1) * 128], xs[p].t[:, dc * 128:(dc + 1) * 128],
                     ident.t, reads=[xs[p], ident], writes=[pt[hh]])
            kb.I("scalar", "copy", xT[p].t[:, hh * 4:(hh + 1) * 4, :], pt[hh].t.rearrange("p (a b) -> p a b", a=4),
                 reads=[pt[hh]], writes=[xT[p]])
        zt = Z[p]
        for k in range(nck):
            c0 = k * 512
            n = min(512, ZC - c0)
            pb = pz[k % 4]
            for dc in range(8):
                kb.I("tensor", "matmul", pb.t[:, :n], xT[p].t[:, dc, :], W[dc].t[:, c0:c0 + n], start=(dc == 0), stop=(dc == 7),
                     reads=[xT[p], W[dc]], writes=[pb])
            if k % 2 == 0:
                kb.I("vector", "tensor_copy", zt.t[:, c0:c0 + n], pb.t[:, :n], reads=[pb], writes=[zt])
            else:
                kb.I("scalar", "copy", zt.t[:, c0:c0 + n], pb.t[:, :n], reads=[pb], writes=[zt])
        kb.I("scalar", "activation", zt.t[:, C_GA:C_GA + 24], zt.t[:, C_GA:C_GA + 24], AF.Sigmoid, reads=[zt], writes=[zt])
        emit_gelu(kb, "gpsimd", zt, C_U, 1024, tmp)
        kb.I("vector", "bn_stats", stats.t, zt.t[:, C_V:C_V + 512], reads=[zt], writes=[stats])
        kb.I("vector", "bn_aggr", mv.t, stats.t, reads=[stats], writes=[mv])
        kb.I("vector", "tensor_scalar", rstd.t, mv.t[:, 1:2], LN_EPS, None, ALU.add, reads=[mv], writes=[rstd])
        kb.I("scalar", "sqrt", rstd.t, rstd.t, reads=[rstd], writes=[rstd])
        kb.I("vector", "reciprocal", rstd.t, rstd.t, reads=[rstd], writes=[rstd])
        kb.I("vector", "tensor_scalar", zt.t[:, C_V:C_V + 512], zt.t[:, C_V:C_V + 512], mv.t[:, 0:1], rstd.t[:, 0:1],
             ALU.subtract, ALU.mult, reads=[zt, mv, rstd], writes=[zt])
        kb.I("vector", "tensor_tensor", zt.t[:, C_V:C_V + 512], zt.t[:, C_V:C_V + 512], SG.t, ALU.mult, reads=[zt, SG], writes=[zt])
        kb.I("vector", "tensor_tensor", zt.t[:, C_V:C_V + 512], zt.t[:, C_V:C_V + 512], SB.t, ALU.add, reads=[zt, SB], writes=[zt])
        emit_rope(kb, zt, C_QA, C_QAR, 8, cc[p], ss[p], tmp)
        emit_rope(kb, zt, C_KS, C_KS, 2, cc[p], ss[p], tmp)
        emit_rope(kb, zt, C_KW, C_KW, 2, cc[p], ss[p], tmp)
        emit_rope(kb, zt, C_QB, C_QB, 12, cc[p], ss[p], tmp)
        emit_rope(kb, zt, C_KB, C_KB, 12, cc[p], ss[p], tmp)
        kb.I("scalar", "copy", ZB[p].t[:, 0:2560], zt.t[:, 0:2560], reads=[zt], writes=[ZB[p]])
        kb.I("vector", "tensor_copy", ZB[p].t[:, 2560:ZW], zt.t[:, 2560:ZW], reads=[zt], writes=[ZB[p]])
        kb.I("vector", "tensor_copy", ZFs[p].t[:, 0:24], zt.t[:, C_GA:C_GA + 24], reads=[zt], writes=[ZFs[p]])
        kb.I("vector", "tensor_copy", ZFs[p].t[:, 24:536], zt.t[:, C_U:C_U + 512], reads=[zt], writes=[ZFs[p]])
        kb.dma("sync", None, zbo[s * 128:(s + 1) * 128, :], ZB[p].t, reads=[ZB[p]], out_dram=True)
        kb.dma("sync", None, zfo[s * 128:(s + 1) * 128, :], ZFs[p].t, reads=[ZFs[p]], out_dram=True)
    kb.finish()
    return kb.nc


NSLOT = 16
DILS = (1, 4, 16)
DIL_CH = [(gi, o) for gi in range(3) for o in range(DILS[gi], -1, -1)]
DIL_C0 = [0, 2, 7]
NDC = len(DIL_CH)
BIGNEG = 10000.0
NWIN = 32


def ncmp_chunks(slot):
    return (64 * slot + 62) // 128 + 1


def build_attn():
    kb = KB()

    def inp(name, shape, dt=BF16):
        return kb.dram(name, shape, dt, "ExternalInput")
    qaT_d = inp("qaT", [128, NSLOT, 2, 512]); qarT_d = inp("qarT", [128, NSLOT, 2, 512])
    kcT_d = inp("kcT", [128, 16400]); vcT_d = inp("vcT", [128, 16400])
    w1k_d = inp("w1k", [128, 2, 4096], F32); w1v_d = inp("w1v", [128, 2, 4096], F32)
    posk_d = inp("posk", [128, 32], F32); posv_d = inp("posv", [128, 32], F32)
    w2k_d = inp("w2k", [128, 128], F32); w2v_d = inp("w2v", [128, 64], F32)
    mmap_d = inp("mmap", [128, 8, 256]); maskC_d = inp("maskC", [128, NSLOT, 2, 128])
    keep_d = inp("keep", [128, NSLOT, 256], F32); force_d = inp("force", [128, NSLOT, 256], F32)
    before_d = inp("before", [128, NSLOT, 256], F32)
    expm_d = inp("expm", [128, 8192])
    ksT_d = inp("ksT", [128, 16384]); vsA_d = inp("vsA", [128, 128, 2, 65])
    oksT_d = inp("own_ksT", [128, NSLOT, 128]); ovs_d = inp("own_vs", [128, NSLOT, 2, 65])
    kwT_d = inp("kwT_win", [128, NWIN * 128]); vw_d = inp("vw_win", [128, NWIN, 2, 65])
    qwT_d = inp("qwT", [128, NSLOT, 2, 512])
    tri_d = inp("tri", [128, 128]); atri_d = inp("atri", [128, 128]); dm_d = inp("dm", [128, NDC, 128])
    qbT_d = inp("qbT", [128, NSLOT, 12, 128]); kbT_d = inp("kbT_win", [128, 6, NWIN * 128]); vb_d = inp("vb_win", [128, NWIN, 12, 65])
    ga_d = inp("ga", [128, NSLOT, 24], F32); gaw_d = inp("gaw", [128, NSLOT, 24], F32)
    vn_d = inp("vn", [128, NSLOT, 512]); u_d = inp("u", [128, NSLOT, 512], F32)
    wsT_d = inp("wsT", [128, 4, 128], F32); bsT_d = inp("bsT", [128, 4], F32); idn_d = inp("idn", [128, 128], F32)
    o_nsa = kb.dram("o_nsa", [NT, 512], F32, "ExternalOutput")
    o_loc = kb.dram("o_loc", [NT, 1280], F32, "ExternalOutput")

    def sbl(name, shape, dt, src, eng="sync"):
        bf = kb.sb(name, shape, dt)
        kb.dma(eng, bf, bf.t, src)
        return bf

    ident = sbl("ident", [128, 128], F32, idn_d)
    TRI = sbl("TRI", [128, 128], BF16, tri_d)
    ATRI = sbl("ATRI", [128, 128], BF16, atri_d)
    DM = sbl("DM", [128, NDC, 128], BF16, dm_d)
    EXPM = sbl("EXPM", [128, 8192], BF16, expm_d)
    POSk = sbl("POSk", [128, 32], BF16, posk_d, "gpsimd")
    POSv = sbl("POSv", [128, 32], BF16, posv_d, "gpsimd")
    W2k = sbl("W2k", [128, 128], BF16, w2k_d, "gpsimd")
    W2v = sbl("W2v", [128, 64], BF16, w2v_d, "gpsimd")
    WST = sbl("WST", [128, 4, 128], BF16, wsT_d, "gpsimd")
    BST = sbl("BST", [128, 4], F32, bsT_d)
    KST = sbl("KST", [128, 16384], BF16, ksT_d)
    VSA = sbl("VSA", [128, 128, 2, 65], BF16, vsA_d)
    RHSC = kb.sb("RHSC", [128, 8, 2, 321], BF16)
    for g in range(2):
        kb.dma("sync", RHSC, RHSC.t[:, :, g, 65:321], mmap_d)
    KCMPT = kb.sb("KCMPT", [128, 1024], BF16)
    KCT = kb.sb("KCT", [128, 8208], BF16)
    W1 = kb.sb("W1", [128, 2, 32, 128], BF16)

    pS = [kb.ps("pS%d" % i, [128, 512], F32) for i in range(2)]
    pC = [kb.ps("pC%d" % i, [128, 512], F32) for i in range(4)]
    pO = kb.ps("pO", [128, 512], F32)
    pM = kb.ps("pM", [128, 512], F32)

    bias = kb.sb("bias", [128, 1], F32)
    hx = kb.sb("hx", [128, 512], F32)
    htmp = kb.sb("htmp", [128, 512], F32)
    hb = kb.sb("hb", [128, 512], BF16)
    kb.I("gpsimd", "memset", RHSC.t[:, :, :, 64:65], 1.0, reads=[], writes=[RHSC])
    for which in range(2):
        POS, src_d, w1_d = (POSk, kcT_d, w1k_d) if which == 0 else (POSv, vcT_d, w1v_d)
        kb.dma("gpsimd", W1, W1.t, w1_d.rearrange("p g (j h) -> p g j h", j=32))
        for j in range(32):
            kb.I("tensor", "matmul", pM.t[:, 0:1], W1.t[:, 0, j, :], POS.t[:, j:j + 1], start=(j == 0), stop=(j == 31),
                 reads=[W1, POS], writes=[pM])
        kb.I("vector", "tensor_copy", bias.t, pM.t[:, 0:1], reads=[pM], writes=[bias])
        for nch in range(2):
            kb.dma("sync", KCT, KCT.t, src_d[:, 8192 * nch:8192 * nch + 8208])
            for g in range(2):
                ph = pC[(g * 2 + nch) % 2]
                for j in range(32):
                    kb.I("tensor", "matmul", ph.t, W1.t[:, g, j, :], KCT.t[:, j:j + 16 * 511 + 1:16],
                         start=(j == 0), stop=(j == 31), reads=[W1, KCT], writes=[ph])
                kb.I("scalar", "activation", hx.t, ph.t, AF.Identity, bias=bias.t[:, 0:1], reads=[ph, bias], writes=[hx])
                kb.I("vector", "tensor_tensor", htmp.t, hx.t, hx.t, ALU.mult, reads=[hx], writes=[htmp])
                kb.I("vector", "tensor_scalar", htmp.t, htmp.t, 0.044715, 1.0, ALU.mult, ALU.add, reads=[htmp], writes=[htmp])
                kb.I("vector", "tensor_tensor", htmp.t, htmp.t, hx.t, ALU.mult, reads=[htmp, hx], writes=[htmp])
                kb.I("scalar", "activation", htmp.t, htmp.t, AF.Sigmoid, scale=GELU_K, reads=[htmp], writes=[htmp])
                kb.I("vector", "tensor_tensor", hb.t, hx.t, htmp.t, ALU.mult, reads=[htmp, hx], writes=[hb])
                if which == 0:
                    kb.I("tensor", "matmul", pM.t, W2k.t, hb.t, start=True, stop=True, reads=[W2k, hb], writes=[pM])
                    kb.I("vector", "tensor_copy", KCMPT.t[64 * g:64 * g + 64, nch * 512:(nch + 1) * 512],
                         pM.t[64 * g:64 * g + 64, :], reads=[pM], writes=[KCMPT])
                else:
                    for q4 in range(4):
                        kb.I("tensor", "matmul", pM.t[:, q4 * 64:(q4 + 1) * 64], hb.t[:, q4 * 128:(q4 + 1) * 128], W2v.t,
                             start=True, stop=True, reads=[W2v, hb], writes=[pM])
                    kb.I("vector", "tensor_copy", RHSC.t[:, nch * 4:(nch + 1) * 4, g, 0:64],
                         pM.t[:, 0:256].rearrange("p (a d) -> p a d", a=4), reads=[pM], writes=[RHSC])

    for g in range(4):
        kb.I("vector", "tensor_tensor", WST.t[:, g, :], WST.t[:, g, :], TRI.t, ALU.mult, reads=[WST, TRI], writes=[WST])

    QA = kb.sb("QA", [128, 2, 512], BF16); QAR = kb.sb("QAR", [128, 2, 512], BF16); QW = kb.sb("QW", [128, 2, 512], BF16)
    MC = kb.sb("MC", [128, 2, 128], BF16)
    KEEP = kb.sb("KEEP", [128, 256], F32); FORCE = kb.sb("FORCE", [128, 256], F32); BEF = kb.sb("BEF", [128, 256], F32)
    OKS = kb.sb("OKS", [128, 128], BF16); OVS = kb.sb("OVS", [128, 2, 65], BF16)
    WK = kb.sb("WK", [128, 5, 128], BF16); WV = kb.sb("WV", [128, 5, 2, 65], BF16)
    QB = kb.sb("QB", [128, 12, 128], BF16); DK = kb.sb("DK", [128, NDC, 2, 128], BF16); DV = kb.sb("DV", [128, NDC, 4, 65], BF16)
    GA = kb.sb("GA", [128, 24], F32); GAW = kb.sb("GAW", [128, 24], F32)
    VN = kb.sb("VN", [128, 512], BF16); UU = kb.sb("UU", [128, 512], F32)
    P = [kb.sb("P%d" % i, [128, 512], BF16) for i in range(3)]
    ON = [kb.sb("ON%d" % i, [128, 512], F32) for i in range(2)]
    OL = [kb.sb("OL%d" % i, [128, 1280], F32) for i in range(2)]
    psl = kb.sb("psl", [128, 256], F32)
    sel = kb.sb("sel", [128, 256], F32); sel2 = kb.sb("sel2", [128, 256], F32)
    m8 = kb.sb("m8", [128, 8], F32); m8b = kb.sb("m8b", [128, 8], F32)
    BPT = kb.sb("BPT", [128, 2, 128], BF16)
    rden = kb.sb("rden", [128, 4], F32); coef = kb.sb("coef", [128, 4], F32)
    pcount = [0]

    def score_exp(kpairs, mask_ap, mask_bufs, bias_mm=None):
        pb = pS[pcount[0] % 2]
        for (la, ra, c0, ncol, rb) in kpairs:
            kb.I("tensor", "matmul", pb.t[:, c0:c0 + ncol], la, ra, start=True, stop=(bias_mm is None), reads=rb, writes=[pb])
        if bias_mm is not None:
            la, ra, rb = bias_mm
            kb.I("tensor", "matmul", pb.t[:, :], la, ra, start=False, stop=True, reads=rb, writes=[pb])
        pcount[0] += 1
        pt_ = P[pcount[0] % 3]
        kb.I("scalar", "activation", pt_.t, pb.t, AF.Exp, scale=0.125, reads=[pb], writes=[pt_])
        if mask_ap is not None:
            kb.I("vector", "tensor_tensor", pt_.t.rearrange("p (h q) -> p h q", h=4), pt_.t.rearrange("p (h q) -> p h q", h=4),
                 mask_ap.unsqueeze(1).broadcast_to([128, 4, 128]), ALU.mult, reads=[pt_] + mask_bufs, writes=[pt_])
        return pt_

    acc = [pC[j].t[:, 0:65] for j in range(4)]

    def pv(pt_, vaps, vbufs, first, last):
        for j in range(4):
            kb.I("tensor", "matmul", acc[j], pt_.t[:, j * 128:(j + 1) * 128], vaps[j], start=first, stop=last,
                 reads=[pt_] + vbufs, writes=[pC[j]])

    def recip_den(j, den_ap, den_buf):
        kb.I("vector", "tensor_scalar", rden.t[:, j:j + 1], den_ap, 1e-30, None, ALU.max, reads=[den_buf], writes=[rden])
        kb.I("vector", "reciprocal", rden.t[:, j:j + 1], rden.t[:, j:j + 1], reads=[rden], writes=[rden])

    def finish_branch(outb, col0, gates, _unused):
        for j in range(4):
            recip_den(j, pC[j].t[:, 64:65], pC[j])
            if gates is not None:
                gb, gc, accumulate = gates
                kb.I("vector", "tensor_tensor", coef.t[:, j:j + 1], rden.t[:, j:j + 1], gb.t[:, gc(j):gc(j) + 1], ALU.mult,
                     reads=[rden, gb], writes=[coef])
                sc = coef
            else:
                accumulate = False
                sc = rden
            oa = outb.t[:, col0 + j * 64:col0 + (j + 1) * 64]
            if accumulate:
                kb.I("vector", "scalar_tensor_tensor", oa, pC[j].t[:, 0:64], sc.t[:, j:j + 1], oa, ALU.mult, ALU.add,
                     reads=[pC[j], sc, outb], writes=[outb])
            else:
                kb.I("vector", "tensor_scalar", oa, pC[j].t[:, 0:64], sc.t[:, j:j + 1], None, ALU.mult,
                     reads=[pC[j], sc], writes=[outb])

    for slot in range(NSLOT):
        on = ON[slot % 2]
        ol = OL[slot % 2]
        kb.dma("sync", QA, QA.t, qaT_d[:, slot]); kb.dma("sync", QAR, QAR.t, qarT_d[:, slot]); kb.dma("sync", QW, QW.t, qwT_d[:, slot])
        kb.dma("sync", MC, MC.t, maskC_d[:, slot])
        kb.dma("sync", KEEP, KEEP.t, keep_d[:, slot]); kb.dma("sync", FORCE, FORCE.t, force_d[:, slot]); kb.dma("sync", BEF, BEF.t, before_d[:, slot])
        kb.dma("sync", OKS, OKS.t, oksT_d[:, slot]); kb.dma("sync", OVS, OVS.t, ovs_d[:, slot])
        kb.dma("sync", WK, WK.t, kwT_d[:, (12 + slot) * 128:(17 + slot) * 128].rearrange("p (n k) -> p n k", k=128))
        kb.dma("sync", WV, WV.t, vw_d[:, 12 + slot:17 + slot])
        kb.dma("sync", QB, QB.t, qbT_d[:, slot])
        for gi in range(3):
            n = DILS[gi] + 1
            c_lo = 16 + slot - DILS[gi]
            for pr in range(2):
                kb.dma("sync", DK, DK.t[:, DIL_C0[gi]:DIL_C0[gi] + n, pr, :],
                       kbT_d[:, 2 * gi + pr, c_lo * 128:(c_lo + n) * 128].rearrange("p (n k) -> p n k", k=128))
            kb.dma("sync", DV, DV.t[:, DIL_C0[gi]:DIL_C0[gi] + n, :, :], vb_d[:, c_lo:c_lo + n, 4 * gi:4 * gi + 4, :])
        kb.dma("sync", GA, GA.t, ga_d[:, slot]); kb.dma("sync", GAW, GAW.t, gaw_d[:, slot])
        kb.dma("sync", VN, VN.t, vn_d[:, slot]); kb.dma("sync", UU, UU.t, u_d[:, slot])

        for g in range(2):
            nck = ncmp_chunks(slot)
            for ck in range(nck):
                m = ck - (nck - 2)
                msk = MC.t[:, m, :] if m >= 0 else None
                pt_ = score_exp([(KCMPT.t[:, ck * 128:(ck + 1) * 128], QA.t[:, g, :], 0, 512, [KCMPT, QA])], msk, [MC])
                for j in range(4):
                    kb.I("tensor", "matmul", pC[j].t[:, 0:321], pt_.t[:, j * 128:(j + 1) * 128], RHSC.t[:, ck, g, :],
                         start=(ck == 0), stop=(ck == nck - 1), reads=[pt_, RHSC], writes=[pC[j]])
            for j in range(4):
                h = 4 * g + j
                recip_den(j, pC[j].t[:, 64:65], pC[j])
                kb.I("vector", "tensor_tensor", coef.t[:, j:j + 1], rden.t[:, j:j + 1], GA.t[:, 3 * h:3 * h + 1], ALU.mult,
                     reads=[rden, GA], writes=[coef])
                kb.I("vector", "tensor_scalar", on.t[:, h * 64:(h + 1) * 64], pC[j].t[:, 0:64], coef.t[:, j:j + 1], None, ALU.mult,
                     reads=[pC[j], coef], writes=[on])
                if j == 0:
                    kb.I("vector", "tensor_scalar", psl.t, pC[j].t[:, 65:321], rden.t[:, j:j + 1], None, ALU.mult,
                         reads=[pC[j], rden], writes=[psl])
                else:
                    kb.I("vector", "scalar_tensor_tensor", psl.t, pC[j].t[:, 65:321], rden.t[:, j:j + 1], psl.t, ALU.mult, ALU.add,
                         reads=[pC[j], rden, psl], writes=[psl])
            kb.I("vector", "tensor_tensor", sel.t, psl.t, KEEP.t, ALU.mult, reads=[psl, KEEP], writes=[sel])
            kb.I("vector", "tensor_tensor", sel.t, sel.t, FORCE.t, ALU.add, reads=[sel, FORCE], writes=[sel])
            kb.I("vector", "max", m8.t, sel.t, reads=[sel], writes=[m8])
            kb.I("vector", "match_replace", sel2.t, m8.t, sel.t, -9.0, reads=[m8, sel], writes=[sel2])
            kb.I("vector", "max", m8b.t, sel2.t, reads=[sel2], writes=[m8b])
            kb.I("vector", "tensor_scalar", sel2.t, sel.t, m8b.t[:, 7:8], None, ALU.is_ge, reads=[sel, m8b], writes=[sel2])
            kb.I("vector", "tensor_tensor", sel2.t, sel2.t, BEF.t, ALU.mult, reads=[sel2, BEF], writes=[sel2])
            kb.I("vector", "tensor_scalar", sel2.t, sel2.t, BIGNEG, -BIGNEG, ALU.mult, ALU.add, reads=[sel2], writes=[sel2])
            for hf in range(2):
                kb.I("tensor", "transpose", pM.t[:, hf * 128:(hf + 1) * 128], sel2.t[:, hf * 128:(hf + 1) * 128], ident.t,
                     reads=[sel2, ident], writes=[pM])
            kb.I("vector", "tensor_copy", BPT.t, pM.t[:, 0:256].rearrange("p (a q) -> p a q", a=2), reads=[pM], writes=[BPT])
            nsel = 8 * slot + 8
            for kc in range(nsel):
                bias_mm = (EXPM.t[:, (kc % 64) * 128:(kc % 64 + 1) * 128],
                           BPT.t[:, kc // 64, :].unsqueeze(1).broadcast_to([128, 4, 128]), [EXPM, BPT])
                pt_ = score_exp([(KST.t[:, kc * 128:(kc + 1) * 128], QAR.t[:, g, :], 0, 512, [KST, QAR])], None, [], bias_mm)
                pv(pt_, [VSA.t[:, kc, g, :]] * 4, [VSA], kc == 0, False)
            pt_ = score_exp([(OKS.t, QAR.t[:, g, :], 0, 512, [OKS, QAR])], TRI.t, [TRI])
            pv(pt_, [OVS.t[:, g, :]] * 4, [OVS], False, True)
            finish_branch(on, 256 * g, (GA, lambda j, g=g: 3 * (4 * g + j) + 1, True), None)
            for wi in range(5):
                msk = ATRI.t if wi == 0 else (TRI.t if wi == 4 else None)
                pt_ = score_exp([(WK.t[:, wi, :], QW.t[:, g, :], 0, 512, [WK, QW])], msk, [ATRI, TRI])
                pv(pt_, [WV.t[:, wi, g, :]] * 4, [WV], wi == 0, wi == 4)
            finish_branch(ol, 256 * g, (GAW, lambda j, g=g: 3 * (4 * g + j) + 2, False), None)
        for cid, (gi, o) in enumerate(DIL_CH):
            pairs = [(DK.t[:, cid, j // 2, :], QB.t[:, 4 * gi + j, :], j * 128, 128, [DK, QB]) for j in range(4)]
            pt_ = score_exp(pairs, DM.t[:, cid, :], [DM])
            pv(pt_, [DV.t[:, cid, j, :] for j in range(4)], [DV], cid == 0, cid == NDC - 1)
        finish_branch(ol, 512, None, None)
        for g4 in range(4):
            kb.I("tensor", "matmul", pM.t[:, g4 * 128:(g4 + 1) * 128], WST.t[:, g4, :], VN.t[:, g4 * 128:(g4 + 1) * 128], start=True, stop=True,
                 reads=[WST, VN], writes=[pM])
        for g4 in range(4):
            kb.I("vector", "scalar_tensor_tensor", ol.t[:, 768 + g4 * 128:768 + (g4 + 1) * 128], pM.t[:, g4 * 128:(g4 + 1) * 128],
                 BST.t[:, g4:g4 + 1], UU.t[:, g4 * 128:(g4 + 1) * 128], ALU.add, ALU.mult, reads=[pM, BST, UU], writes=[ol])
        kb.dma("sync", None, o_nsa[slot * 128:(slot + 1) * 128, :], on.t, reads=[on], out_dram=True)
        kb.dma("sync", None, o_loc[slot * 128:(slot + 1) * 128, :], ol.t, reads=[ol], out_dram=True)
    kb.finish()
    return kb.nc


def core_tokens(c):
    return np.concatenate([np.arange(128 * (8 * i + c), 128 * (8 * i + c) + 128) for i in range(NSLOT)])


_CONST = {}


def attn_consts():
    if _CONST:
        return _CONST
    f = np.float32
    ci = np.arange(1024)[:, None]; sj = np.arange(256)[None, :]
    M = ((ci >= 4 * sj - 1) & (ci <= 4 * sj + 3)).astype(NPBF)
    _CONST["mmap"] = np.ascontiguousarray(M.reshape(8, 128, 256).transpose(1, 0, 2))
    x = np.arange(8192)[None, :]; j = np.arange(128)[:, None]
    _CONST["expm"] = (j == x // 64).astype(NPBF)
    k = np.arange(128)[:, None]; q = np.arange(128)[None, :]
    _CONST["tri"] = (k <= q).astype(NPBF); _CONST["atri"] = (k > q).astype(NPBF)
    dm = np.zeros((128, NDC, 128), NPBF)
    for cid, (gi, o) in enumerate(DIL_CH):
        dil = DILS[gi]
        dlt = 128 * o + q - k
        dm[:, cid, :] = ((dlt % dil == 0) & (dlt >= 0) & (dlt <= 128 * dil)).astype(NPBF)
    _CONST["dm"] = dm
    _CONST["idn"] = np.eye(128, dtype=f)
    per = []
    for c in range(8):
        maskC = np.zeros((128, NSLOT, 2, 128), NPBF)
        keep = np.zeros((128, NSLOT, 256), f); force = np.zeros((128, NSLOT, 256), f); before = np.zeros((128, NSLOT, 256), f)
        for i in range(NSLOT):
            bb = 8 * i + c
            t = 128 * bb + np.arange(128)
            nck = ncmp_chunks(i)
            for m in range(2):
                ck = nck - 2 + m
                if ck < 0:
                    continue
                cmp_idx = 128 * ck + np.arange(128)
                maskC[:, i, m, :] = (16 * cmp_idx[:, None] + 31 <= t[None, :]).astype(NPBF)
            cur = (t // 64)[:, None]; jj = np.arange(256)[None, :]
            forced = (jj == 0) | (jj == cur) | (jj == cur - 1)
            future = jj * 64 > t[:, None]
            force[:, i, :] = np.where(forced, 1e6, np.where(future, -1.0, 0.0))
            keep[:, i, :] = (~(forced | future)).astype(f)
            before[:, i, :] = (jj < 2 * bb).astype(f)
        per.append(dict(maskC=maskC, keep=keep, force=force, before=before))
    _CONST["per"] = per
    return _CONST


def prep_attn(zb, zf, P):
    f = np.float32
    C = attn_consts()
    S = zb.shape[0]

    def chunked(a):
        return np.ascontiguousarray(np.moveaxis(a.reshape((a.shape[0] // 128, 128) + a.shape[1:]), 0, 1))

    def with_ones(v):
        return np.concatenate([v, np.ones(v.shape[:-1] + (1,), v.dtype)], -1)

    def padT(cols):
        o = np.zeros((128, 16400), NPBF); o[:, :S] = zb[:, cols:cols + 128].T
        return o

    def w1l(w):
        a = w.reshape(32, 64, 128).transpose(1, 0, 2).reshape(64, 4096)
        o = np.zeros((128, 2, 4096), f)
        o[0:64, 0] = a; o[64:128, 1] = a
        return o

    def zpad_groups(a):
        o = np.zeros(a.shape[:-1] + (2, a.shape[-1]), a.dtype)
        o[0:64, ..., 0, :] = a[0:64]; o[64:128, ..., 1, :] = a[64:128]
        return o
    posk = np.zeros((128, 32), f); posk[0:64] = P["phi_k_pos"].T
    posv = np.zeros((128, 32), f); posv[0:64] = P["phi_v_pos"].T
    ksT = np.ascontiguousarray(zb[:, C_KS:C_KS + 128].T)
    shared = dict(
        kcT=padT(C_KC), vcT=padT(C_VC), w1k=w1l(P["phi_k_w1"]), w1v=w1l(P["phi_v_w1"]), posk=posk, posv=posv,
        w2k=np.ascontiguousarray(np.concatenate([P["phi_k_w2"], P["phi_k_w2"]], 1)), w2v=np.ascontiguousarray(P["phi_v_w2"]),
        mmap=C["mmap"], expm=C["expm"], tri=C["tri"], atri=C["atri"], dm=C["dm"], idn=C["idn"],
        ksT=ksT, vsA=chunked(with_ones(zb[:, C_VS:C_VS + 128].reshape(S, 2, 64))),
        wsT=np.ascontiguousarray(P["sgu_w"].transpose(2, 0, 1)), bsT=np.ascontiguousarray(P["sgu_b"].T),
    )
    Z0 = np.concatenate([np.zeros((NT, zb.shape[1]), NPBF), zb], 0)
    onesp = np.concatenate([np.zeros((NT, 1), NPBF), np.ones((S, 1), NPBF)], 0)

    def qlay(rows, cols):
        a = zb[rows, cols:cols + 512].reshape(NSLOT, 128, 2, 4, 64)
        return zpad_groups(np.ascontiguousarray(a.transpose(2, 4, 0, 3, 1).reshape(128, NSLOT, 512)))
    maps = []
    for c in range(8):
        tok = core_tokens(c)
        loc = np.arange(NT * c, NT * (c + 1))
        win = slice(NT * c, NT * (c + 2))
        d = dict(shared)
        d.update(C["per"][c])
        d["qaT"] = qlay(tok, C_QA); d["qarT"] = qlay(tok, C_QAR); d["qwT"] = qlay(loc, C_QAR)
        d["own_ksT"] = np.ascontiguousarray(ksT[:, tok].reshape(128, NSLOT, 128))
        d["own_vs"] = np.ascontiguousarray(with_ones(zb[tok, C_VS:C_VS + 128].reshape(NSLOT, 128, 2, 64)).transpose(1, 0, 2, 3))
        zw = Z0[win]; ow = onesp[win]
        d["kwT_win"] = np.ascontiguousarray(zw[:, C_KW:C_KW + 128].T)
        vw = np.concatenate([zw[:, C_VW:C_VW + 128].reshape(2 * NT, 2, 64), np.broadcast_to(ow[:, None, :], (2 * NT, 2, 1))], -1)
        d["vw_win"] = chunked(vw)
        d["kbT_win"] = np.ascontiguousarray(zw[:, C_KB:C_KB + 768].T.reshape(6, 128, 2 * NT).transpose(1, 0, 2))
        vb = np.concatenate([zw[:, C_VB:C_VB + 768].reshape(2 * NT, 12, 64), np.broadcast_to(ow[:, None, :], (2 * NT, 12, 1))], -1)
        d["vb_win"] = chunked(vb)
        qb = zb[loc, C_QB:C_QB + 768].reshape(NSLOT, 128, 6, 2, 64).transpose(3, 4, 0, 2, 1)
        qz = np.zeros((128, NSLOT, 12, 128), NPBF)
        for hh in range(2):
            qz[64 * hh:64 * hh + 64, :, hh::2, :] = qb[hh]
        d["qbT"] = qz
        d["ga"] = np.ascontiguousarray(zf[tok, 0:24].reshape(NSLOT, 128, 24).transpose(1, 0, 2))
        d["gaw"] = np.ascontiguousarray(zf[loc, 0:24].reshape(NSLOT, 128, 24).transpose(1, 0, 2))
        d["vn"] = np.ascontiguousarray(zb[loc, C_V:C_V + 512].reshape(NSLOT, 128, 512).transpose(1, 0, 2))
        d["u"] = np.ascontiguousarray(zf[loc, 24:536].reshape(NSLOT, 128, 512).transpose(1, 0, 2))
        maps.append(d)
    return maps


def gather_attn(results):
    S = NT * 8
    o_nsa = np.zeros((S, 512), np.float32); o_loc = np.zeros((S, 1280), np.float32)
    for c in range(8):
        o_nsa[core_tokens(c)] = results[c]["o_nsa"]
        o_loc[NT * c:NT * (c + 1)] = results[c]["o_loc"]
    return o_nsa, o_loc


def build_merge():
    kb = KB()
    x = kb.dram("x", [NT, D], F32, "ExternalInput")
    on_d = kb.dram("o_nsa", [NT, 512], F32, "ExternalInput")
    ol_d = kb.dram("o_loc", [NT, 1280], F32, "ExternalInput")
    wgm_d = kb.dram("wgm", [D, 3 * D], F32, "ExternalInput")
    wa_d = kb.dram("wa", [512, D], F32, "ExternalInput")
    wb_d = kb.dram("wb", [256, D], F32, "ExternalInput")
    wc_d = kb.dram("wc", [512, D], F32, "ExternalInput")
    wo_d = kb.dram("wo", [D, D], F32, "ExternalInput")
    lng = kb.dram("lng", [128, D], F32, "ExternalInput")
    lnb = kb.dram("lnb", [128, D], F32, "ExternalInput")
    idn = kb.dram("idn", [128, 128], F32, "ExternalInput")
    y = kb.dram("y", [NT, D], F32, "ExternalOutput")

    WGM = [kb.sb("WGM%d" % i, [128, 3 * D], BF16) for i in range(8)]
    WA = kb.sb("WA", [128, 4, D], BF16); WB = kb.sb("WB", [128, 2, D], BF16); WC = kb.sb("WC", [128, 4, D], BF16)
    WO = kb.sb("WO", [128, 8, D], BF16)
    G = kb.sb("G", [128, D], F32); Bt = kb.sb("Bt", [128, D], F32)
    ident = kb.sb("ident", [128, 128], F32)
    xs = [kb.sb("xs%d" % i, [128, D], F32) for i in range(2)]
    ons = [kb.sb("ons%d" % i, [128, 512], F32) for i in range(2)]
    ols = [kb.sb("ols%d" % i, [128, 1280], F32) for i in range(2)]
    xT = kb.sb("xT", [128, 8, 128], BF16)
    oT = kb.sb("oT", [128, 14, 128], BF16)
    GM = kb.sb("GM", [128, 3 * D], F32)
    mg = kb.sb("mg", [128, D], F32)
    tmp = [kb.sb("tmp%d" % i, [128, 512], F32) for i in range(2)]
    mT = kb.sb("mT", [128, 8, 128], BF16)
    rr = [kb.sb("rr%d" % i, [128, D], F32) for i in range(2)]
    oo = [kb.sb("oo%d" % i, [128, D], F32) for i in range(2)]
    stats = kb.sb("stats", [128, 12], F32); mv = kb.sb("mv", [128, 2], F32); rstd = kb.sb("rstd", [128, 1], F32)
    pt = [kb.ps("pt%d" % i, [128, 512], F32) for i in range(2)]
    pg = [kb.ps("pg%d" % i, [128, 512], F32) for i in range(2)]
    py = [kb.ps("py%d" % i, [128, 512], F32) for i in range(2)]
    po = [kb.ps("po%d" % i, [128, 512], F32) for i in range(2)]

    kb.dma("sync", ident, ident.t, idn); kb.dma("sync", G, G.t, lng); kb.dma("sync", Bt, Bt.t, lnb)

    def load(s):
        p = s % 2
        kb.dma("sync", xs[p], xs[p].t, x[s * 128:(s + 1) * 128, :])
        kb.dma("sync", ons[p], ons[p].t, on_d[s * 128:(s + 1) * 128, :])
        kb.dma("sync", ols[p], ols[p].t, ol_d[s * 128:(s + 1) * 128, :])
    load(0)
    for i in range(8):
        kb.dma("gpsimd", WGM[i], WGM[i].t, wgm_d[i * 128:(i + 1) * 128, :])
    kb.dma("gpsimd", WA, WA.t, wa_d.rearrange("(a p) d -> p a d", p=128))
    kb.dma("gpsimd", WB, WB.t, wb_d.rearrange("(a p) d -> p a d", p=128))
    kb.dma("gpsimd", WC, WC.t, wc_d.rearrange("(a p) d -> p a d", p=128))
    kb.dma("gpsimd", WO, WO.t, wo_d.rearrange("(a p) d -> p a d", p=128))
    eps2 = LN_EPS / (ALPHA * ALPHA)
    nsub = NT // 128
    tcount = [0]

    def transpose_group(srcs, dst, dst0):
        pb = pt[tcount[0] % 2]
        tcount[0] += 1
        for i, (bf, c0) in enumerate(srcs):
            kb.I("tensor", "transpose", pb.t[:, i * 128:(i + 1) * 128], bf.t[:, c0:c0 + 128], ident.t, reads=[bf, ident], writes=[pb])
        n = len(srcs)
        kb.I("scalar", "copy", dst.t[:, dst0:dst0 + n, :], pb.t[:, 0:n * 128].rearrange("p (a b) -> p a b", a=n), reads=[pb], writes=[dst])

    for s in range(nsub):
        p = s % 2
        if s + 1 < nsub:
            load(s + 1)
        for hh in range(2):
            transpose_group([(xs[p], (hh * 4 + j) * 128) for j in range(4)], xT, hh * 4)
        transpose_group([(ons[p], j * 128) for j in range(4)], oT, 0)
        transpose_group([(ols[p], j * 128) for j in range(4)], oT, 4)
        transpose_group([(ols[p], (4 + j) * 128) for j in range(4)], oT, 8)
        transpose_group([(ols[p], (8 + j) * 128) for j in range(2)], oT, 12)
        for k in range(6):
            pb = pg[k % 2]
            for dc in range(8):
                kb.I("tensor", "matmul", pb.t, xT.t[:, dc, :], WGM[dc].t[:, k * 512:(k + 1) * 512], start=(dc == 0), stop=(dc == 7),
                     reads=[xT, WGM[dc]], writes=[pb])
            kb.I("scalar", "activation", GM.t[:, k * 512:(k + 1) * 512], pb.t, AF.Sigmoid, reads=[pb], writes=[GM])
        yc = 0
        for hh in range(2):
            cs = slice(hh * 512, (hh + 1) * 512)
            pb = py[yc % 2]; yc += 1
            for kc in range(8):
                kb.I("tensor", "matmul", pb.t, oT.t[:, kc, :], WA.t[:, kc % 4, cs], start=(kc == 0), stop=(kc == 7), reads=[oT, WA], writes=[pb])
            kb.I("vector", "tensor_tensor", mg.t[:, cs], pb.t, GM.t[:, hh * 512:(hh + 1) * 512], ALU.mult, reads=[pb, GM], writes=[mg])
            pb = py[yc % 2]; yc += 1
            for kc in range(2):
                kb.I("tensor", "matmul", pb.t, oT.t[:, 8 + kc, :], WB.t[:, kc, cs], start=(kc == 0), stop=(kc == 1), reads=[oT, WB], writes=[pb])
            kb.I("vector", "tensor_tensor", tmp[0].t, pb.t, GM.t[:, D + hh * 512:D + (hh + 1) * 512], ALU.mult, reads=[pb, GM], writes=[tmp[0]])
            kb.I("gpsimd", "tensor_tensor", mg.t[:, cs], mg.t[:, cs], tmp[0].t, ALU.add, reads=[mg, tmp[0]], writes=[mg])
            pb = py[yc % 2]; yc += 1
            for kc in range(4):
                kb.I("tensor", "matmul", pb.t, oT.t[:, 10 + kc, :], WC.t[:, kc, cs], start=(kc == 0), stop=(kc == 3), reads=[oT, WC], writes=[pb])
            kb.I("vector", "tensor_tensor", tmp[1].t, pb.t, GM.t[:, 2 * D + hh * 512:2 * D + (hh + 1) * 512], ALU.mult, reads=[pb, GM], writes=[tmp[1]])
            kb.I("gpsimd", "tensor_tensor", mg.t[:, cs], mg.t[:, cs], tmp[1].t, ALU.add, reads=[mg, tmp[1]], writes=[mg])
        for hh in range(2):
            transpose_group([(mg, (hh * 4 + j) * 128) for j in range(4)], mT, hh * 4)
        rb = rr[p]; ob = oo[p]
        for hh in range(2):
            for dc in range(8):
                kb.I("tensor", "matmul", po[hh].t, mT.t[:, dc, :], WO.t[:, dc, hh * 512:(hh + 1) * 512], start=(dc == 0), stop=(dc == 7),
                     reads=[mT, WO], writes=[po[hh]])
            kb.I("vector", "scalar_tensor_tensor", rb.t[:, hh * 512:(hh + 1) * 512], po[hh].t, 1.0 / ALPHA, xs[p].t[:, hh * 512:(hh + 1) * 512],
                 ALU.mult, ALU.add, reads=[po[hh], xs[p]], writes=[rb])
        emit_ln(kb, rb, ob, G, Bt, stats, mv, rstd, eps2)
        kb.dma("sync", None, y[s * 128:(s + 1) * 128, :], ob.t, reads=[ob], out_dram=True)
    kb.finish()
    return kb.nc


def _rope_tables_np():
    pos = np.arange(16384, dtype=np.float32)
    inv = (1.0 / (np.float32(10000.0) ** (np.arange(0, 64, 2, dtype=np.float32) / np.float32(64)))).astype(np.float32)
    ang = pos[:, None] * inv[None, :]
    cos, sin = np.cos(ang).astype(np.float32), np.sin(ang).astype(np.float32)
    return np.concatenate([cos, cos], 1), np.concatenate([-sin, sin], 1)


_NC = {}


def _prog(name, builder):
    if name not in _NC:
        _NC[name] = builder()
    return _NC[name]


def _run(name, builder, maps):
    return run_bass_kernel_spmd(builder(), maps, core_ids=list(range(8))).results


def _rows(a, c):
    return np.ascontiguousarray(a[NT * c:NT * (c + 1)])


def _bc(v):
    return np.ascontiguousarray(np.broadcast_to(np.asarray(v, np.float32), (128, v.shape[-1])))


def kernel(**inputs):
    f = np.float32
    x = np.asarray(inputs["x"], f)[0]
    idn = np.eye(128, dtype=f)
    CC, SS = _rope_tables_np()
    for l in range(DEPTH):
        P = {k: np.asarray(v[l], f) for k, v in inputs.items() if k != "x"}
        maps = [dict(x=_rows(x, c), wg=P["ffn1_gate"], wu=P["ffn1_up"], wd=P["ffn1_down"],
                     lng=_bc(P["ln_g"][0]), lnb=_bc(P["ln_b"][0]), idn=idn) for c in range(8)]
        x1 = np.concatenate([r["y"] for r in _run("ffn", build_ffn, maps)], 0)
        win = np.ascontiguousarray(P["w_in"][:, :ZC])
        maps = [dict(x=_rows(x1, c), win=win, idn=idn, cc=_rows(CC, c), ss=_rows(SS, c),
                     sg=_bc(P["sgu_ln_g"]), sb=_bc(P["sgu_ln_b"])) for c in range(8)]
        res = _run("win", build_win, maps)
        zb = np.concatenate([r["zb"] for r in res], 0)
        zf = np.concatenate([r["zf"] for r in res], 0)
        res = _run("attn", build_attn, prep_attn(zb, zf, P))
        o_nsa, o_loc = gather_attn(res)
        wgm = np.ascontiguousarray(P["w_in"][:, ZC:])
        maps = [dict(x=_rows(x1, c), o_nsa=_rows(o_nsa, c), o_loc=_rows(o_loc, c), wgm=wgm, wa=P["w_branch_a"], wb=P["w_branch_b"],
                     wc=P["w_branch_c"], wo=P["w_out"], lng=_bc(P["ln_g"][1]), lnb=_bc(P["ln_b"][1]), idn=idn) for c in range(8)]
        x2 = np.concatenate([r["y"] for r in _run("merge", build_merge, maps)], 0)
        maps = [dict(x=_rows(x2, c), wg=P["ffn2_gate"], wu=P["ffn2_up"], wd=P["ffn2_down"],
                     lng=_bc(P["ln_g"][2]), lnb=_bc(P["ln_b"][2]), idn=idn) for c in range(8)]
        x = np.concatenate([r["y"] for r in _run("ffn", build_ffn, maps)], 0)
    return x[None].astype(np.float32)
```

```python
import contextlib
import numpy as np
import ml_dtypes
import concourse.bass as bass
import concourse.mybir as mybir
from concourse.bass_utils import run_bass_kernel_spmd

F32 = mybir.dt.float32
BF16 = mybir.dt.bfloat16
AF = mybir.ActivationFunctionType
ALU = mybir.AluOpType
AX = mybir.AxisListType
NPBF = ml_dtypes.bfloat16

ENGS = ("tensor", "vector", "scalar", "gpsimd", "sync")


class Buf:
    def __init__(self, name, t):
        self.name = name
        self.t = t[:]
        self.w = {}
        self.r = {}
        self.dsem = None


class KB:
    def __init__(self):
        self.nc = bass.Bass("TRN2", target_bir_lowering=False)
        self.es = contextlib.ExitStack()
        self.q = {e: [] for e in ENGS}
        self.cnt = {e: 0 for e in ENGS}
        self.known = {e: {} for e in ENGS}
        self.sems = {}
        self.semcnt = {}
        self.out_events = {}
        for e in ENGS:
            self.sems[e] = self.es.enter_context(self.nc.semaphore("c_" + e))
        self.nd = 0
        global _LAST_KB
        _LAST_KB = self

    def dram(self, name, shape, dt, kind):
        return self.nc.dram_tensor(name, list(shape), dt, kind=kind).ap()

    def sb(self, name, shape, dt):
        t = self.es.enter_context(self.nc.sbuf_tensor(name, list(shape), dt))
        return Buf(name, t)

    def ps(self, name, shape, dt):
        t = self.es.enter_context(self.nc.psum_tensor(name, list(shape), dt))
        return Buf(name, t)

    def _newsem(self, name):
        s = self.es.enter_context(self.nc.semaphore(name))
        self.sems[name] = s
        self.semcnt[name] = 0
        return name

    def _waits(self, eng, reads, writes):
        need = {}
        for b in reads:
            for k, v in b.w.items():
                need[k] = max(need.get(k, 0), v)
        for b in writes:
            for k, v in b.w.items():
                if k == eng and eng != "sync" and eng != "gpsimd":
                    continue
                need[k] = max(need.get(k, 0), v)
            for k, v in b.r.items():
                if k == eng:
                    continue
                need[k] = max(need.get(k, 0), v)
        out = []
        kn = self.known[eng]
        for k, v in need.items():
            if eng == "tensor" and k == "tensor":
                continue
            if kn.get(k, 0) >= v:
                continue
            kn[k] = v
            out.append((k, v))
        return out

    def op(self, eng, fn, reads=(), writes=()):
        waits = self._waits(eng, reads, writes)
        self.cnt[eng] += 1
        n = self.cnt[eng]
        self.q[eng].append((waits, fn, (eng, 1)))
        for b in reads:
            b.r[eng] = n
        for b in writes:
            b.w = {eng: n}
            b.r = {}

    def I(self, eng, name, *args, reads=(), writes=(), **kw):
        self.op(eng, lambda e, a=args, k=kw, n=name: getattr(e, n)(*a, **k), reads=reads, writes=writes)

    def dma(self, eng, dst, out_ap, in_ap, reads=(), out_dram=False, **kw):
        writes = [dst] if dst is not None else []
        waits = self._waits(eng, reads, writes)
        if dst is not None:
            if dst.dsem is None:
                dst.dsem = self._newsem("d_%d_%s" % (self.nd, dst.name))
                self.nd += 1
            sk = dst.dsem
        else:
            b0 = reads[0]
            if getattr(b0, "ssem", None) is None:
                b0.ssem = self._newsem("s_%d_%s" % (self.nd, b0.name))
                self.nd += 1
            sk = b0.ssem
        self.semcnt[sk] += 16
        v = self.semcnt[sk]
        fn = lambda e, o=out_ap, i=in_ap, kw=kw: e.dma_start(out=o, in_=i, **kw)
        self.q[eng].append((waits, fn, (sk, 16)))
        for b in reads:
            b.r[sk] = v
        if dst is not None:
            neww = {k: val for k, val in dst.w.items() if k == sk}
            neww[sk] = v
            dst.w = neww
            dst.r = {}
        if out_dram:
            self.out_events[sk] = v

    def finish(self):
        nc = self.nc
        fin = [(k, v) for k, v in self.out_events.items()]
        with nc.Block() as block:
            def run(engname):
                def body(e):
                    for waits, fn, (sk, inc) in self.q[engname]:
                        for k, v in waits:
                            e.wait_ge(self.sems[k], v)
                        fn(e).then_inc(self.sems[sk], inc)
                    if engname == "sync":
                        for k, v in fin:
                            e.wait_ge(self.sems[k], v)
                return body
            block.tensor(run("tensor"))
            block.vector(run("vector"))
            block.scalar(run("scalar"))
            block.gpsimd(run("gpsimd"))
            block.sync(run("sync"))
        self.es.close()


D = 1024
FF = 2816
NT = 2048
DEPTH = 4
ALPHA = (2 * DEPTH) ** 0.25
LN_EPS = 1e-5
NFC = FF // 128
IN_COLS = 7704


def bcast_rows(ap_row, nparts):
    return ap_row.broadcast(0, nparts) if hasattr(ap_row, "broadcast") else ap_row


def emit_ln(kb, r, outb, G, Bt, stats, mv, rstd, eps):
    for hh in range(2):
        kb.I("vector", "bn_stats", stats.t[:, hh * 6:(hh + 1) * 6], r.t[:, hh * 512:(hh + 1) * 512], reads=[r], writes=[stats])
    kb.I("vector", "bn_aggr", mv.t[:, :], stats.t[:, :], reads=[stats], writes=[mv])
    kb.I("vector", "tensor_scalar", rstd.t[:, :], mv.t[:, 1:2], eps, None, ALU.add, reads=[mv], writes=[rstd])
    kb.I("scalar", "sqrt", rstd.t[:, :], rstd.t[:, :], reads=[rstd], writes=[rstd])
    kb.I("vector", "reciprocal", rstd.t[:, :], rstd.t[:, :], reads=[rstd], writes=[rstd])
    kb.I("vector", "tensor_scalar", outb.t[:, :], r.t[:, :], mv.t[:, 0:1], rstd.t[:, 0:1], ALU.subtract, ALU.mult,
         reads=[r, mv, rstd], writes=[outb])
    kb.I("gpsimd", "tensor_tensor", outb.t[:, :], outb.t[:, :], G.t[:, :], ALU.mult, reads=[outb, G], writes=[outb])
    kb.I("gpsimd", "tensor_tensor", outb.t[:, :], outb.t[:, :], Bt.t[:, :], ALU.add, reads=[outb, Bt], writes=[outb])


def build_ffn():
    kb = KB()
    x = kb.dram("x", [NT, D], F32, "ExternalInput")
    wg = kb.dram("wg", [D, FF], F32, "ExternalInput")
    wu = kb.dram("wu", [D, FF], F32, "ExternalInput")
    wd = kb.dram("wd", [FF, D], F32, "ExternalInput")
    lng = kb.dram("lng", [128, D], F32, "ExternalInput")
    lnb = kb.dram("lnb", [128, D], F32, "ExternalInput")
    idn = kb.dram("idn", [128, 128], F32, "ExternalInput")
    y = kb.dram("y", [NT, D], F32, "ExternalOutput")

    Wg = [kb.sb("Wg%d" % i, [128, FF], BF16) for i in range(8)]
    Wu = [kb.sb("Wu%d" % i, [128, FF], BF16) for i in range(8)]
    Wd = [kb.sb("Wd%d" % i, [128, 2 * D], BF16) for i in range(NFC // 2)]
    G = kb.sb("G", [128, D], F32)
    Bt = kb.sb("Bt", [128, D], F32)
    ident = kb.sb("ident", [128, 128], F32)
    TT = 256
    NSUB = TT // 128
    xs = [kb.sb("xs%d" % i, [128, D], F32) for i in range(2 * NSUB)]
    xT = [kb.sb("xT%d" % i, [128, 8, TT], BF16) for i in range(2)]
    hT = [kb.sb("hT%d" % i, [128, TT], BF16) for i in range(NFC)]
    sg = [kb.sb("sg%d" % i, [128, TT], F32) for i in range(2)]
    rr = [kb.sb("rr%d" % i, [128, D], F32) for i in range(2)]
    oo = [kb.sb("oo%d" % i, [128, D], F32) for i in range(2)]
    stats = kb.sb("stats", [128, 12], F32)
    mv = kb.sb("mv", [128, 2], F32)
    rstd = kb.sb("rstd", [128, 1], F32)
    pg = [kb.ps("pg%d" % i, [128, 512], F32) for i in range(2)]
    pu = [kb.ps("pu%d" % i, [128, 512], F32) for i in range(2)]
    pt = [kb.ps("pt%d" % i, [128, 512], F32) for i in range(2)]
    po = [kb.ps("po%d" % i, [128, 512], F32) for i in range(2)]

    kb.dma("sync", ident, ident.t[:, :], idn[:, :])
    kb.dma("sync", G, G.t[:, :], lng[:, :])
    kb.dma("sync", Bt, Bt.t[:, :], lnb[:, :])
    for s in range(NSUB):
        kb.dma("sync", xs[s], xs[s].t[:, :], x[s * 128:(s + 1) * 128, :])
    for i in range(8):
        kb.dma("gpsimd", Wg[i], Wg[i].t[:, :], wg[i * 128:(i + 1) * 128, :])
        kb.dma("gpsimd", Wu[i], Wu[i].t[:, :], wu[i * 128:(i + 1) * 128, :])
    for i in range(NFC // 2):
        kb.dma("gpsimd", Wd[i], Wd[i].t[:, :].rearrange("p (a d) -> p a d", a=2),
               wd[i * 256:(i + 1) * 256, :].rearrange("(a p) d -> p a d", p=128))

    eps2 = LN_EPS / (ALPHA * ALPHA)
    ntile = NT // TT
    for t in range(ntile):
        par = t % 2
        xcur = xs[par * NSUB:(par + 1) * NSUB]
        if t + 1 < ntile:
            nxt = xs[(1 - par) * NSUB:(2 - par) * NSUB]
            for s in range(NSUB):
                r0 = (t + 1) * TT + s * 128
                kb.dma("sync", nxt[s], nxt[s].t[:, :], x[r0:r0 + 128, :])
        for s in range(NSUB):
            for hh in range(2):
                for j in range(4):
                    dc = hh * 4 + j
                    kb.I("tensor", "transpose", pt[hh].t[:, j * 128:(j + 1) * 128],
                         xcur[s].t[:, dc * 128:(dc + 1) * 128], ident.t[:, :], reads=[xcur[s], ident], writes=[pt[hh]])
                kb.I("scalar", "copy", xT[par].t[:, hh * 4:(hh + 1) * 4, s * 128:(s + 1) * 128],
                     pt[hh].t[:, :].rearrange("p (a b) -> p a b", a=4), reads=[pt[hh]], writes=[xT[par]])
        for fc in range(NFC):
            pp = fc % 2
            for dc in range(8):
                kb.I("tensor", "matmul", pg[pp].t[:, :TT], Wg[dc].t[:, fc * 128:(fc + 1) * 128], xT[par].t[:, dc, :],
                     start=(dc == 0), stop=(dc == 7), reads=[Wg[dc], xT[par]], writes=[pg[pp]])
            for dc in range(8):
                kb.I("tensor", "matmul", pu[pp].t[:, :TT], Wu[dc].t[:, fc * 128:(fc + 1) * 128], xT[par].t[:, dc, :],
                     start=(dc == 0), stop=(dc == 7), reads=[Wu[dc], xT[par]], writes=[pu[pp]])
            kb.I("scalar", "activation", sg[pp].t[:, :], pg[pp].t[:, :TT], AF.Silu, reads=[pg[pp]], writes=[sg[pp]])
            kb.I("vector", "tensor_tensor", hT[fc].t[:, :], sg[pp].t[:, :], pu[pp].t[:, :TT], ALU.mult,
                 reads=[sg[pp], pu[pp]], writes=[hT[fc]])
        for s in range(NSUB):
            rb = rr[s % 2]
            ob = oo[s % 2]
            for hh in range(2):
                for fc in range(NFC):
                    kb.I("tensor", "matmul", po[hh].t[:, :], hT[fc].t[:, s * 128:(s + 1) * 128],
                         Wd[fc // 2].t[:, (fc % 2) * D + hh * 512:(fc % 2) * D + (hh + 1) * 512],
                         start=(fc == 0), stop=(fc == NFC - 1), reads=[hT[fc], Wd[fc // 2]], writes=[po[hh]])
                kb.I("vector", "scalar_tensor_tensor", rb.t[:, hh * 512:(hh + 1) * 512], po[hh].t[:, :], 0.5 / ALPHA,
                     xcur[s].t[:, hh * 512:(hh + 1) * 512], ALU.mult, ALU.add, reads=[po[hh], xcur[s]], writes=[rb])
            emit_ln(kb, rb, ob, G, Bt, stats, mv, rstd, eps2)
            r0 = t * TT + s * 128
            kb.dma("sync", None, y[r0:r0 + 128, :], ob.t[:, :], reads=[ob], out_dram=True)
    kb.finish()
    return kb.nc


ZC = 4632
ZW = ZC + 512
ZF = 536
C_QA, C_KC, C_VC, C_KS, C_VS, C_KW, C_VW, C_GA, C_QB, C_KB, C_VB, C_U, C_V, C_GM, C_QAR = (
    0, 512, 640, 768, 896, 1024, 1152, 1280, 1304, 2072, 2840, 3608, 4120, 4632, 4632)
GELU_K = 1.5957691216057308


def emit_gelu(kb, eng_v, zt, c0, n, tmp):
    xa = zt.t[:, c0:c0 + n]
    ta = tmp.t[:, 0:n]
    kb.I(eng_v, "tensor_tensor", ta, xa, xa, ALU.mult, reads=[zt], writes=[tmp])
    kb.I(eng_v, "tensor_scalar", ta, ta, 0.044715, 1.0, ALU.mult, ALU.add, reads=[tmp], writes=[tmp])
    kb.I(eng_v, "tensor_tensor", ta, ta, xa, ALU.mult, reads=[tmp, zt], writes=[tmp])
    kb.I("scalar", "activation", ta, ta, AF.Sigmoid, scale=GELU_K, reads=[tmp], writes=[tmp])
    kb.I(eng_v, "tensor_tensor", xa, xa, ta, ALU.mult, reads=[tmp, zt], writes=[zt])


def emit_rope(kb, zt, c_in, c_out, H, cc, ss, tmp):
    xin = zt.t[:, c_in:c_in + H * 64].rearrange("p (h d) -> p h d", d=64)
    xout = zt.t[:, c_out:c_out + H * 64].rearrange("p (h d) -> p h d", d=64)
    tt = tmp.t[:, 0:H * 64].rearrange("p (h d) -> p h d", d=64)
    ccb = cc.t[:, :].unsqueeze(1).broadcast_to([128, H, 64])
    s1 = ss.t[:, 0:32].unsqueeze(1).broadcast_to([128, H, 32])
    s2 = ss.t[:, 32:64].unsqueeze(1).broadcast_to([128, H, 32])
    kb.I("vector", "tensor_tensor", tt[:, :, 0:32], xin[:, :, 32:64], s1, ALU.mult, reads=[zt, ss], writes=[tmp])
    kb.I("vector", "tensor_tensor", tt[:, :, 32:64], xin[:, :, 0:32], s2, ALU.mult, reads=[zt, ss], writes=[tmp])
    kb.I("vector", "tensor_tensor", xout, xin, ccb, ALU.mult, reads=[zt, cc], writes=[zt])
    kb.I("vector", "tensor_tensor", xout, xout, tt, ALU.add, reads=[zt, tmp], writes=[zt])


def build_win():
    kb = KB()
    x = kb.dram("x", [NT, D], F32, "ExternalInput")
    win = kb.dram("win", [D, ZC], F32, "ExternalInput")
    idn = kb.dram("idn", [128, 128], F32, "ExternalInput")
    ccd = kb.dram("cc", [NT, 64], F32, "ExternalInput")
    ssd = kb.dram("ss", [NT, 64], F32, "ExternalInput")
    sgd = kb.dram("sg", [128, 512], F32, "ExternalInput")
    sbd = kb.dram("sb", [128, 512], F32, "ExternalInput")
    zbo = kb.dram("zb", [NT, ZW], BF16, "ExternalOutput")
    zfo = kb.dram("zf", [NT, ZF], F32, "ExternalOutput")

    W = [kb.sb("W%d" % i, [128, ZC], BF16) for i in range(8)]
    ident = kb.sb("ident", [128, 128], F32)
    SG = kb.sb("SG", [128, 512], F32)
    SB = kb.sb("SB", [128, 512], F32)
    xs = [kb.sb("xs%d" % i, [128, D], F32) for i in range(2)]
    cc = [kb.sb("cc%d" % i, [128, 64], F32) for i in range(2)]
    ss = [kb.sb("ss%d" % i, [128, 64], F32) for i in range(2)]
    xT = [kb.sb("xT%d" % i, [128, 8, 128], BF16) for i in range(2)]
    Z = [kb.sb("Z%d" % i, [128, ZW], F32) for i in range(2)]
    ZB = [kb.sb("ZB%d" % i, [128, ZW], BF16) for i in range(2)]
    ZFs = [kb.sb("ZF%d" % i, [128, ZF], F32) for i in range(2)]
    tmp = kb.sb("tmp", [128, 1024], F32)
    stats = kb.sb("stats", [128, 6], F32)
    mv = kb.sb("mv", [128, 2], F32)
    rstd = kb.sb("rstd", [128, 1], F32)
    pt = [kb.ps("pt%d" % i, [128, 512], F32) for i in range(2)]
    pz = [kb.ps("pz%d" % i, [128, 512], F32) for i in range(4)]

    kb.dma("sync", ident, ident.t, idn)
    kb.dma("sync", SG, SG.t, sgd)
    kb.dma("sync", SB, SB.t, sbd)

    def load(s):
        p = s % 2
        kb.dma("sync", xs[p], xs[p].t, x[s * 128:(s + 1) * 128, :])
        kb.dma("sync", cc[p], cc[p].t, ccd[s * 128:(s + 1) * 128, :])
        kb.dma("sync", ss[p], ss[p].t, ssd[s * 128:(s + 1) * 128, :])
    load(0)
    for i in range(8):
        kb.dma("gpsimd", W[i], W[i].t, win[i * 128:(i + 1) * 128, :])
    nsub = NT // 128
    nck = (ZC + 511) // 512
    for s in range(nsub):
        p = s % 2
        if s + 1 < nsub:
            load(s + 1)
        for hh in range(2):
            for j in range(4):
                dc = hh * 4 + j
                kb.I("tensor", "transpose", pt[hh].t[:, j * 128:(j + 1) * 128], xs[p].t[:, dc * 128:(dc + 1) * 128],
                     ident.t, reads=[xs[p], ident], writes=[pt[hh]])
            kb.I("scalar", "copy", xT[p].t[:, hh * 4:(hh + 1) * 4, :], pt[hh].t.rearrange("p (a b) -> p a b", a=4),
                 reads=[pt[hh]], writes=[xT[p]])
        zt = Z[p]
        for k in range(nck):
            c0 = k * 512
            n = min(512, ZC - c0)
            pb = pz[k % 4]
            for dc in range(8):
                kb.I("tensor", "matmul", pb.t[:, :n], xT[p].t[:, dc, :], W[dc].t[:, c0:c0 + n], start=(dc == 0), stop=(dc == 7),
                     reads=[xT[p], W[dc]], writes=[pb])
            if k % 2 == 0:
                kb.I("vector", "tensor_copy", zt.t[:, c0:c0 + n], pb.t[:, :n], reads=[pb], writes=[zt])
            else:
                kb.I("scalar", "copy", zt.t[:, c0:c0 + n], pb.t[:, :n], reads=[pb], writes=[zt])
        kb.I("scalar", "activation", zt.t[:, C_GA:C_GA + 24], zt.t[:, C_GA:C_GA + 24], AF.Sigmoid, reads=[zt], writes=[zt])
        emit_gelu(kb, "gpsimd", zt, C_U, 1024, tmp)
        kb.I("vector", "bn_stats", stats.t, zt.t[:, C_V:C_V + 512], reads=[zt], writes=[stats])
        kb.I("vector", "bn_aggr", mv.t, stats.t, reads=[stats], writes=[mv])
        kb.I("vector", "tensor_scalar", rstd.t, mv.t[:, 1:2], LN_EPS, None, ALU.add, reads=[mv], writes=[rstd])
        kb.I("scalar", "sqrt", rstd.t, rstd.t, reads=[rstd], writes=[rstd])
        kb.I("vector", "reciprocal", rstd.t, rstd.t, reads=[rstd], writes=[rstd])
        kb.I("vector", "tensor_scalar", zt.t[:, C_V:C_V + 512], zt.t[:, C_V:C_V + 512], mv.t[:, 0:1], rstd.t[:, 0:1],
             ALU.subtract, ALU.mult, reads=[zt, mv, rstd], writes=[zt])
        kb.I("vector", "tensor_tensor", zt.t[:, C_V:C_V + 512], zt.t[:, C_V:C_V + 512], SG.t, ALU.mult, reads=[zt, SG], writes=[zt])
        kb.I("vector", "tensor_tensor", zt.t[:, C_V:C_V + 512], zt.t[:, C_V:C_V + 512], SB.t, ALU.add, reads=[zt, SB], writes=[zt])
        emit_rope(kb, zt, C_QA, C_QAR, 8, cc[p], ss[p], tmp)
        emit_rope(kb, zt, C_KS, C_KS, 2, cc[p], ss[p], tmp)
        emit_rope(kb, zt, C_KW, C_KW, 2, cc[p], ss[p], tmp)
        emit_rope(kb, zt, C_QB, C_QB, 12, cc[p], ss[p], tmp)
        emit_rope(kb, zt, C_KB, C_KB, 12, cc[p], ss[p], tmp)
        kb.I("scalar", "copy", ZB[p].t[:, 0:2560], zt.t[:, 0:2560], reads=[zt], writes=[ZB[p]])
        kb.I("vector", "tensor_copy", ZB[p].t[:, 2560:ZW], zt.t[:, 2560:ZW], reads=[zt], writes=[ZB[p]])
        kb.I("vector", "tensor_copy", ZFs[p].t[:, 0:24], zt.t[:, C_GA:C_GA + 24], reads=[zt], writes=[ZFs[p]])
        kb.I("vector", "tensor_copy", ZFs[p].t[:, 24:536], zt.t[:, C_U:C_U + 512], reads=[zt], writes=[ZFs[p]])
        kb.dma("sync", None, zbo[s * 128:(s + 1) * 128, :], ZB[p].t, reads=[ZB[p]], out_dram=True)
        kb.dma("sync", None, zfo[s * 128:(s + 1) * 128, :], ZFs[p].t, reads=[ZFs[p]], out_dram=True)
    kb.finish()
    return kb.nc


NSLOT = 16
DILS = (1, 4, 16)
DIL_CH = [(gi, o) for gi in range(3) for o in range(DILS[gi], -1, -1)]
DIL_C0 = [0, 2, 7]
NDC = len(DIL_CH)
BIGNEG = 10000.0
NWIN = 32


def ncmp_chunks(slot):
    return (64 * slot + 62) // 128 + 1


def build_attn():
    kb = KB()

    def inp(name, shape, dt=BF16):
        return kb.dram(name, shape, dt, "ExternalInput")
    qaT_d = inp("qaT", [128, NSLOT, 2, 512]); qarT_d = inp("qarT", [128, NSLOT, 2, 512])
    kcT_d = inp("kcT", [128, 16400]); vcT_d = inp("vcT", [128, 16400])
    w1k_d = inp("w1k", [128, 2, 4096], F32); w1v_d = inp("w1v", [128, 2, 4096], F32)
    posk_d = inp("posk", [128, 32], F32); posv_d = inp("posv", [128, 32], F32)
    w2k_d = inp("w2k", [128, 128], F32); w2v_d = inp("w2v", [128, 64], F32)
    mmap_d = inp("mmap", [128, 8, 256]); maskC_d = inp("maskC", [128, NSLOT, 2, 128])
    keep_d = inp("keep", [128, NSLOT, 256]); force_d = inp("force", [128, NSLOT, 256])
    before_d = inp("before", [128, NSLOT, 256])
    expm_d = inp("expm", [128, 8192])
    ksT_d = inp("ksT", [128, 16384]); vsA_d = inp("vsA", [128, 128, 2, 65])
    oksT_d = inp("own_ksT", [128, NSLOT, 128]); ovs_d = inp("own_vs", [128, NSLOT, 2, 65])
    kwT_d = inp("kwT_win", [128, NWIN * 128]); vw_d = inp("vw_win", [128, NWIN, 2, 65])
    qwT_d = inp("qwT", [128, NSLOT, 2, 512])
    tri_d = inp("tri", [128, 128]); atri_d = inp("atri", [128, 128]); dm_d = inp("dm", [128, NDC, 128])
    qbT_d = inp("qbT", [128, NSLOT, 12, 128]); kbT_d = inp("kbT_win", [128, 6, NWIN * 128]); vb_d = inp("vb_win", [128, NWIN, 12, 65])
    ga_d = inp("ga", [128, NSLOT, 24], F32); gaw_d = inp("gaw", [128, NSLOT, 24], F32)
    vn_d = inp("vn", [128, NSLOT, 512]); u_d = inp("u", [128, NSLOT, 512], F32)
    wsT_d = inp("wsT", [128, 4, 128], F32); bsT_d = inp("bsT", [128, 4], F32); idn_d = inp("idn", [128, 128], F32)
    o_nsa = kb.dram("o_nsa", [NT, 512], F32, "ExternalOutput")
    o_loc = kb.dram("o_loc", [NT, 1280], F32, "ExternalOutput")

    def sbl(name, shape, dt, src, eng="sync"):
        bf = kb.sb(name, shape, dt)
        kb.dma(eng, bf, bf.t, src)
        return bf

    ident = sbl("ident", [128, 128], F32, idn_d)
    TRI = sbl("TRI", [128, 128], BF16, tri_d)
    ATRI = sbl("ATRI", [128, 128], BF16, atri_d)
    DM = sbl("DM", [128, NDC, 128], BF16, dm_d)
    EXPM = sbl("EXPM", [128, 8192], BF16, expm_d)
    POSk = sbl("POSk", [128, 32], BF16, posk_d, "gpsimd")
    POSv = sbl("POSv", [128, 32], BF16, posv_d, "gpsimd")
    W2k = sbl("W2k", [128, 128], BF16, w2k_d, "gpsimd")
    W2v = sbl("W2v", [128, 64], BF16, w2v_d, "gpsimd")
    WST = sbl("WST", [128, 4, 128], BF16, wsT_d, "gpsimd")
    BST = sbl("BST", [128, 4], F32, bsT_d)
    KST = sbl("KST", [128, 16384], BF16, ksT_d)
    VSA = sbl("VSA", [128, 128, 2, 65], BF16, vsA_d)
    RHSC = kb.sb("RHSC", [128, 8, 2, 321], BF16)
    for g in range(2):
        kb.dma("sync", RHSC, RHSC.t[:, :, g, 65:321], mmap_d)
    KCMPT = kb.sb("KCMPT", [128, 1024], BF16)
    KCT = kb.sb("KCT", [128, 8208], BF16)
    W1 = kb.sb("W1", [128, 2, 32, 128], BF16)

    pS = [kb.ps("pS%d" % i, [128, 512], F32) for i in range(3)]
    pC = [kb.ps("pC%d" % i, [128, 512], F32) for i in range(4)]
    pM = kb.ps("pM", [128, 512], F32)

    OL = [kb.sb("OL%d" % i, [128, 1280], F32) for i in range(2)]

    class _View:
        def __init__(self, parent, ap):
            self.__dict__["parent"] = parent; self.__dict__["t"] = ap
        def __getattr__(self, k):
            return getattr(self.parent, k)
        def __setattr__(self, k, v):
            setattr(self.parent, k, v)
    hx = _View(OL[0], OL[0].t[:, 0:512])
    htmp = _View(OL[1], OL[1].t[:, 0:512])
    bias = kb.sb("bias", [128, 1], F32)
    hb = kb.sb("hb", [128, 512], BF16)
    kb.I("gpsimd", "memset", RHSC.t[:, :, :, 64:65], 1.0, reads=[], writes=[RHSC])
    for which in range(2):
        POS, src_d, w1_d = (POSk, kcT_d, w1k_d) if which == 0 else (POSv, vcT_d, w1v_d)
        kb.dma("gpsimd", W1, W1.t, w1_d.rearrange("p g (j h) -> p g j h", j=32))
        for j in range(32):
            kb.I("tensor", "matmul", pM.t[:, 0:1], W1.t[:, 0, j, :], POS.t[:, j:j + 1], start=(j == 0), stop=(j == 31),
                 reads=[W1, POS], writes=[pM])
        kb.I("vector", "tensor_copy", bias.t, pM.t[:, 0:1], reads=[pM], writes=[bias])
        for nch in range(2):
            kb.dma("sync", KCT, KCT.t, src_d[:, 8192 * nch:8192 * nch + 8208])
            for g in range(2):
                ph = pC[(g * 2 + nch) % 2]
                for j in range(32):
                    kb.I("tensor", "matmul", ph.t, W1.t[:, g, j, :], KCT.t[:, j:j + 16 * 511 + 1:16],
                         start=(j == 0), stop=(j == 31), reads=[W1, KCT], writes=[ph])
                kb.I("scalar", "activation", hx.t, ph.t, AF.Identity, bias=bias.t[:, 0:1], reads=[ph, bias], writes=[hx])
                kb.I("vector", "tensor_tensor", htmp.t, hx.t, hx.t, ALU.mult, reads=[hx], writes=[htmp])
                kb.I("vector", "tensor_scalar", htmp.t, htmp.t, 0.044715, 1.0, ALU.mult, ALU.add, reads=[htmp], writes=[htmp])
                kb.I("vector", "tensor_tensor", htmp.t, htmp.t, hx.t, ALU.mult, reads=[htmp, hx], writes=[htmp])
                kb.I("scalar", "activation", htmp.t, htmp.t, AF.Sigmoid, scale=GELU_K, reads=[htmp], writes=[htmp])
                kb.I("vector", "tensor_tensor", hb.t, hx.t, htmp.t, ALU.mult, reads=[htmp, hx], writes=[hb])
                if which == 0:
                    kb.I("tensor", "matmul", pM.t, W2k.t, hb.t, start=True, stop=True, reads=[W2k, hb], writes=[pM])
                    kb.I("vector", "tensor_copy", KCMPT.t[64 * g:64 * g + 64, nch * 512:(nch + 1) * 512],
                         pM.t[64 * g:64 * g + 64, :], reads=[pM], writes=[KCMPT])
                else:
                    for q4 in range(4):
                        kb.I("tensor", "matmul", pM.t[:, q4 * 64:(q4 + 1) * 64], hb.t[:, q4 * 128:(q4 + 1) * 128], W2v.t,
                             start=True, stop=True, reads=[W2v, hb], writes=[pM])
                    kb.I("vector", "tensor_copy", RHSC.t[:, nch * 4:(nch + 1) * 4, g, 0:64],
                         pM.t[:, 0:256].rearrange("p (a d) -> p a d", a=4), reads=[pM], writes=[RHSC])

    for g in range(4):
        kb.I("vector", "tensor_tensor", WST.t[:, g, :], WST.t[:, g, :], TRI.t, ALU.mult, reads=[WST, TRI], writes=[WST])

    QA = [kb.sb("QA%d" % i, [128, 2, 512], BF16) for i in range(2)]; QAR = [kb.sb("QAR%d" % i, [128, 2, 512], BF16) for i in range(2)]
    QW = kb.sb("QW", [128, 2, 512], BF16)
    MC = [kb.sb("MC%d" % i, [128, 2, 128], BF16) for i in range(2)]
    KEEP = [kb.sb("KEEP%d" % i, [128, 256], BF16) for i in range(2)]; FORCE = [kb.sb("FORCE%d" % i, [128, 256], BF16) for i in range(2)]
    BEF = [kb.sb("BEF%d" % i, [128, 256], BF16) for i in range(2)]
    OKS = [kb.sb("OKS%d" % i, [128, 128], BF16) for i in range(2)]; OVS = [kb.sb("OVS%d" % i, [128, 2, 65], BF16) for i in range(2)]
    WK = kb.sb("WK", [128, 5, 128], BF16); WV = kb.sb("WV", [128, 5, 2, 65], BF16)
    QB = kb.sb("QB", [128, 12, 128], BF16); DK = kb.sb("DK", [128, NDC, 2, 128], BF16); DV = kb.sb("DV", [128, NDC, 4, 65], BF16)
    GA = [kb.sb("GA%d" % i, [128, 24], F32) for i in range(2)]; GAW = kb.sb("GAW", [128, 24], F32)
    VN = kb.sb("VN", [128, 512], BF16); UU = kb.sb("UU", [128, 512], F32)
    P = [kb.sb("P%d" % i, [128, 512], BF16) for i in range(4)]
    ON = [kb.sb("ON%d" % i, [128, 512], F32) for i in range(2)]
    psl = kb.sb("psl", [128, 256], F32)
    sel = kb.sb("sel", [128, 256], F32); sel2 = kb.sb("sel2", [128, 256], F32)
    m8 = kb.sb("m8", [128, 8], F32); m8b = kb.sb("m8b", [128, 8], F32)
    BPT = kb.sb("BPT", [128, 2, 128], BF16)
    rden = kb.sb("rden", [128, 4], F32); coef = kb.sb("coef", [128, 4], F32)
    sidx = [0]

    class Step:
        def __init__(self, kpairs, vaps, vbufs, first, last, width=65, mask=None, mask_bufs=(), bias_mm=None,
                     barrier=False, pre=None, post=None):
            self.kpairs = kpairs; self.vaps = vaps; self.vbufs = list(vbufs); self.first = first; self.last = last
            self.width = width; self.mask = mask; self.mask_bufs = list(mask_bufs); self.bias_mm = bias_mm
            self.barrier = barrier; self.pre = pre; self.post = post
            self.pe_done = False

    def step_pe(s):
        s.idx = sidx[0]; sidx[0] += 1
        pb = pS[s.idx % 3]
        for (la, ra, c0, ncol, rb) in s.kpairs:
            kb.I("tensor", "matmul", pb.t[:, c0:c0 + ncol], la, ra, start=True, stop=(s.bias_mm is None), reads=rb, writes=[pb])
        if s.bias_mm is not None:
            la, ra, rb = s.bias_mm
            kb.I("tensor", "matmul", pb.t, la, ra, start=False, stop=True, reads=rb, writes=[pb])
        s.pe_done = True

    def step_act(s):
        pb = pS[s.idx % 3]
        pt_ = P[s.idx % 4]
        kb.I("scalar", "activation", pt_.t, pb.t, AF.Exp, scale=0.125, reads=[pb], writes=[pt_])
        if s.mask is not None:
            kb.I("vector", "tensor_tensor", pt_.t.rearrange("p (h q) -> p h q", h=4), pt_.t.rearrange("p (h q) -> p h q", h=4),
                 s.mask.unsqueeze(1).broadcast_to([128, 4, 128]), ALU.mult, reads=[pt_] + s.mask_bufs, writes=[pt_])
        s.pt = pt_

    def step_pv(s):
        for j in range(4):
            kb.I("tensor", "matmul", pC[j].t[:, 0:s.width], s.pt.t[:, j * 128:(j + 1) * 128], s.vaps[j], start=s.first, stop=s.last,
                 reads=[s.pt] + s.vbufs, writes=[pC[j]])
        if s.post is not None:
            s.post()

    def run_steps(steps):
        n = len(steps)
        for k in range(n):
            s = steps[k]
            if not s.pe_done:
                if s.pre is not None:
                    s.pre()
                step_pe(s)
            if k + 1 < n and not steps[k + 1].barrier and steps[k + 1].pre is None:
                step_pe(steps[k + 1])
            step_act(s)
            step_pv(s)

    def recip_den(j, den_ap, den_buf):
        kb.I("vector", "tensor_scalar", rden.t[:, j:j + 1], den_ap, 1e-30, None, ALU.max, reads=[den_buf], writes=[rden])
        kb.I("vector", "reciprocal", rden.t[:, j:j + 1], rden.t[:, j:j + 1], reads=[rden], writes=[rden])

    def finish_branch(outb, col0, gates):
        for j in range(4):
            recip_den(j, pC[j].t[:, 64:65], pC[j])
            if gates is not None:
                gb, gc, accumulate = gates
                kb.I("vector", "tensor_tensor", coef.t[:, j:j + 1], rden.t[:, j:j + 1], gb.t[:, gc(j):gc(j) + 1], ALU.mult,
                     reads=[rden, gb], writes=[coef])
                sc = coef
            else:
                accumulate = False
                sc = rden
            oa = outb.t[:, col0 + j * 64:col0 + (j + 1) * 64]
            if accumulate:
                kb.I("vector", "scalar_tensor_tensor", oa, pC[j].t[:, 0:64], sc.t[:, j:j + 1], oa, ALU.mult, ALU.add,
                     reads=[pC[j], sc, outb], writes=[outb])
            else:
                kb.I("vector", "tensor_scalar", oa, pC[j].t[:, 0:64], sc.t[:, j:j + 1], None, ALU.mult,
                     reads=[pC[j], sc], writes=[outb])

    def cmp_finish_and_select(g, on, GAb, KEEPb, FORCEb, BEFb):
        for j in range(4):
            h = 4 * g + j
            recip_den(j, pC[j].t[:, 64:65], pC[j])
            kb.I("vector", "tensor_tensor", coef.t[:, j:j + 1], rden.t[:, j:j + 1], GAb.t[:, 3 * h:3 * h + 1], ALU.mult,
                 reads=[rden, GAb], writes=[coef])
            kb.I("vector", "tensor_scalar", on.t[:, h * 64:(h + 1) * 64], pC[j].t[:, 0:64], coef.t[:, j:j + 1], None, ALU.mult,
                 reads=[pC[j], coef], writes=[on])
            if j == 0:
                kb.I("vector", "tensor_scalar", psl.t, pC[j].t[:, 65:321], rden.t[:, j:j + 1], None, ALU.mult,
                     reads=[pC[j], rden], writes=[psl])
            else:
                kb.I("vector", "scalar_tensor_tensor", psl.t, pC[j].t[:, 65:321], rden.t[:, j:j + 1], psl.t, ALU.mult, ALU.add,
                     reads=[pC[j], rden, psl], writes=[psl])
        kb.I("vector", "tensor_tensor", sel.t, psl.t, KEEPb.t, ALU.mult, reads=[psl, KEEPb], writes=[sel])
        kb.I("vector", "tensor_tensor", sel.t, sel.t, FORCEb.t, ALU.add, reads=[sel, FORCEb], writes=[sel])
        kb.I("vector", "max", m8.t, sel.t, reads=[sel], writes=[m8])
        kb.I("vector", "match_replace", sel2.t, m8.t, sel.t, -9.0, reads=[m8, sel], writes=[sel2])
        kb.I("vector", "max", m8b.t, sel2.t, reads=[sel2], writes=[m8b])
        kb.I("vector", "tensor_scalar", sel2.t, sel.t, m8b.t[:, 7:8], None, ALU.is_ge, reads=[sel, m8b], writes=[sel2])
        kb.I("vector", "tensor_tensor", sel2.t, sel2.t, BEFb.t, ALU.mult, reads=[sel2, BEFb], writes=[sel2])
        kb.I("vector", "tensor_scalar", sel2.t, sel2.t, BIGNEG, -BIGNEG, ALU.mult, ALU.add, reads=[sel2], writes=[sel2])
        for hf in range(2):
            kb.I("tensor", "transpose", pM.t[:, hf * 128:(hf + 1) * 128], sel2.t[:, hf * 128:(hf + 1) * 128], ident.t,
                 reads=[sel2, ident], writes=[pM])
        kb.I("vector", "tensor_copy", BPT.t, pM.t[:, 0:256].rearrange("p (a q) -> p a q", a=2), reads=[pM], writes=[BPT])

    def nsa_loads(slot):
        p = slot % 2
        kb.dma("sync", QA[p], QA[p].t, qaT_d[:, slot]); kb.dma("sync", QAR[p], QAR[p].t, qarT_d[:, slot])
        kb.dma("sync", MC[p], MC[p].t, maskC_d[:, slot])
        kb.dma("sync", KEEP[p], KEEP[p].t, keep_d[:, slot]); kb.dma("sync", FORCE[p], FORCE[p].t, force_d[:, slot])
        kb.dma("sync", BEF[p], BEF[p].t, before_d[:, slot])
        kb.dma("sync", OKS[p], OKS[p].t, oksT_d[:, slot]); kb.dma("sync", OVS[p], OVS[p].t, ovs_d[:, slot])
        kb.dma("sync", GA[p], GA[p].t, ga_d[:, slot])

    nsa_loads(0)
    for slot in range(NSLOT):
        p = slot % 2
        on = ON[p]
        ol = OL[p]
        kb.dma("sync", QW, QW.t, qwT_d[:, slot])
        kb.dma("sync", WK, WK.t, kwT_d[:, (12 + slot) * 128:(17 + slot) * 128].rearrange("p (n k) -> p n k", k=128))
        kb.dma("sync", WV, WV.t, vw_d[:, 12 + slot:17 + slot])
        kb.dma("sync", GAW, GAW.t, gaw_d[:, slot])
        kb.dma("sync", QB, QB.t, qbT_d[:, slot])
        for gi in range(3):
            n = DILS[gi] + 1
            c_lo = 16 + slot - DILS[gi]
            for pr in range(2):
                kb.dma("sync", DK, DK.t[:, DIL_C0[gi]:DIL_C0[gi] + n, pr, :],
                       kbT_d[:, 2 * gi + pr, c_lo * 128:(c_lo + n) * 128].rearrange("p (n k) -> p n k", k=128))
            kb.dma("sync", DV, DV.t[:, DIL_C0[gi]:DIL_C0[gi] + n, :, :], vb_d[:, c_lo:c_lo + n, 4 * gi:4 * gi + 4, :])
        kb.dma("sync", VN, VN.t, vn_d[:, slot]); kb.dma("sync", UU, UU.t, u_d[:, slot])
        if slot + 1 < NSLOT:
            nsa_loads(slot + 1)

        steps = []
        for g in range(2):
            nck = ncmp_chunks(slot)
            for ck in range(nck):
                m = ck - (nck - 2)
                post = (lambda g=g, on=on, p=p: cmp_finish_and_select(g, on, GA[p], KEEP[p], FORCE[p], BEF[p])) if ck == nck - 1 else None
                steps.append(Step([(KCMPT.t[:, ck * 128:(ck + 1) * 128], QA[p].t[:, g, :], 0, 512, [KCMPT, QA[p]])],
                                  [RHSC.t[:, ck, g, :]] * 4, [RHSC], ck == 0, ck == nck - 1, width=321,
                                  mask=(MC[p].t[:, m, :] if m >= 0 else None), mask_bufs=[MC[p]], post=post))
            nsel = 8 * slot + 8
            for kc in range(nsel):
                bias_mm = (EXPM.t[:, (kc % 64) * 128:(kc % 64 + 1) * 128],
                           BPT.t[:, kc // 64, :].unsqueeze(1).broadcast_to([128, 4, 128]), [EXPM, BPT])
                steps.append(Step([(KST.t[:, kc * 128:(kc + 1) * 128], QAR[p].t[:, g, :], 0, 512, [KST, QAR[p]])],
                                  [VSA.t[:, kc, g, :]] * 4, [VSA], kc == 0, False, bias_mm=bias_mm, barrier=(kc == 0)))
            steps.append(Step([(OKS[p].t, QAR[p].t[:, g, :], 0, 512, [OKS[p], QAR[p]])], [OVS[p].t[:, g, :]] * 4, [OVS[p]], False, True,
                              mask=TRI.t, mask_bufs=[TRI],
                              post=(lambda g=g, on=on, p=p: finish_branch(on, 256 * g, (GA[p], lambda j, g=g: 3 * (4 * g + j) + 1, True)))))
            for wi in range(5):
                msk = ATRI.t if wi == 0 else (TRI.t if wi == 4 else None)
                post = (lambda g=g, ol=ol: finish_branch(ol, 256 * g, (GAW, lambda j, g=g: 3 * (4 * g + j) + 2, False))) if wi == 4 else None
                steps.append(Step([(WK.t[:, wi, :], QW.t[:, g, :], 0, 512, [WK, QW])], [WV.t[:, wi, g, :]] * 4, [WV], wi == 0, wi == 4,
                                  mask=msk, mask_bufs=[ATRI, TRI], post=post))
        for cid, (gi, o) in enumerate(DIL_CH):
            pairs = [(DK.t[:, cid, j // 2, :], QB.t[:, 4 * gi + j, :], j * 128, 128, [DK, QB]) for j in range(4)]
            post = (lambda ol=ol: finish_branch(ol, 512, None)) if cid == NDC - 1 else None
            steps.append(Step(pairs, [DV.t[:, cid, j, :] for j in range(4)], [DV], cid == 0, cid == NDC - 1,
                              mask=DM.t[:, cid, :], mask_bufs=[DM], post=post))
        run_steps(steps)
        for g4 in range(4):
            kb.I("tensor", "matmul", pM.t[:, g4 * 128:(g4 + 1) * 128], WST.t[:, g4, :], VN.t[:, g4 * 128:(g4 + 1) * 128], start=True, stop=True,
                 reads=[WST, VN], writes=[pM])
        for g4 in range(4):
            kb.I("vector", "scalar_tensor_tensor", ol.t[:, 768 + g4 * 128:768 + (g4 + 1) * 128], pM.t[:, g4 * 128:(g4 + 1) * 128],
                 BST.t[:, g4:g4 + 1], UU.t[:, g4 * 128:(g4 + 1) * 128], ALU.add, ALU.mult, reads=[pM, BST, UU], writes=[ol])
        kb.dma("sync", None, o_nsa[slot * 128:(slot + 1) * 128, :], on.t, reads=[on], out_dram=True)
        kb.dma("sync", None, o_loc[slot * 128:(slot + 1) * 128, :], ol.t, reads=[ol], out_dram=True)
    kb.finish()
    return kb.nc


def core_tokens(c):
    return np.concatenate([np.arange(128 * (8 * i + c), 128 * (8 * i + c) + 128) for i in range(NSLOT)])


_CONST = {}


def attn_consts():
    if _CONST:
        return _CONST
    f = np.float32
    ci = np.arange(1024)[:, None]; sj = np.arange(256)[None, :]
    M = ((ci >= 4 * sj - 1) & (ci <= 4 * sj + 3)).astype(NPBF)
    _CONST["mmap"] = np.ascontiguousarray(M.reshape(8, 128, 256).transpose(1, 0, 2))
    x = np.arange(8192)[None, :]; j = np.arange(128)[:, None]
    _CONST["expm"] = (j == x // 64).astype(NPBF)
    k = np.arange(128)[:, None]; q = np.arange(128)[None, :]
    _CONST["tri"] = (k <= q).astype(NPBF); _CONST["atri"] = (k > q).astype(NPBF)
    dm = np.zeros((128, NDC, 128), NPBF)
    for cid, (gi, o) in enumerate(DIL_CH):
        dil = DILS[gi]
        dlt = 128 * o + q - k
        dm[:, cid, :] = ((dlt % dil == 0) & (dlt >= 0) & (dlt <= 128 * dil)).astype(NPBF)
    _CONST["dm"] = dm
    _CONST["idn"] = np.eye(128, dtype=f)
    per = []
    for c in range(8):
        maskC = np.zeros((128, NSLOT, 2, 128), NPBF)
        keep = np.zeros((128, NSLOT, 256), f); force = np.zeros((128, NSLOT, 256), f); before = np.zeros((128, NSLOT, 256), f)
        for i in range(NSLOT):
            bb = 8 * i + c
            t = 128 * bb + np.arange(128)
            nck = ncmp_chunks(i)
            for m in range(2):
                ck = nck - 2 + m
                if ck < 0:
                    continue
                cmp_idx = 128 * ck + np.arange(128)
                maskC[:, i, m, :] = (16 * cmp_idx[:, None] + 31 <= t[None, :]).astype(NPBF)
            cur = (t // 64)[:, None]; jj = np.arange(256)[None, :]
            forced = (jj == 0) | (jj == cur) | (jj == cur - 1)
            future = jj * 64 > t[:, None]
            force[:, i, :] = np.where(forced, 1e6, np.where(future, -1.0, 0.0))
            keep[:, i, :] = (~(forced | future)).astype(f)
            before[:, i, :] = (jj < 2 * bb).astype(f)
        per.append(dict(maskC=maskC, keep=keep.astype(NPBF), force=force.astype(NPBF), before=before.astype(NPBF)))
    _CONST["per"] = per
    return _CONST


def prep_attn(zb, zf, P):
    f = np.float32
    C = attn_consts()
    S = zb.shape[0]

    def chunked(a):
        return np.ascontiguousarray(np.moveaxis(a.reshape((a.shape[0] // 128, 128) + a.shape[1:]), 0, 1))

    def with_ones(v):
        return np.concatenate([v, np.ones(v.shape[:-1] + (1,), v.dtype)], -1)

    def padT(cols):
        o = np.zeros((128, 16400), NPBF); o[:, :S] = zb[:, cols:cols + 128].T
        return o

    def w1l(w):
        a = w.reshape(32, 64, 128).transpose(1, 0, 2).reshape(64, 4096)
        o = np.zeros((128, 2, 4096), f)
        o[0:64, 0] = a; o[64:128, 1] = a
        return o

    def zpad_groups(a):
        o = np.zeros(a.shape[:-1] + (2, a.shape[-1]), a.dtype)
        o[0:64, ..., 0, :] = a[0:64]; o[64:128, ..., 1, :] = a[64:128]
        return o
    posk = np.zeros((128, 32), f); posk[0:64] = P["phi_k_pos"].T
    posv = np.zeros((128, 32), f); posv[0:64] = P["phi_v_pos"].T
    ksT = np.ascontiguousarray(zb[:, C_KS:C_KS + 128].T)
    shared = dict(
        kcT=padT(C_KC), vcT=padT(C_VC), w1k=w1l(P["phi_k_w1"]), w1v=w1l(P["phi_v_w1"]), posk=posk, posv=posv,
        w2k=np.ascontiguousarray(np.concatenate([P["phi_k_w2"], P["phi_k_w2"]], 1)), w2v=np.ascontiguousarray(P["phi_v_w2"]),
        mmap=C["mmap"], expm=C["expm"], tri=C["tri"], atri=C["atri"], dm=C["dm"], idn=C["idn"],
        ksT=ksT, vsA=chunked(with_ones(zb[:, C_VS:C_VS + 128].reshape(S, 2, 64))),
        wsT=np.ascontiguousarray(P["sgu_w"].transpose(2, 0, 1)), bsT=np.ascontiguousarray(P["sgu_b"].T),
    )
    Z0 = np.concatenate([np.zeros((NT, zb.shape[1]), NPBF), zb], 0)
    onesp = np.concatenate([np.zeros((NT, 1), NPBF), np.ones((S, 1), NPBF)], 0)

    def qlay(rows, cols):
        a = zb[rows, cols:cols + 512].reshape(NSLOT, 128, 2, 4, 64)
        return zpad_groups(np.ascontiguousarray(a.transpose(2, 4, 0, 3, 1).reshape(128, NSLOT, 512)))
    maps = []
    for c in range(8):
        tok = core_tokens(c)
        loc = np.arange(NT * c, NT * (c + 1))
        win = slice(NT * c, NT * (c + 2))
        d = dict(shared)
        d.update(C["per"][c])
        d["qaT"] = qlay(tok, C_QA); d["qarT"] = qlay(tok, C_QAR); d["qwT"] = qlay(loc, C_QAR)
        d["own_ksT"] = np.ascontiguousarray(ksT[:, tok].reshape(128, NSLOT, 128))
        d["own_vs"] = np.ascontiguousarray(with_ones(zb[tok, C_VS:C_VS + 128].reshape(NSLOT, 128, 2, 64)).transpose(1, 0, 2, 3))
        zw = Z0[win]; ow = onesp[win]
        d["kwT_win"] = np.ascontiguousarray(zw[:, C_KW:C_KW + 128].T)
        vw = np.concatenate([zw[:, C_VW:C_VW + 128].reshape(2 * NT, 2, 64), np.broadcast_to(ow[:, None, :], (2 * NT, 2, 1))], -1)
        d["vw_win"] = chunked(vw)
        d["kbT_win"] = np.ascontiguousarray(zw[:, C_KB:C_KB + 768].T.reshape(6, 128, 2 * NT).transpose(1, 0, 2))
        vb = np.concatenate([zw[:, C_VB:C_VB + 768].reshape(2 * NT, 12, 64), np.broadcast_to(ow[:, None, :], (2 * NT, 12, 1))], -1)
        d["vb_win"] = chunked(vb)
        qb = zb[loc, C_QB:C_QB + 768].reshape(NSLOT, 128, 6, 2, 64).transpose(3, 4, 0, 2, 1)
        qz = np.zeros((128, NSLOT, 12, 128), NPBF)
        for hh in range(2):
            qz[64 * hh:64 * hh + 64, :, hh::2, :] = qb[hh]
        d["qbT"] = qz
        d["ga"] = np.ascontiguousarray(zf[tok, 0:24].reshape(NSLOT, 128, 24).transpose(1, 0, 2))
        d["gaw"] = np.ascontiguousarray(zf[loc, 0:24].reshape(NSLOT, 128, 24).transpose(1, 0, 2))
        d["vn"] = np.ascontiguousarray(zb[loc, C_V:C_V + 512].reshape(NSLOT, 128, 512).transpose(1, 0, 2))
        d["u"] = np.ascontiguousarray(zf[loc, 24:536].reshape(NSLOT, 128, 512).transpose(1, 0, 2))
        maps.append(d)
    return maps


def gather_attn(results):
    S = NT * 8
    o_nsa = np.zeros((S, 512), np.float32); o_loc = np.zeros((S, 1280), np.float32)
    for c in range(8):
        o_nsa[core_tokens(c)] = results[c]["o_nsa"]
        o_loc[NT * c:NT * (c + 1)] = results[c]["o_loc"]
    return o_nsa, o_loc


def build_merge():
    kb = KB()
    x = kb.dram("x", [NT, D], F32, "ExternalInput")
    on_d = kb.dram("o_nsa", [NT, 512], F32, "ExternalInput")
    ol_d = kb.dram("o_loc", [NT, 1280], F32, "ExternalInput")
    wgm_d = kb.dram("wgm", [D, 3 * D], F32, "ExternalInput")
    wa_d = kb.dram("wa", [512, D], F32, "ExternalInput")
    wb_d = kb.dram("wb", [256, D], F32, "ExternalInput")
    wc_d = kb.dram("wc", [512, D], F32, "ExternalInput")
    wo_d = kb.dram("wo", [D, D], F32, "ExternalInput")
    lng = kb.dram("lng", [128, D], F32, "ExternalInput")
    lnb = kb.dram("lnb", [128, D], F32, "ExternalInput")
    idn = kb.dram("idn", [128, 128], F32, "ExternalInput")
    y = kb.dram("y", [NT, D], F32, "ExternalOutput")

    WGM = [kb.sb("WGM%d" % i, [128, 3 * D], BF16) for i in range(8)]
    WA = kb.sb("WA", [128, 4, D], BF16); WB = kb.sb("WB", [128, 2, D], BF16); WC = kb.sb("WC", [128, 4, D], BF16)
    WO = kb.sb("WO", [128, 8, D], BF16)
    G = kb.sb("G", [128, D], F32); Bt = kb.sb("Bt", [128, D], F32)
    ident = kb.sb("ident", [128, 128], F32)
    xs = [kb.sb("xs%d" % i, [128, D], F32) for i in range(2)]
    ons = [kb.sb("ons%d" % i, [128, 512], F32) for i in range(2)]
    ols = [kb.sb("ols%d" % i, [128, 1280], F32) for i in range(2)]
    xT = kb.sb("xT", [128, 8, 128], BF16)
    oT = kb.sb("oT", [128, 14, 128], BF16)
    GM = kb.sb("GM", [128, 3 * D], F32)
    mg = kb.sb("mg", [128, D], F32)
    tmp = [kb.sb("tmp%d" % i, [128, 512], F32) for i in range(2)]
    mT = kb.sb("mT", [128, 8, 128], BF16)
    rr = [kb.sb("rr%d" % i, [128, D], F32) for i in range(2)]
    oo = [kb.sb("oo%d" % i, [128, D], F32) for i in range(2)]
    stats = kb.sb("stats", [128, 12], F32); mv = kb.sb("mv", [128, 2], F32); rstd = kb.sb("rstd", [128, 1], F32)
    pt = [kb.ps("pt%d" % i, [128, 512], F32) for i in range(2)]
    pg = [kb.ps("pg%d" % i, [128, 512], F32) for i in range(2)]
    py = [kb.ps("py%d" % i, [128, 512], F32) for i in range(2)]
    po = [kb.ps("po%d" % i, [128, 512], F32) for i in range(2)]

    kb.dma("sync", ident, ident.t, idn); kb.dma("sync", G, G.t, lng); kb.dma("sync", Bt, Bt.t, lnb)

    def load(s):
        p = s % 2
        kb.dma("sync", xs[p], xs[p].t, x[s * 128:(s + 1) * 128, :])
        kb.dma("sync", ons[p], ons[p].t, on_d[s * 128:(s + 1) * 128, :])
        kb.dma("sync", ols[p], ols[p].t, ol_d[s * 128:(s + 1) * 128, :])
    load(0)
    for i in range(8):
        kb.dma("gpsimd", WGM[i], WGM[i].t, wgm_d[i * 128:(i + 1) * 128, :])
    kb.dma("gpsimd", WA, WA.t, wa_d.rearrange("(a p) d -> p a d", p=128))
    kb.dma("gpsimd", WB, WB.t, wb_d.rearrange("(a p) d -> p a d", p=128))
    kb.dma("gpsimd", WC, WC.t, wc_d.rearrange("(a p) d -> p a d", p=128))
    kb.dma("gpsimd", WO, WO.t, wo_d.rearrange("(a p) d -> p a d", p=128))
    eps2 = LN_EPS / (ALPHA * ALPHA)
    nsub = NT // 128
    tcount = [0]

    def transpose_group(srcs, dst, dst0):
        pb = pt[tcount[0] % 2]
        tcount[0] += 1
        for i, (bf, c0) in enumerate(srcs):
            kb.I("tensor", "transpose", pb.t[:, i * 128:(i + 1) * 128], bf.t[:, c0:c0 + 128], ident.t, reads=[bf, ident], writes=[pb])
        n = len(srcs)
        kb.I("scalar", "copy", dst.t[:, dst0:dst0 + n, :], pb.t[:, 0:n * 128].rearrange("p (a b) -> p a b", a=n), reads=[pb], writes=[dst])

    for s in range(nsub):
        p = s % 2
        if s + 1 < nsub:
            load(s + 1)
        for hh in range(2):
            transpose_group([(xs[p], (hh * 4 + j) * 128) for j in range(4)], xT, hh * 4)
        transpose_group([(ons[p], j * 128) for j in range(4)], oT, 0)
        transpose_group([(ols[p], j * 128) for j in range(4)], oT, 4)
        transpose_group([(ols[p], (4 + j) * 128) for j in range(4)], oT, 8)
        transpose_group([(ols[p], (8 + j) * 128) for j in range(2)], oT, 12)
        for k in range(6):
            pb = pg[k % 2]
            for dc in range(8):
                kb.I("tensor", "matmul", pb.t, xT.t[:, dc, :], WGM[dc].t[:, k * 512:(k + 1) * 512], start=(dc == 0), stop=(dc == 7),
                     reads=[xT, WGM[dc]], writes=[pb])
            kb.I("scalar", "activation", GM.t[:, k * 512:(k + 1) * 512], pb.t, AF.Sigmoid, reads=[pb], writes=[GM])
        yc = 0
        for hh in range(2):
            cs = slice(hh * 512, (hh + 1) * 512)
            pb = py[yc % 2]; yc += 1
            for kc in range(8):
                kb.I("tensor", "matmul", pb.t, oT.t[:, kc, :], WA.t[:, kc % 4, cs], start=(kc == 0), stop=(kc == 7), reads=[oT, WA], writes=[pb])
            kb.I("vector", "tensor_tensor", mg.t[:, cs], pb.t, GM.t[:, hh * 512:(hh + 1) * 512], ALU.mult, reads=[pb, GM], writes=[mg])
            pb = py[yc % 2]; yc += 1
            for kc in range(2):
                kb.I("tensor", "matmul", pb.t, oT.t[:, 8 + kc, :], WB.t[:, kc, cs], start=(kc == 0), stop=(kc == 1), reads=[oT, WB], writes=[pb])
            kb.I("vector", "tensor_tensor", tmp[0].t, pb.t, GM.t[:, D + hh * 512:D + (hh + 1) * 512], ALU.mult, reads=[pb, GM], writes=[tmp[0]])
            kb.I("gpsimd", "tensor_tensor", mg.t[:, cs], mg.t[:, cs], tmp[0].t, ALU.add, reads=[mg, tmp[0]], writes=[mg])
            pb = py[yc % 2]; yc += 1
            for kc in range(4):
                kb.I("tensor", "matmul", pb.t, oT.t[:, 10 + kc, :], WC.t[:, kc, cs], start=(kc == 0), stop=(kc == 3), reads=[oT, WC], writes=[pb])
            kb.I("vector", "tensor_tensor", tmp[1].t, pb.t, GM.t[:, 2 * D + hh * 512:2 * D + (hh + 1) * 512], ALU.mult, reads=[pb, GM], writes=[tmp[1]])
            kb.I("gpsimd", "tensor_tensor", mg.t[:, cs], mg.t[:, cs], tmp[1].t, ALU.add, reads=[mg, tmp[1]], writes=[mg])
        for hh in range(2):
            transpose_group([(mg, (hh * 4 + j) * 128) for j in range(4)], mT, hh * 4)
        rb = rr[p]; ob = oo[p]
        for hh in range(2):
            for dc in range(8):
                kb.I("tensor", "matmul", po[hh].t, mT.t[:, dc, :], WO.t[:, dc, hh * 512:(hh + 1) * 512], start=(dc == 0), stop=(dc == 7),
                     reads=[mT, WO], writes=[po[hh]])
            kb.I("vector", "scalar_tensor_tensor", rb.t[:, hh * 512:(hh + 1) * 512], po[hh].t, 1.0 / ALPHA, xs[p].t[:, hh * 512:(hh + 1) * 512],
                 ALU.mult, ALU.add, reads=[po[hh], xs[p]], writes=[rb])
        emit_ln(kb, rb, ob, G, Bt, stats, mv, rstd, eps2)
        kb.dma("sync", None, y[s * 128:(s + 1) * 128, :], ob.t, reads=[ob], out_dram=True)
    kb.finish()
    return kb.nc


def _rope_tables_np():
    pos = np.arange(16384, dtype=np.float32)
    inv = (1.0 / (np.float32(10000.0) ** (np.arange(0, 64, 2, dtype=np.float32) / np.float32(64)))).astype(np.float32)
    ang = pos[:, None] * inv[None, :]
    cos, sin = np.cos(ang).astype(np.float32), np.sin(ang).astype(np.float32)
    return np.concatenate([cos, cos], 1), np.concatenate([-sin, sin], 1)


_NC = {}


def _prog(name, builder):
    if name not in _NC:
        _NC[name] = builder()
    return _NC[name]


def _run(name, builder, maps):
    return run_bass_kernel_spmd(builder(), maps, core_ids=list(range(8))).results


def _rows(a, c):
    return np.ascontiguousarray(a[NT * c:NT * (c + 1)])


def _bc(v):
    return np.ascontiguousarray(np.broadcast_to(np.asarray(v, np.float32), (128, v.shape[-1])))


def kernel(**inputs):
    f = np.float32
    x = np.asarray(inputs["x"], f)[0]
    idn = np.eye(128, dtype=f)
    CC, SS = _rope_tables_np()
    for l in range(DEPTH):
        P = {k: np.asarray(v[l], f) for k, v in inputs.items() if k != "x"}
        maps = [dict(x=_rows(x, c), wg=P["ffn1_gate"], wu=P["ffn1_up"], wd=P["ffn1_down"],
                     lng=_bc(P["ln_g"][0]), lnb=_bc(P["ln_b"][0]), idn=idn) for c in range(8)]
        x1 = np.concatenate([r["y"] for r in _run("ffn", build_ffn, maps)], 0)
        win = np.ascontiguousarray(P["w_in"][:, :ZC])
        maps = [dict(x=_rows(x1, c), win=win, idn=idn, cc=_rows(CC, c), ss=_rows(SS, c),
                     sg=_bc(P["sgu_ln_g"]), sb=_bc(P["sgu_ln_b"])) for c in range(8)]
        res = _run("win", build_win, maps)
        zb = np.concatenate([r["zb"] for r in res], 0)
        zf = np.concatenate([r["zf"] for r in res], 0)
        res = _run("attn", build_attn, prep_attn(zb, zf, P))
        o_nsa, o_loc = gather_attn(res)
        wgm = np.ascontiguousarray(P["w_in"][:, ZC:])
        maps = [dict(x=_rows(x1, c), o_nsa=_rows(o_nsa, c), o_loc=_rows(o_loc, c), wgm=wgm, wa=P["w_branch_a"], wb=P["w_branch_b"],
                     wc=P["w_branch_c"], wo=P["w_out"], lng=_bc(P["ln_g"][1]), lnb=_bc(P["ln_b"][1]), idn=idn) for c in range(8)]
        x2 = np.concatenate([r["y"] for r in _run("merge", build_merge, maps)], 0)
        maps = [dict(x=_rows(x2, c), wg=P["ffn2_gate"], wu=P["ffn2_up"], wd=P["ffn2_down"],
                     lng=_bc(P["ln_g"][2]), lnb=_bc(P["ln_b"][2]), idn=idn) for c in range(8)]
        x = np.concatenate([r["y"] for r in _run("ffn", build_ffn, maps)], 0)
    return x[None].astype(np.float32)
```

```python
import contextlib
import numpy as np
import ml_dtypes
import concourse.bass as bass
import concourse.mybir as mybir
from concourse.bass_utils import run_bass_kernel_spmd

F32 = mybir.dt.float32
BF16 = mybir.dt.bfloat16
AF = mybir.ActivationFunctionType
ALU = mybir.AluOpType
AX = mybir.AxisListType
NPBF = ml_dtypes.bfloat16

ENGS = ("tensor", "vector", "scalar", "gpsimd", "sync")


class Buf:
    def __init__(self, name, t):
        self.name = name
        self.t = t[:]
        self.w = {}
        self.r = {}
        self.dsem = None


class KB:
    def __init__(self):
        self.nc = bass.Bass("TRN2", target_bir_lowering=False)
        self.es = contextlib.ExitStack()
        self.q = {e: [] for e in ENGS}
        self.cnt = {e: 0 for e in ENGS}
        self.known = {e: {} for e in ENGS}
        self.sems = {}
        self.semcnt = {}
        self.out_events = {}
        for e in ENGS:
            self.sems[e] = self.es.enter_context(self.nc.semaphore("c_" + e))
        self.nd = 0
        global _LAST_KB
        _LAST_KB = self

    def dram(self, name, shape, dt, kind):
        return self.nc.dram_tensor(name, list(shape), dt, kind=kind).ap()

    def sb(self, name, shape, dt):
        t = self.es.enter_context(self.nc.sbuf_tensor(name, list(shape), dt))
        return Buf(name, t)

    def ps(self, name, shape, dt):
        t = self.es.enter_context(self.nc.psum_tensor(name, list(shape), dt))
        return Buf(name, t)

    def _newsem(self, name):
        s = self.es.enter_context(self.nc.semaphore(name))
        self.sems[name] = s
        self.semcnt[name] = 0
        return name

    def _waits(self, eng, reads, writes):
        need = {}
        for b in reads:
            for k, v in b.w.items():
                need[k] = max(need.get(k, 0), v)
        for b in writes:
            for k, v in b.w.items():
                if k == eng and eng != "sync" and eng != "gpsimd":
                    continue
                need[k] = max(need.get(k, 0), v)
            for k, v in b.r.items():
                if k == eng:
                    continue
                need[k] = max(need.get(k, 0), v)
        out = []
        kn = self.known[eng]
        for k, v in need.items():
            if eng == "tensor" and k == "tensor":
                continue
            if kn.get(k, 0) >= v:
                continue
            kn[k] = v
            out.append((k, v))
        return out

    def op(self, eng, fn, reads=(), writes=()):
        waits = self._waits(eng, reads, writes)
        self.cnt[eng] += 1
        n = self.cnt[eng]
        self.q[eng].append((waits, fn, (eng, 1)))
        for b in reads:
            b.r[eng] = n
        for b in writes:
            b.w = {eng: n}
            b.r = {}

    def I(self, eng, name, *args, reads=(), writes=(), **kw):
        self.op(eng, lambda e, a=args, k=kw, n=name: getattr(e, n)(*a, **k), reads=reads, writes=writes)

    def dma(self, eng, dst, out_ap, in_ap, reads=(), out_dram=False, **kw):
        writes = [dst] if dst is not None else []
        waits = self._waits(eng, reads, writes)
        if dst is not None:
            if dst.dsem is None:
                dst.dsem = self._newsem("d_%d_%s" % (self.nd, dst.name))
                self.nd += 1
            sk = dst.dsem
        else:
            b0 = reads[0]
            if getattr(b0, "ssem", None) is None:
                b0.ssem = self._newsem("s_%d_%s" % (self.nd, b0.name))
                self.nd += 1
            sk = b0.ssem
        self.semcnt[sk] += 16
        v = self.semcnt[sk]
        fn = lambda e, o=out_ap, i=in_ap, kw=kw: e.dma_start(out=o, in_=i, **kw)
        self.q[eng].append((waits, fn, (sk, 16)))
        for b in reads:
            b.r[sk] = v
        if dst is not None:
            neww = {k: val for k, val in dst.w.items() if k == sk}
            neww[sk] = v
            dst.w = neww
            dst.r = {}
        if out_dram:
            self.out_events[sk] = v

    def finish(self):
        nc = self.nc
        fin = [(k, v) for k, v in self.out_events.items()]
        with nc.Block() as block:
            def run(engname):
                def body(e):
                    for waits, fn, (sk, inc) in self.q[engname]:
                        for k, v in waits:
                            e.wait_ge(self.sems[k], v)
                        fn(e).then_inc(self.sems[sk], inc)
                    if engname == "sync":
                        for k, v in fin:
                            e.wait_ge(self.sems[k], v)
                return body
            block.tensor(run("tensor"))
            block.vector(run("vector"))
            block.scalar(run("scalar"))
            block.gpsimd(run("gpsimd"))
            block.sync(run("sync"))
        self.es.close()


D = 1024
FF = 2816
NT = 2048
DEPTH = 4
ALPHA = (2 * DEPTH) ** 0.25
LN_EPS = 1e-5
NFC = FF // 128
IN_COLS = 7704


def bcast_rows(ap_row, nparts):
    return ap_row.broadcast(0, nparts) if hasattr(ap_row, "broadcast") else ap_row


def emit_ln(kb, r, outb, G, Bt, stats, mv, rstd, eps):
    for hh in range(2):
        kb.I("vector", "bn_stats", stats.t[:, hh * 6:(hh + 1) * 6], r.t[:, hh * 512:(hh + 1) * 512], reads=[r], writes=[stats])
    kb.I("vector", "bn_aggr", mv.t[:, :], stats.t[:, :], reads=[stats], writes=[mv])
    kb.I("vector", "tensor_scalar", rstd.t[:, :], mv.t[:, 1:2], eps, None, ALU.add, reads=[mv], writes=[rstd])
    kb.I("scalar", "sqrt", rstd.t[:, :], rstd.t[:, :], reads=[rstd], writes=[rstd])
    kb.I("vector", "reciprocal", rstd.t[:, :], rstd.t[:, :], reads=[rstd], writes=[rstd])
    kb.I("vector", "tensor_scalar", outb.t[:, :], r.t[:, :], mv.t[:, 0:1], rstd.t[:, 0:1], ALU.subtract, ALU.mult,
         reads=[r, mv, rstd], writes=[outb])
    kb.I("gpsimd", "tensor_tensor", outb.t[:, :], outb.t[:, :], G.t[:, :], ALU.mult, reads=[outb, G], writes=[outb])
    kb.I("gpsimd", "tensor_tensor", outb.t[:, :], outb.t[:, :], Bt.t[:, :], ALU.add, reads=[outb, Bt], writes=[outb])


def build_ffn():
    kb = KB()
    x = kb.dram("x", [NT, D], F32, "ExternalInput")
    wg = kb.dram("wg", [D, FF], F32, "ExternalInput")
    wu = kb.dram("wu", [D, FF], F32, "ExternalInput")
    wd = kb.dram("wd", [FF, D], F32, "ExternalInput")
    lng = kb.dram("lng", [128, D], F32, "ExternalInput")
    lnb = kb.dram("lnb", [128, D], F32, "ExternalInput")
    idn = kb.dram("idn", [128, 128], F32, "ExternalInput")
    y = kb.dram("y", [NT, D], F32, "ExternalOutput")

    Wg = [kb.sb("Wg%d" % i, [128, FF], BF16) for i in range(8)]
    Wu = [kb.sb("Wu%d" % i, [128, FF], BF16) for i in range(8)]
    Wd = [kb.sb("Wd%d" % i, [128, 2 * D], BF16) for i in range(NFC // 2)]
    G = kb.sb("G", [128, D], F32)
    Bt = kb.sb("Bt", [128, D], F32)
    ident = kb.sb("ident", [128, 128], F32)
    TT = 256
    NSUB = TT // 128
    xs = [kb.sb("xs%d" % i, [128, D], F32) for i in range(2 * NSUB)]
    xT = [kb.sb("xT%d" % i, [128, 8, TT], BF16) for i in range(2)]
    hT = [kb.sb("hT%d" % i, [128, TT], BF16) for i in range(NFC)]
    sg = [kb.sb("sg%d" % i, [128, TT], F32) for i in range(2)]
    rr = [kb.sb("rr%d" % i, [128, D], F32) for i in range(2)]
    oo = [kb.sb("oo%d" % i, [128, D], F32) for i in range(2)]
    stats = kb.sb("stats", [128, 12], F32)
    mv = kb.sb("mv", [128, 2], F32)
    rstd = kb.sb("rstd", [128, 1], F32)
    pg = [kb.ps("pg%d" % i, [128, 512], F32) for i in range(2)]
    pu = [kb.ps("pu%d" % i, [128, 512], F32) for i in range(2)]
    pt = [kb.ps("pt%d" % i, [128, 512], F32) for i in range(2)]
    po = [kb.ps("po%d" % i, [128, 512], F32) for i in range(2)]

    kb.dma("sync", ident, ident.t[:, :], idn[:, :])
    kb.dma("sync", G, G.t[:, :], lng[:, :])
    kb.dma("sync", Bt, Bt.t[:, :], lnb[:, :])
    for s in range(NSUB):
        kb.dma("sync", xs[s], xs[s].t[:, :], x[s * 128:(s + 1) * 128, :])
    for i in range(8):
        kb.dma("gpsimd", Wg[i], Wg[i].t[:, :], wg[i * 128:(i + 1) * 128, :])
        kb.dma("gpsimd", Wu[i], Wu[i].t[:, :], wu[i * 128:(i + 1) * 128, :])
    for i in range(NFC // 2):
        kb.dma("gpsimd", Wd[i], Wd[i].t[:, :].rearrange("p (a d) -> p a d", a=2),
               wd[i * 256:(i + 1) * 256, :].rearrange("(a p) d -> p a d", p=128))

    eps2 = LN_EPS / (ALPHA * ALPHA)
    ntile = NT // TT
    for t in range(ntile):
        par = t % 2
        xcur = xs[par * NSUB:(par + 1) * NSUB]
        if t + 1 < ntile:
            nxt = xs[(1 - par) * NSUB:(2 - par) * NSUB]
            for s in range(NSUB):
                r0 = (t + 1) * TT + s * 128
                kb.dma("sync", nxt[s], nxt[s].t[:, :], x[r0:r0 + 128, :])
        for s in range(NSUB):
            for hh in range(2):
                for j in range(4):
                    dc = hh * 4 + j
                    kb.I("tensor", "transpose", pt[hh].t[:, j * 128:(j + 1) * 128],
                         xcur[s].t[:, dc * 128:(dc + 1) * 128], ident.t[:, :], reads=[xcur[s], ident], writes=[pt[hh]])
                kb.I("scalar", "copy", xT[par].t[:, hh * 4:(hh + 1) * 4, s * 128:(s + 1) * 128],
                     pt[hh].t[:, :].rearrange("p (a b) -> p a b", a=4), reads=[pt[hh]], writes=[xT[par]])
        for fc in range(NFC):
            pp = fc % 2
            for dc in range(8):
                kb.I("tensor", "matmul", pg[pp].t[:, :TT], Wg[dc].t[:, fc * 128:(fc + 1) * 128], xT[par].t[:, dc, :],
                     start=(dc == 0), stop=(dc == 7), reads=[Wg[dc], xT[par]], writes=[pg[pp]])
            for dc in range(8):
                kb.I("tensor", "matmul", pu[pp].t[:, :TT], Wu[dc].t[:, fc * 128:(fc + 1) * 128], xT[par].t[:, dc, :],
                     start=(dc == 0), stop=(dc == 7), reads=[Wu[dc], xT[par]], writes=[pu[pp]])
            kb.I("scalar", "activation", sg[pp].t[:, :], pg[pp].t[:, :TT], AF.Silu, reads=[pg[pp]], writes=[sg[pp]])
            kb.I("vector", "tensor_tensor", hT[fc].t[:, :], sg[pp].t[:, :], pu[pp].t[:, :TT], ALU.mult,
                 reads=[sg[pp], pu[pp]], writes=[hT[fc]])
        for s in range(NSUB):
            rb = rr[s % 2]
            ob = oo[s % 2]
            for hh in range(2):
                for fc in range(NFC):
                    kb.I("tensor", "matmul", po[hh].t[:, :], hT[fc].t[:, s * 128:(s + 1) * 128],
                         Wd[fc // 2].t[:, (fc % 2) * D + hh * 512:(fc % 2) * D + (hh + 1) * 512],
                         start=(fc == 0), stop=(fc == NFC - 1), reads=[hT[fc], Wd[fc // 2]], writes=[po[hh]])
                kb.I("vector", "scalar_tensor_tensor", rb.t[:, hh * 512:(hh + 1) * 512], po[hh].t[:, :], 0.5 / ALPHA,
                     xcur[s].t[:, hh * 512:(hh + 1) * 512], ALU.mult, ALU.add, reads=[po[hh], xcur[s]], writes=[rb])
            emit_ln(kb, rb, ob, G, Bt, stats, mv, rstd, eps2)
            r0 = t * TT + s * 128
            kb.dma("sync", None, y[r0:r0 + 128, :], ob.t[:, :], reads=[ob], out_dram=True)
    kb.finish()
    return kb.nc


ZC = 4632
ZW = ZC + 512
ZF = 536
C_QA, C_KC, C_VC, C_KS, C_VS, C_KW, C_VW, C_GA, C_QB, C_KB, C_VB, C_U, C_V, C_GM, C_QAR = (
    0, 512, 640, 768, 896, 1024, 1152, 1280, 1304, 2072, 2840, 3608, 4120, 4632, 4632)
GELU_K = 1.5957691216057308


def emit_gelu(kb, eng_v, zt, c0, n, tmp):
    xa = zt.t[:, c0:c0 + n]
    ta = tmp.t[:, 0:n]
    kb.I(eng_v, "tensor_tensor", ta, xa, xa, ALU.mult, reads=[zt], writes=[tmp])
    kb.I(eng_v, "tensor_scalar", ta, ta, 0.044715, 1.0, ALU.mult, ALU.add, reads=[tmp], writes=[tmp])
    kb.I(eng_v, "tensor_tensor", ta, ta, xa, ALU.mult, reads=[tmp, zt], writes=[tmp])
    kb.I("scalar", "activation", ta, ta, AF.Sigmoid, scale=GELU_K, reads=[tmp], writes=[tmp])
    kb.I(eng_v, "tensor_tensor", xa, xa, ta, ALU.mult, reads=[tmp, zt], writes=[zt])


def emit_rope(kb, zt, c_in, c_out, H, cc, ss, tmp):
    xin = zt.t[:, c_in:c_in + H * 64].rearrange("p (h d) -> p h d", d=64)
    xout = zt.t[:, c_out:c_out + H * 64].rearrange("p (h d) -> p h d", d=64)
    tt = tmp.t[:, 0:H * 64].rearrange("p (h d) -> p h d", d=64)
    ccb = cc.t[:, :].unsqueeze(1).broadcast_to([128, H, 64])
    s1 = ss.t[:, 0:32].unsqueeze(1).broadcast_to([128, H, 32])
    s2 = ss.t[:, 32:64].unsqueeze(1).broadcast_to([128, H, 32])
    kb.I("vector", "tensor_tensor", tt[:, :, 0:32], xin[:, :, 32:64], s1, ALU.mult, reads=[zt, ss], writes=[tmp])
    kb.I("vector", "tensor_tensor", tt[:, :, 32:64], xin[:, :, 0:32], s2, ALU.mult, reads=[zt, ss], writes=[tmp])
    kb.I("vector", "tensor_tensor", xout, xin, ccb, ALU.mult, reads=[zt, cc], writes=[zt])
    kb.I("vector", "tensor_tensor", xout, xout, tt, ALU.add, reads=[zt, tmp], writes=[zt])


def build_win():
    kb = KB()
    x = kb.dram("x", [NT, D], F32, "ExternalInput")
    win = kb.dram("win", [D, ZC], F32, "ExternalInput")
    idn = kb.dram("idn", [128, 128], F32, "ExternalInput")
    ccd = kb.dram("cc", [NT, 64], F32, "ExternalInput")
    ssd = kb.dram("ss", [NT, 64], F32, "ExternalInput")
    sgd = kb.dram("sg", [128, 512], F32, "ExternalInput")
    sbd = kb.dram("sb", [128, 512], F32, "ExternalInput")
    zbo = kb.dram("zb", [NT, ZW], BF16, "ExternalOutput")
    zfo = kb.dram("zf", [NT, ZF], F32, "ExternalOutput")

    W = [kb.sb("W%d" % i, [128, ZC], BF16) for i in range(8)]
    ident = kb.sb("ident", [128, 128], F32)
    SG = kb.sb("SG", [128, 512], F32)
    SB = kb.sb("SB", [128, 512], F32)
    xs = [kb.sb("xs%d" % i, [128, D], F32) for i in range(2)]
    cc = [kb.sb("cc%d" % i, [128, 64], F32) for i in range(2)]
    ss = [kb.sb("ss%d" % i, [128, 64], F32) for i in range(2)]
    xT = [kb.sb("xT%d" % i, [128, 8, 128], BF16) for i in range(2)]
    Z = [kb.sb("Z%d" % i, [128, ZW], F32) for i in range(2)]
    ZB = [kb.sb("ZB%d" % i, [128, ZW], BF16) for i in range(2)]
    ZFs = [kb.sb("ZF%d" % i, [128, ZF], F32) for i in range(2)]
    tmp = kb.sb("tmp", [128, 1024], F32)
    stats = kb.sb("stats", [128, 6], F32)
    mv = kb.sb("mv", [128, 2], F32)
    rstd = kb.sb("rstd", [128, 1], F32)
    pt = [kb.ps("pt%d" % i, [128, 512], F32) for i in range(2)]
    pz = [kb.ps("pz%d" % i, [128, 512], F32) for i in range(4)]

    kb.dma("sync", ident, ident.t, idn)
    kb.dma("sync", SG, SG.t, sgd)
    kb.dma("sync", SB, SB.t, sbd)

    def load(s):
        p = s % 2
        kb.dma("sync", xs[p], xs[p].t, x[s * 128:(s + 1) * 128, :])
        kb.dma("sync", cc[p], cc[p].t, ccd[s * 128:(s + 1) * 128, :])
        kb.dma("sync", ss[p], ss[p].t, ssd[s * 128:(s + 1) * 128, :])
    load(0)
    for i in range(8):
        kb.dma("gpsimd", W[i], W[i].t, win[i * 128:(i + 1) * 128, :])
    nsub = NT // 128
    nck = (ZC + 511) // 512
    for s in range(nsub):
        p = s % 2
        if s + 1 < nsub:
            load(s + 1)
        for hh in range(2):
            for j in range(4):
                dc = hh * 4 + j
                kb.I("tensor", "transpose", pt[hh].t[:, j * 128:(j + 1) * 128], xs[p].t[:, dc * 128:(dc + 1) * 128],
                     ident.t, reads=[xs[p], ident], writes=[pt[hh]])
            kb.I("scalar", "copy", xT[p].t[:, hh * 4:(hh + 1) * 4, :], pt[hh].t.rearrange("p (a b) -> p a b", a=4),
                 reads=[pt[hh]], writes=[xT[p]])
        zt = Z[p]
        for k in range(nck):
            c0 = k * 512
            n = min(512, ZC - c0)
            pb = pz[k % 4]
            for dc in range(8):
                kb.I("tensor", "matmul", pb.t[:, :n], xT[p].t[:, dc, :], W[dc].t[:, c0:c0 + n], start=(dc == 0), stop=(dc == 7),
                     reads=[xT[p], W[dc]], writes=[pb])
            if k % 2 == 0:
                kb.I("vector", "tensor_copy", zt.t[:, c0:c0 + n], pb.t[:, :n], reads=[pb], writes=[zt])
            else:
                kb.I("scalar", "copy", zt.t[:, c0:c0 + n], pb.t[:, :n], reads=[pb], writes=[zt])
        kb.I("scalar", "activation", zt.t[:, C_GA:C_GA + 24], zt.t[:, C_GA:C_GA + 24], AF.Sigmoid, reads=[zt], writes=[zt])
        emit_gelu(kb, "gpsimd", zt, C_U, 1024, tmp)
        kb.I("vector", "bn_stats", stats.t, zt.t[:, C_V:C_V + 512], reads=[zt], writes=[stats])
        kb.I("vector", "bn_aggr", mv.t, stats.t, reads=[stats], writes=[mv])
        kb.I("vector", "tensor_scalar", rstd.t, mv.t[:, 1:2], LN_EPS, None, ALU.add, reads=[mv], writes=[rstd])
        kb.I("scalar", "sqrt", rstd.t, rstd.t, reads=[rstd], writes=[rstd])
        kb.I("vector", "reciprocal", rstd.t, rstd.t, reads=[rstd], writes=[rstd])
        kb.I("vector", "tensor_scalar", zt.t[:, C_V:C_V + 512], zt.t[:, C_V:C_V + 512], mv.t[:, 0:1], rstd.t[:, 0:1],
             ALU.subtract, ALU.mult, reads=[zt, mv, rstd], writes=[zt])
        kb.I("vector", "tensor_tensor", zt.t[:, C_V:C_V + 512], zt.t[:, C_V:C_V + 512], SG.t, ALU.mult, reads=[zt, SG], writes=[zt])
        kb.I("vector", "tensor_tensor", zt.t[:, C_V:C_V + 512], zt.t[:, C_V:C_V + 512], SB.t, ALU.add, reads=[zt, SB], writes=[zt])
        emit_rope(kb, zt, C_QA, C_QAR, 8, cc[p], ss[p], tmp)
        emit_rope(kb, zt, C_KS, C_KS, 2, cc[p], ss[p], tmp)
        emit_rope(kb, zt, C_KW, C_KW, 2, cc[p], ss[p], tmp)
        emit_rope(kb, zt, C_QB, C_QB, 12, cc[p], ss[p], tmp)
        emit_rope(kb, zt, C_KB, C_KB, 12, cc[p], ss[p], tmp)
        kb.I("scalar", "copy", ZB[p].t[:, 0:2560], zt.t[:, 0:2560], reads=[zt], writes=[ZB[p]])
        kb.I("vector", "tensor_copy", ZB[p].t[:, 2560:ZW], zt.t[:, 2560:ZW], reads=[zt], writes=[ZB[p]])
        kb.I("vector", "tensor_copy", ZFs[p].t[:, 0:24], zt.t[:, C_GA:C_GA + 24], reads=[zt], writes=[ZFs[p]])
        kb.I("vector", "tensor_copy", ZFs[p].t[:, 24:536], zt.t[:, C_U:C_U + 512], reads=[zt], writes=[ZFs[p]])
        kb.dma("sync", None, zbo[s * 128:(s + 1) * 128, :], ZB[p].t, reads=[ZB[p]], out_dram=True)
        kb.dma("sync", None, zfo[s * 128:(s + 1) * 128, :], ZFs[p].t, reads=[ZFs[p]], out_dram=True)
    kb.finish()
    return kb.nc


NSLOT = 16
DILS = (1, 4, 16)
DIL_CH = [(gi, o) for gi in range(3) for o in range(DILS[gi], -1, -1)]
DIL_C0 = [0, 2, 7]
NDC = len(DIL_CH)
BIGNEG = 10000.0
NWIN = 32


def ncmp_chunks(slot):
    return (64 * slot + 62) // 128 + 1


def build_attn():
    kb = KB()

    def inp(name, shape, dt=BF16):
        return kb.dram(name, shape, dt, "ExternalInput")
    qaT_d = inp("qaT", [128, NSLOT, 2, 512]); qarT_d = inp("qarT", [128, NSLOT, 2, 512])
    kcT_d = inp("kcT", [128, 16400]); vcT_d = inp("vcT", [128, 16400])
    w1k_d = inp("w1k", [128, 2, 4096], F32); w1v_d = inp("w1v", [128, 2, 4096], F32)
    posk_d = inp("posk", [128, 32], F32); posv_d = inp("posv", [128, 32], F32)
    w2k_d = inp("w2k", [128, 128], F32); w2v_d = inp("w2v", [128, 64], F32)
    mmap_d = inp("mmap", [128, 8, 256]); maskC_d = inp("maskC", [128, NSLOT, 2, 128])
    keep_d = inp("keep", [128, NSLOT, 256]); force_d = inp("force", [128, NSLOT, 256])
    before_d = inp("before", [128, NSLOT, 256])
    expm_d = inp("expm", [128, 8192])
    ksT_d = inp("ksT", [128, 16384]); vsA_d = inp("vsA", [128, 128, 2, 65])
    oksT_d = inp("own_ksT", [128, NSLOT, 128]); ovs_d = inp("own_vs", [128, NSLOT, 2, 65])
    kwT_d = inp("kwT_win", [128, NWIN * 128]); vw_d = inp("vw_win", [128, NWIN, 2, 65])
    qwT_d = inp("qwT", [128, NSLOT, 2, 512])
    tri_d = inp("tri", [128, 128]); atri_d = inp("atri", [128, 128]); dm_d = inp("dm", [128, NDC, 128])
    qbT_d = inp("qbT", [128, NSLOT, 12, 128]); kbT_d = inp("kbT_win", [128, 6, NWIN * 128]); vb_d = inp("vb_win", [128, NWIN, 12, 65])
    ga_d = inp("ga", [128, NSLOT, 24], F32); gaw_d = inp("gaw", [128, NSLOT, 24], F32)
    vn_d = inp("vn", [128, NSLOT, 512]); u_d = inp("u", [128, NSLOT, 512], F32)
    wsT_d = inp("wsT", [128, 4, 128], F32); bsT_d = inp("bsT", [128, 4], F32); idn_d = inp("idn", [128, 128], F32)
    o_nsa = kb.dram("o_nsa", [NT, 512], F32, "ExternalOutput")
    o_loc = kb.dram("o_loc", [NT, 1280], F32, "ExternalOutput")

    def sbl(name, shape, dt, src, eng="sync"):
        bf = kb.sb(name, shape, dt)
        kb.dma(eng, bf, bf.t, src)
        return bf

    ident = sbl("ident", [128, 128], F32, idn_d)
    TRI = sbl("TRI", [128, 128], BF16, tri_d)
    ATRI = sbl("ATRI", [128, 128], BF16, atri_d)
    DM = sbl("DM", [128, NDC, 128], BF16, dm_d)
    EXPM = sbl("EXPM", [128, 8192], BF16, expm_d)
    POSk = sbl("POSk", [128, 32], BF16, posk_d, "gpsimd")
    POSv = sbl("POSv", [128, 32], BF16, posv_d, "gpsimd")
    W2k = sbl("W2k", [128, 128], BF16, w2k_d, "gpsimd")
    W2v = sbl("W2v", [128, 64], BF16, w2v_d, "gpsimd")
    WST = sbl("WST", [128, 4, 128], BF16, wsT_d, "gpsimd")
    BST = sbl("BST", [128, 4], F32, bsT_d)
    KST = sbl("KST", [128, 16384], BF16, ksT_d)
    VSA = sbl("VSA", [128, 128, 2, 65], BF16, vsA_d)
    RHSC = kb.sb("RHSC", [128, 8, 2, 321], BF16)
    for g in range(2):
        kb.dma("sync", RHSC, RHSC.t[:, :, g, 65:321], mmap_d)
    KCMPT = kb.sb("KCMPT", [128, 1024], BF16)
    KCT = kb.sb("KCT", [128, 8208], BF16)
    W1 = kb.sb("W1", [128, 2, 32, 128], BF16)

    pS = [kb.ps("pS%d" % i, [128, 512], F32) for i in range(3)]
    pC = [kb.ps("pC%d" % i, [128, 512], F32) for i in range(4)]
    pM = kb.ps("pM", [128, 512], F32)

    OL = [kb.sb("OL%d" % i, [128, 1280], F32) for i in range(2)]

    class _View:
        def __init__(self, parent, ap):
            self.__dict__["parent"] = parent; self.__dict__["t"] = ap
        def __getattr__(self, k):
            return getattr(self.parent, k)
        def __setattr__(self, k, v):
            setattr(self.parent, k, v)
    hx = _View(OL[0], OL[0].t[:, 0:512])
    htmp = _View(OL[1], OL[1].t[:, 0:512])
    bias = kb.sb("bias", [128, 1], F32)
    hb = kb.sb("hb", [128, 512], BF16)
    kb.I("gpsimd", "memset", RHSC.t[:, :, :, 64:65], 1.0, reads=[], writes=[RHSC])
    for which in range(2):
        POS, src_d, w1_d = (POSk, kcT_d, w1k_d) if which == 0 else (POSv, vcT_d, w1v_d)
        kb.dma("gpsimd", W1, W1.t, w1_d.rearrange("p g (j h) -> p g j h", j=32))
        for j in range(32):
            kb.I("tensor", "matmul", pM.t[:, 0:1], W1.t[:, 0, j, :], POS.t[:, j:j + 1], start=(j == 0), stop=(j == 31),
                 reads=[W1, POS], writes=[pM])
        kb.I("vector", "tensor_copy", bias.t, pM.t[:, 0:1], reads=[pM], writes=[bias])
        for nch in range(2):
            kb.dma("sync", KCT, KCT.t, src_d[:, 8192 * nch:8192 * nch + 8208])
            for g in range(2):
                ph = pC[(g * 2 + nch) % 2]
                for j in range(32):
                    kb.I("tensor", "matmul", ph.t, W1.t[:, g, j, :], KCT.t[:, j:j + 16 * 511 + 1:16],
                         start=(j == 0), stop=(j == 31), reads=[W1, KCT], writes=[ph])
                kb.I("scalar", "activation", hx.t, ph.t, AF.Identity, bias=bias.t[:, 0:1], reads=[ph, bias], writes=[hx])
                kb.I("vector", "tensor_tensor", htmp.t, hx.t, hx.t, ALU.mult, reads=[hx], writes=[htmp])
                kb.I("vector", "tensor_scalar", htmp.t, htmp.t, 0.044715, 1.0, ALU.mult, ALU.add, reads=[htmp], writes=[htmp])
                kb.I("vector", "tensor_tensor", htmp.t, htmp.t, hx.t, ALU.mult, reads=[htmp, hx], writes=[htmp])
                kb.I("scalar", "activation", htmp.t, htmp.t, AF.Sigmoid, scale=GELU_K, reads=[htmp], writes=[htmp])
                kb.I("vector", "tensor_tensor", hb.t, hx.t, htmp.t, ALU.mult, reads=[htmp, hx], writes=[hb])
                if which == 0:
                    kb.I("tensor", "matmul", pM.t, W2k.t, hb.t, start=True, stop=True, reads=[W2k, hb], writes=[pM])
                    kb.I("vector", "tensor_copy", KCMPT.t[64 * g:64 * g + 64, nch * 512:(nch + 1) * 512],
                         pM.t[64 * g:64 * g + 64, :], reads=[pM], writes=[KCMPT])
                else:
                    for q4 in range(4):
                        kb.I("tensor", "matmul", pM.t[:, q4 * 64:(q4 + 1) * 64], hb.t[:, q4 * 128:(q4 + 1) * 128], W2v.t,
                             start=True, stop=True, reads=[W2v, hb], writes=[pM])
                    kb.I("vector", "tensor_copy", RHSC.t[:, nch * 4:(nch + 1) * 4, g, 0:64],
                         pM.t[:, 0:256].rearrange("p (a d) -> p a d", a=4), reads=[pM], writes=[RHSC])

    for g in range(4):
        kb.I("vector", "tensor_tensor", WST.t[:, g, :], WST.t[:, g, :], TRI.t, ALU.mult, reads=[WST, TRI], writes=[WST])

    QA = [kb.sb("QA%d" % i, [128, 2, 512], BF16) for i in range(2)]; QAR = [kb.sb("QAR%d" % i, [128, 2, 512], BF16) for i in range(2)]
    QW = kb.sb("QW", [128, 2, 512], BF16)
    MC = [kb.sb("MC%d" % i, [128, 2, 128], BF16) for i in range(2)]
    KEEP = [kb.sb("KEEP%d" % i, [128, 256], BF16) for i in range(2)]; FORCE = [kb.sb("FORCE%d" % i, [128, 256], BF16) for i in range(2)]
    BEF = [kb.sb("BEF%d" % i, [128, 256], BF16) for i in range(2)]
    OKS = [kb.sb("OKS%d" % i, [128, 128], BF16) for i in range(2)]; OVS = [kb.sb("OVS%d" % i, [128, 2, 65], BF16) for i in range(2)]
    WK = kb.sb("WK", [128, 5, 128], BF16); WV = kb.sb("WV", [128, 5, 2, 65], BF16)
    QB = kb.sb("QB", [128, 12, 128], BF16); DK = kb.sb("DK", [128, NDC, 2, 128], BF16); DV = kb.sb("DV", [128, NDC, 4, 65], BF16)
    GA = [kb.sb("GA%d" % i, [128, 24], F32) for i in range(2)]; GAW = kb.sb("GAW", [128, 24], F32)
    VN = kb.sb("VN", [128, 512], BF16); UU = kb.sb("UU", [128, 512], F32)
    P = [kb.sb("P%d" % i, [128, 512], BF16) for i in range(4)]
    ON = [kb.sb("ON%d" % i, [128, 512], F32) for i in range(2)]
    psl = kb.sb("psl", [128, 256], F32)
    sel = kb.sb("sel", [128, 256], F32); sel2 = kb.sb("sel2", [128, 256], F32)
    m8 = kb.sb("m8", [128, 8], F32); m8b = kb.sb("m8b", [128, 8], F32)
    BPT = kb.sb("BPT", [128, 2, 128], BF16)
    rden = kb.sb("rden", [128, 4], F32); coef = kb.sb("coef", [128, 4], F32)
    sidx = [0]

    class Step:
        def __init__(self, kpairs, vaps, vbufs, first, last, width=65, mask=None, mask_bufs=(), bias_mm=None,
                     barrier=False, pre=None, post=None):
            self.kpairs = kpairs; self.vaps = vaps; self.vbufs = list(vbufs); self.first = first; self.last = last
            self.width = width; self.mask = mask; self.mask_bufs = list(mask_bufs); self.bias_mm = bias_mm
            self.barrier = barrier; self.pre = pre; self.post = post
            self.pe_done = False

    def step_pe(s):
        s.idx = sidx[0]; sidx[0] += 1
        pb = pS[s.idx % 3]
        for (la, ra, c0, ncol, rb) in s.kpairs:
            kb.I("tensor", "matmul", pb.t[:, c0:c0 + ncol], la, ra, start=True, stop=(s.bias_mm is None), reads=rb, writes=[pb])
        if s.bias_mm is not None:
            la, ra, rb = s.bias_mm
            kb.I("tensor", "matmul", pb.t, la, ra, start=False, stop=True, reads=rb, writes=[pb])
        s.pe_done = True

    def step_act(s):
        pb = pS[s.idx % 3]
        pt_ = P[s.idx % 4]
        kb.I("scalar", "activation", pt_.t, pb.t, AF.Exp, scale=0.125, reads=[pb], writes=[pt_])
        if s.mask is not None:
            kb.I("vector", "tensor_tensor", pt_.t.rearrange("p (h q) -> p h q", h=4), pt_.t.rearrange("p (h q) -> p h q", h=4),
                 s.mask.unsqueeze(1).broadcast_to([128, 4, 128]), ALU.mult, reads=[pt_] + s.mask_bufs, writes=[pt_])
        s.pt = pt_

    def step_pv(s):
        for j in range(4):
            kb.I("tensor", "matmul", pC[j].t[:, 0:s.width], s.pt.t[:, j * 128:(j + 1) * 128], s.vaps[j], start=s.first, stop=s.last,
                 reads=[s.pt] + s.vbufs, writes=[pC[j]])
        if s.post is not None:
            s.post()

    def run_steps(steps):
        n = len(steps)
        for k in range(n):
            s = steps[k]
            if not s.pe_done:
                if s.pre is not None:
                    s.pre()
                step_pe(s)
            if k + 1 < n and not steps[k + 1].barrier and steps[k + 1].pre is None:
                step_pe(steps[k + 1])
            step_act(s)
            step_pv(s)

    def recip_den(j, den_ap, den_buf):
        kb.I("vector", "tensor_scalar", rden.t[:, j:j + 1], den_ap, 1e-30, None, ALU.max, reads=[den_buf], writes=[rden])
        kb.I("vector", "reciprocal", rden.t[:, j:j + 1], rden.t[:, j:j + 1], reads=[rden], writes=[rden])

    def finish_branch(outb, col0, gates):
        for j in range(4):
            recip_den(j, pC[j].t[:, 64:65], pC[j])
            if gates is not None:
                gb, gc, accumulate = gates
                kb.I("vector", "tensor_tensor", coef.t[:, j:j + 1], rden.t[:, j:j + 1], gb.t[:, gc(j):gc(j) + 1], ALU.mult,
                     reads=[rden, gb], writes=[coef])
                sc = coef
            else:
                accumulate = False
                sc = rden
            oa = outb.t[:, col0 + j * 64:col0 + (j + 1) * 64]
            if accumulate:
                kb.I("vector", "scalar_tensor_tensor", oa, pC[j].t[:, 0:64], sc.t[:, j:j + 1], oa, ALU.mult, ALU.add,
                     reads=[pC[j], sc, outb], writes=[outb])
            else:
                kb.I("vector", "tensor_scalar", oa, pC[j].t[:, 0:64], sc.t[:, j:j + 1], None, ALU.mult,
                     reads=[pC[j], sc], writes=[outb])

    def cmp_finish_and_select(g, on, GAb, KEEPb, FORCEb, BEFb):
        for j in range(4):
            h = 4 * g + j
            recip_den(j, pC[j].t[:, 64:65], pC[j])
            kb.I("vector", "tensor_tensor", coef.t[:, j:j + 1], rden.t[:, j:j + 1], GAb.t[:, 3 * h:3 * h + 1], ALU.mult,
                 reads=[rden, GAb], writes=[coef])
            kb.I("vector", "tensor_scalar", on.t[:, h * 64:(h + 1) * 64], pC[j].t[:, 0:64], coef.t[:, j:j + 1], None, ALU.mult,
                 reads=[pC[j], coef], writes=[on])
            if j == 0:
                kb.I("vector", "tensor_scalar", psl.t, pC[j].t[:, 65:321], rden.t[:, j:j + 1], None, ALU.mult,
                     reads=[pC[j], rden], writes=[psl])
            else:
                kb.I("vector", "scalar_tensor_tensor", psl.t, pC[j].t[:, 65:321], rden.t[:, j:j + 1], psl.t, ALU.mult, ALU.add,
                     reads=[pC[j], rden, psl], writes=[psl])
        kb.I("vector", "tensor_tensor", sel.t, psl.t, KEEPb.t, ALU.mult, reads=[psl, KEEPb], writes=[sel])
        kb.I("vector", "tensor_tensor", sel.t, sel.t, FORCEb.t, ALU.add, reads=[sel, FORCEb], writes=[sel])
        kb.I("vector", "max", m8.t, sel.t, reads=[sel], writes=[m8])
        kb.I("vector", "match_replace", sel2.t, m8.t, sel.t, -9.0, reads=[m8, sel], writes=[sel2])
        kb.I("vector", "max", m8b.t, sel2.t, reads=[sel2], writes=[m8b])
        kb.I("vector", "tensor_scalar", sel2.t, sel.t, m8b.t[:, 7:8], None, ALU.is_ge, reads=[sel, m8b], writes=[sel2])
        kb.I("vector", "tensor_tensor", sel2.t, sel2.t, BEFb.t, ALU.mult, reads=[sel2, BEFb], writes=[sel2])
        kb.I("vector", "tensor_scalar", sel2.t, sel2.t, BIGNEG, -BIGNEG, ALU.mult, ALU.add, reads=[sel2], writes=[sel2])
        for hf in range(2):
            kb.I("tensor", "transpose", pM.t[:, hf * 128:(hf + 1) * 128], sel2.t[:, hf * 128:(hf + 1) * 128], ident.t,
                 reads=[sel2, ident], writes=[pM])
        kb.I("vector", "tensor_copy", BPT.t, pM.t[:, 0:256].rearrange("p (a q) -> p a q", a=2), reads=[pM], writes=[BPT])

    def nsa_loads(slot):
        p = slot % 2
        kb.dma("sync", QA[p], QA[p].t, qaT_d[:, slot]); kb.dma("sync", QAR[p], QAR[p].t, qarT_d[:, slot])
        kb.dma("sync", MC[p], MC[p].t, maskC_d[:, slot])
        kb.dma("sync", KEEP[p], KEEP[p].t, keep_d[:, slot]); kb.dma("sync", FORCE[p], FORCE[p].t, force_d[:, slot])
        kb.dma("sync", BEF[p], BEF[p].t, before_d[:, slot])
        kb.dma("sync", OKS[p], OKS[p].t, oksT_d[:, slot]); kb.dma("sync", OVS[p], OVS[p].t, ovs_d[:, slot])
        kb.dma("sync", GA[p], GA[p].t, ga_d[:, slot])

    nsa_loads(0)
    for slot in range(NSLOT):
        p = slot % 2
        on = ON[p]
        ol = OL[p]
        kb.dma("sync", QW, QW.t, qwT_d[:, slot])
        kb.dma("sync", WK, WK.t, kwT_d[:, (12 + slot) * 128:(17 + slot) * 128].rearrange("p (n k) -> p n k", k=128))
        kb.dma("sync", WV, WV.t, vw_d[:, 12 + slot:17 + slot])
        kb.dma("sync", GAW, GAW.t, gaw_d[:, slot])
        kb.dma("sync", QB, QB.t, qbT_d[:, slot])
        for gi in range(3):
            n = DILS[gi] + 1
            c_lo = 16 + slot - DILS[gi]
            for pr in range(2):
                kb.dma("sync", DK, DK.t[:, DIL_C0[gi]:DIL_C0[gi] + n, pr, :],
                       kbT_d[:, 2 * gi + pr, c_lo * 128:(c_lo + n) * 128].rearrange("p (n k) -> p n k", k=128))
            kb.dma("sync", DV, DV.t[:, DIL_C0[gi]:DIL_C0[gi] + n, :, :], vb_d[:, c_lo:c_lo + n, 4 * gi:4 * gi + 4, :])
        kb.dma("sync", VN, VN.t, vn_d[:, slot]); kb.dma("sync", UU, UU.t, u_d[:, slot])
        if slot + 1 < NSLOT:
            nsa_loads(slot + 1)

        steps = []
        for g in range(2):
            nck = ncmp_chunks(slot)
            for ck in range(nck):
                m = ck - (nck - 2)
                post = (lambda g=g, on=on, p=p: cmp_finish_and_select(g, on, GA[p], KEEP[p], FORCE[p], BEF[p])) if ck == nck - 1 else None
                steps.append(Step([(KCMPT.t[:, ck * 128:(ck + 1) * 128], QA[p].t[:, g, :], 0, 512, [KCMPT, QA[p]])],
                                  [RHSC.t[:, ck, g, :]] * 4, [RHSC], ck == 0, ck == nck - 1, width=321,
                                  mask=(MC[p].t[:, m, :] if m >= 0 else None), mask_bufs=[MC[p]], post=post))
            nsel = 8 * slot + 8
            for kc in range(nsel):
                bias_mm = (EXPM.t[:, (kc % 64) * 128:(kc % 64 + 1) * 128],
                           BPT.t[:, kc // 64, :].unsqueeze(1).broadcast_to([128, 4, 128]), [EXPM, BPT])
                steps.append(Step([(KST.t[:, kc * 128:(kc + 1) * 128], QAR[p].t[:, g, :], 0, 512, [KST, QAR[p]])],
                                  [VSA.t[:, kc, g, :]] * 4, [VSA], kc == 0, False, bias_mm=bias_mm, barrier=(kc == 0)))
            steps.append(Step([(OKS[p].t, QAR[p].t[:, g, :], 0, 512, [OKS[p], QAR[p]])], [OVS[p].t[:, g, :]] * 4, [OVS[p]], False, True,
                              mask=TRI.t, mask_bufs=[TRI],
                              post=(lambda g=g, on=on, p=p: finish_branch(on, 256 * g, (GA[p], lambda j, g=g: 3 * (4 * g + j) + 1, True)))))
            for wi in range(5):
                msk = ATRI.t if wi == 0 else (TRI.t if wi == 4 else None)
                post = (lambda g=g, ol=ol: finish_branch(ol, 256 * g, (GAW, lambda j, g=g: 3 * (4 * g + j) + 2, False))) if wi == 4 else None
                steps.append(Step([(WK.t[:, wi, :], QW.t[:, g, :], 0, 512, [WK, QW])], [WV.t[:, wi, g, :]] * 4, [WV], wi == 0, wi == 4,
                                  mask=msk, mask_bufs=[ATRI, TRI], post=post))
        for cid, (gi, o) in enumerate(DIL_CH):
            pairs = [(DK.t[:, cid, j // 2, :], QB.t[:, 4 * gi + j, :], j * 128, 128, [DK, QB]) for j in range(4)]
            post = (lambda ol=ol: finish_branch(ol, 512, None)) if cid == NDC - 1 else None
            steps.append(Step(pairs, [DV.t[:, cid, j, :] for j in range(4)], [DV], cid == 0, cid == NDC - 1,
                              mask=DM.t[:, cid, :], mask_bufs=[DM], post=post))
        run_steps(steps)
        for g4 in range(4):
            kb.I("tensor", "matmul", pM.t[:, g4 * 128:(g4 + 1) * 128], WST.t[:, g4, :], VN.t[:, g4 * 128:(g4 + 1) * 128], start=True, stop=True,
                 reads=[WST, VN], writes=[pM])
        for g4 in range(4):
            kb.I("vector", "scalar_tensor_tensor", ol.t[:, 768 + g4 * 128:768 + (g4 + 1) * 128], pM.t[:, g4 * 128:(g4 + 1) * 128],
                 BST.t[:, g4:g4 + 1], UU.t[:, g4 * 128:(g4 + 1) * 128], ALU.add, ALU.mult, reads=[pM, BST, UU], writes=[ol])
        kb.dma("sync", None, o_nsa[slot * 128:(slot + 1) * 128, :], on.t, reads=[on], out_dram=True)
        kb.dma("sync", None, o_loc[slot * 128:(slot + 1) * 128, :], ol.t, reads=[ol], out_dram=True)
    kb.finish()
    return kb.nc


def core_tokens(c):
    return np.concatenate([np.arange(128 * (8 * i + c), 128 * (8 * i + c) + 128) for i in range(NSLOT)])


_CONST = {}


def attn_consts():
    if _CONST:
        return _CONST
    f = np.float32
    ci = np.arange(1024)[:, None]; sj = np.arange(256)[None, :]
    M = ((ci >= 4 * sj - 1) & (ci <= 4 * sj + 3)).astype(NPBF)
    _CONST["mmap"] = np.ascontiguousarray(M.reshape(8, 128, 256).transpose(1, 0, 2))
    x = np.arange(8192)[None, :]; j = np.arange(128)[:, None]
    _CONST["expm"] = (j == x // 64).astype(NPBF)
    k = np.arange(128)[:, None]; q = np.arange(128)[None, :]
    _CONST["tri"] = (k <= q).astype(NPBF); _CONST["atri"] = (k > q).astype(NPBF)
    dm = np.zeros((128, NDC, 128), NPBF)
    for cid, (gi, o) in enumerate(DIL_CH):
        dil = DILS[gi]
        dlt = 128 * o + q - k
        dm[:, cid, :] = ((dlt % dil == 0) & (dlt >= 0) & (dlt <= 128 * dil)).astype(NPBF)
    _CONST["dm"] = dm
    _CONST["idn"] = np.eye(128, dtype=f)
    per = []
    for c in range(8):
        maskC = np.zeros((128, NSLOT, 2, 128), NPBF)
        keep = np.zeros((128, NSLOT, 256), f); force = np.zeros((128, NSLOT, 256), f); before = np.zeros((128, NSLOT, 256), f)
        for i in range(NSLOT):
            bb = 8 * i + c
            t = 128 * bb + np.arange(128)
            nck = ncmp_chunks(i)
            for m in range(2):
                ck = nck - 2 + m
                if ck < 0:
                    continue
                cmp_idx = 128 * ck + np.arange(128)
                maskC[:, i, m, :] = (16 * cmp_idx[:, None] + 31 <= t[None, :]).astype(NPBF)
            cur = (t // 64)[:, None]; jj = np.arange(256)[None, :]
            forced = (jj == 0) | (jj == cur) | (jj == cur - 1)
            future = jj * 64 > t[:, None]
            force[:, i, :] = np.where(forced, 1e6, np.where(future, -1.0, 0.0))
            keep[:, i, :] = (~(forced | future)).astype(f)
            before[:, i, :] = (jj < 2 * bb).astype(f)
        per.append(dict(maskC=maskC, keep=keep.astype(NPBF), force=force.astype(NPBF), before=before.astype(NPBF)))
    _CONST["per"] = per
    return _CONST


def prep_attn(zb, zf, P):
    f = np.float32
    C = attn_consts()
    S = zb.shape[0]

    def chunked(a):
        return np.ascontiguousarray(np.moveaxis(a.reshape((a.shape[0] // 128, 128) + a.shape[1:]), 0, 1))

    def with_ones(v):
        return np.concatenate([v, np.ones(v.shape[:-1] + (1,), v.dtype)], -1)

    def padT(cols):
        o = np.zeros((128, 16400), NPBF); o[:, :S] = zb[:, cols:cols + 128].T
        return o

    def w1l(w):
        a = w.reshape(32, 64, 128).transpose(1, 0, 2).reshape(64, 4096)
        o = np.zeros((128, 2, 4096), f)
        o[0:64, 0] = a; o[64:128, 1] = a
        return o

    def zpad_groups(a):
        o = np.zeros(a.shape[:-1] + (2, a.shape[-1]), a.dtype)
        o[0:64, ..., 0, :] = a[0:64]; o[64:128, ..., 1, :] = a[64:128]
        return o
    posk = np.zeros((128, 32), f); posk[0:64] = P["phi_k_pos"].T
    posv = np.zeros((128, 32), f); posv[0:64] = P["phi_v_pos"].T
    ksT = np.ascontiguousarray(zb[:, C_KS:C_KS + 128].T)
    shared = dict(
        kcT=padT(C_KC), vcT=padT(C_VC), w1k=w1l(P["phi_k_w1"]), w1v=w1l(P["phi_v_w1"]), posk=posk, posv=posv,
        w2k=np.ascontiguousarray(np.concatenate([P["phi_k_w2"], P["phi_k_w2"]], 1)), w2v=np.ascontiguousarray(P["phi_v_w2"]),
        mmap=C["mmap"], expm=C["expm"], tri=C["tri"], atri=C["atri"], dm=C["dm"], idn=C["idn"],
        ksT=ksT, vsA=chunked(with_ones(zb[:, C_VS:C_VS + 128].reshape(S, 2, 64))),
        wsT=np.ascontiguousarray(P["sgu_w"].transpose(2, 0, 1)), bsT=np.ascontiguousarray(P["sgu_b"].T),
    )
    Z0 = np.concatenate([np.zeros((NT, zb.shape[1]), NPBF), zb], 0)
    onesp = np.concatenate([np.zeros((NT, 1), NPBF), np.ones((S, 1), NPBF)], 0)

    def qlay(rows, cols):
        a = zb[rows, cols:cols + 512].reshape(NSLOT, 128, 2, 4, 64)
        return zpad_groups(np.ascontiguousarray(a.transpose(2, 4, 0, 3, 1).reshape(128, NSLOT, 512)))
    maps = []
    for c in range(8):
        tok = core_tokens(c)
        loc = np.arange(NT * c, NT * (c + 1))
        win = slice(NT * c, NT * (c + 2))
        d = dict(shared)
        d.update(C["per"][c])
        d["qaT"] = qlay(tok, C_QA); d["qarT"] = qlay(tok, C_QAR); d["qwT"] = qlay(loc, C_QAR)
        d["own_ksT"] = np.ascontiguousarray(ksT[:, tok].reshape(128, NSLOT, 128))
        d["own_vs"] = np.ascontiguousarray(with_ones(zb[tok, C_VS:C_VS + 128].reshape(NSLOT, 128, 2, 64)).transpose(1, 0, 2, 3))
        zw = Z0[win]; ow = onesp[win]
        d["kwT_win"] = np.ascontiguousarray(zw[:, C_KW:C_KW + 128].T)
        vw = np.concatenate([zw[:, C_VW:C_VW + 128].reshape(2 * NT, 2, 64), np.broadcast_to(ow[:, None, :], (2 * NT, 2, 1))], -1)
        d["vw_win"] = chunked(vw)
        d["kbT_win"] = np.ascontiguousarray(zw[:, C_KB:C_KB + 768].T.reshape(6, 128, 2 * NT).transpose(1, 0, 2))
        vb = np.concatenate([zw[:, C_VB:C_VB + 768].reshape(2 * NT, 12, 64), np.broadcast_to(ow[:, None, :], (2 * NT, 12, 1))], -1)
        d["vb_win"] = chunked(vb)
        qb = zb[loc, C_QB:C_QB + 768].reshape(NSLOT, 128, 6, 2, 64).transpose(3, 4, 0, 2, 1)
        qz = np.zeros((128, NSLOT, 12, 128), NPBF)
        for hh in range(2):
            qz[64 * hh:64 * hh + 64, :, hh::2, :] = qb[hh]
        d["qbT"] = qz
        d["ga"] = np.ascontiguousarray(zf[tok, 0:24].reshape(NSLOT, 128, 24).transpose(1, 0, 2))
        d["gaw"] = np.ascontiguousarray(zf[loc, 0:24].reshape(NSLOT, 128, 24).transpose(1, 0, 2))
        d["vn"] = np.ascontiguousarray(zb[loc, C_V:C_V + 512].reshape(NSLOT, 128, 512).transpose(1, 0, 2))
        d["u"] = np.ascontiguousarray(zf[loc, 24:536].reshape(NSLOT, 128, 512).transpose(1, 0, 2))
        maps.append(d)
    return maps


def gather_attn(results):
    S = NT * 8
    o_nsa = np.zeros((S, 512), np.float32); o_loc = np.zeros((S, 1280), np.float32)
    for c in range(8):
        o_nsa[core_tokens(c)] = results[c]["o_nsa"]
        o_loc[NT * c:NT * (c + 1)] = results[c]["o_loc"]
    return o_nsa, o_loc


def build_merge():
    kb = KB()
    x = kb.dram("x", [NT, D], F32, "ExternalInput")
    on_d = kb.dram("o_nsa", [NT, 512], F32, "ExternalInput")
    ol_d = kb.dram("o_loc", [NT, 1280], F32, "ExternalInput")
    wgm_d = kb.dram("wgm", [D, 3 * D], F32, "ExternalInput")
    wa_d = kb.dram("wa", [512, D], F32, "ExternalInput")
    wb_d = kb.dram("wb", [256, D], F32, "ExternalInput")
    wc_d = kb.dram("wc", [512, D], F32, "ExternalInput")
    wo_d = kb.dram("wo", [D, D], F32, "ExternalInput")
    lng = kb.dram("lng", [128, D], F32, "ExternalInput")
    lnb = kb.dram("lnb", [128, D], F32, "ExternalInput")
    idn = kb.dram("idn", [128, 128], F32, "ExternalInput")
    y = kb.dram("y", [NT, D], F32, "ExternalOutput")

    WGM = [kb.sb("WGM%d" % i, [128, 3 * D], BF16) for i in range(8)]
    WA = kb.sb("WA", [128, 4, D], BF16); WB = kb.sb("WB", [128, 2, D], BF16); WC = kb.sb("WC", [128, 4, D], BF16)
    WO = kb.sb("WO", [128, 8, D], BF16)
    G = kb.sb("G", [128, D], F32); Bt = kb.sb("Bt", [128, D], F32)
    ident = kb.sb("ident", [128, 128], F32)
    xs = [kb.sb("xs%d" % i, [128, D], F32) for i in range(2)]
    ons = [kb.sb("ons%d" % i, [128, 512], F32) for i in range(2)]
    ols = [kb.sb("ols%d" % i, [128, 1280], F32) for i in range(2)]
    xTs = [kb.sb("xT%d" % i, [128, 8, 128], BF16) for i in range(2)]
    oTs = [kb.sb("oT%d" % i, [128, 14, 128], BF16) for i in range(2)]
    GMs = [kb.sb("GM%d" % i, [128, 3 * D], F32) for i in range(2)]
    mgs = [kb.sb("mg%d" % i, [128, D], F32) for i in range(2)]
    tmp = [kb.sb("tmp%d" % i, [128, 512], F32) for i in range(2)]
    mTs = [kb.sb("mT%d" % i, [128, 8, 128], BF16) for i in range(2)]
    rr = [kb.sb("rr%d" % i, [128, D], F32) for i in range(2)]
    oo = [kb.sb("oo%d" % i, [128, D], F32) for i in range(2)]
    stats = kb.sb("stats", [128, 12], F32); mv = kb.sb("mv", [128, 2], F32); rstd = kb.sb("rstd", [128, 1], F32)
    pt = [kb.ps("pt%d" % i, [128, 512], F32) for i in range(2)]
    pg = [kb.ps("pg%d" % i, [128, 512], F32) for i in range(2)]
    py = [kb.ps("py%d" % i, [128, 512], F32) for i in range(2)]
    po = [kb.ps("po%d" % i, [128, 512], F32) for i in range(2)]

    kb.dma("sync", ident, ident.t, idn); kb.dma("sync", G, G.t, lng); kb.dma("sync", Bt, Bt.t, lnb)

    def load(s):
        p = s % 2
        kb.dma("sync", xs[p], xs[p].t, x[s * 128:(s + 1) * 128, :])
        kb.dma("sync", ons[p], ons[p].t, on_d[s * 128:(s + 1) * 128, :])
        kb.dma("sync", ols[p], ols[p].t, ol_d[s * 128:(s + 1) * 128, :])
    load(0)
    for i in range(8):
        kb.dma("gpsimd", WGM[i], WGM[i].t, wgm_d[i * 128:(i + 1) * 128, :])
    kb.dma("gpsimd", WA, WA.t, wa_d.rearrange("(a p) d -> p a d", p=128))
    kb.dma("gpsimd", WB, WB.t, wb_d.rearrange("(a p) d -> p a d", p=128))
    kb.dma("gpsimd", WC, WC.t, wc_d.rearrange("(a p) d -> p a d", p=128))
    kb.dma("gpsimd", WO, WO.t, wo_d.rearrange("(a p) d -> p a d", p=128))
    eps2 = LN_EPS / (ALPHA * ALPHA)
    nsub = NT // 128
    tcount = [0]

    def transpose_group(srcs, dst, dst0):
        pb = pt[tcount[0] % 2]
        tcount[0] += 1
        for i, (bf, c0) in enumerate(srcs):
            kb.I("tensor", "transpose", pb.t[:, i * 128:(i + 1) * 128], bf.t[:, c0:c0 + 128], ident.t, reads=[bf, ident], writes=[pb])
        n = len(srcs)
        kb.I("scalar", "copy", dst.t[:, dst0:dst0 + n, :], pb.t[:, 0:n * 128].rearrange("p (a b) -> p a b", a=n), reads=[pb], writes=[dst])

    for s in range(nsub):
        p = s % 2
        xT, oT, GM, mg, mT = xTs[p], oTs[p], GMs[p], mgs[p], mTs[p]
        if s + 1 < nsub:
            load(s + 1)
        for hh in range(2):
            transpose_group([(xs[p], (hh * 4 + j) * 128) for j in range(4)], xT, hh * 4)
        transpose_group([(ons[p], j * 128) for j in range(4)], oT, 0)
        transpose_group([(ols[p], j * 128) for j in range(4)], oT, 4)
        transpose_group([(ols[p], (4 + j) * 128) for j in range(4)], oT, 8)
        transpose_group([(ols[p], (8 + j) * 128) for j in range(2)], oT, 12)
        for k in range(6):
            pb = pg[k % 2]
            for dc in range(8):
                kb.I("tensor", "matmul", pb.t, xT.t[:, dc, :], WGM[dc].t[:, k * 512:(k + 1) * 512], start=(dc == 0), stop=(dc == 7),
                     reads=[xT, WGM[dc]], writes=[pb])
            kb.I("scalar", "activation", GM.t[:, k * 512:(k + 1) * 512], pb.t, AF.Sigmoid, reads=[pb], writes=[GM])
        yc = 0
        for hh in range(2):
            cs = slice(hh * 512, (hh + 1) * 512)
            pb = py[yc % 2]; yc += 1
            for kc in range(8):
                kb.I("tensor", "matmul", pb.t, oT.t[:, kc, :], WA.t[:, kc % 4, cs], start=(kc == 0), stop=(kc == 7), reads=[oT, WA], writes=[pb])
            kb.I("vector", "tensor_tensor", mg.t[:, cs], pb.t, GM.t[:, hh * 512:(hh + 1) * 512], ALU.mult, reads=[pb, GM], writes=[mg])
            pb = py[yc % 2]; yc += 1
            for kc in range(2):
                kb.I("tensor", "matmul", pb.t, oT.t[:, 8 + kc, :], WB.t[:, kc, cs], start=(kc == 0), stop=(kc == 1), reads=[oT, WB], writes=[pb])
            kb.I("vector", "tensor_tensor", tmp[0].t, pb.t, GM.t[:, D + hh * 512:D + (hh + 1) * 512], ALU.mult, reads=[pb, GM], writes=[tmp[0]])
            kb.I("gpsimd", "tensor_tensor", mg.t[:, cs], mg.t[:, cs], tmp[0].t, ALU.add, reads=[mg, tmp[0]], writes=[mg])
            pb = py[yc % 2]; yc += 1
            for kc in range(4):
                kb.I("tensor", "matmul", pb.t, oT.t[:, 10 + kc, :], WC.t[:, kc, cs], start=(kc == 0), stop=(kc == 3), reads=[oT, WC], writes=[pb])
            kb.I("vector", "tensor_tensor", tmp[1].t, pb.t, GM.t[:, 2 * D + hh * 512:2 * D + (hh + 1) * 512], ALU.mult, reads=[pb, GM], writes=[tmp[1]])
            kb.I("gpsimd", "tensor_tensor", mg.t[:, cs], mg.t[:, cs], tmp[1].t, ALU.add, reads=[mg, tmp[1]], writes=[mg])
        for hh in range(2):
            transpose_group([(mg, (hh * 4 + j) * 128) for j in range(4)], mT, hh * 4)
        rb = rr[p]; ob = oo[p]
        for hh in range(2):
            for dc in range(8):
                kb.I("tensor", "matmul", po[hh].t, mT.t[:, dc, :], WO.t[:, dc, hh * 512:(hh + 1) * 512], start=(dc == 0), stop=(dc == 7),
                     reads=[mT, WO], writes=[po[hh]])
            kb.I("vector", "scalar_tensor_tensor", rb.t[:, hh * 512:(hh + 1) * 512], po[hh].t, 1.0 / ALPHA, xs[p].t[:, hh * 512:(hh + 1) * 512],
                 ALU.mult, ALU.add, reads=[po[hh], xs[p]], writes=[rb])
        emit_ln(kb, rb, ob, G, Bt, stats, mv, rstd, eps2)
        kb.dma("sync", None, y[s * 128:(s + 1) * 128, :], ob.t, reads=[ob], out_dram=True)
    kb.finish()
    return kb.nc


def _rope_tables_np():
    pos = np.arange(16384, dtype=np.float32)
    inv = (1.0 / (np.float32(10000.0) ** (np.arange(0, 64, 2, dtype=np.float32) / np.float32(64)))).astype(np.float32)
    ang = pos[:, None] * inv[None, :]
    cos, sin = np.cos(ang).astype(np.float32), np.sin(ang).astype(np.float32)
    return np.concatenate([cos, cos], 1), np.concatenate([-sin, sin], 1)


_NC = {}


def _prog(name, builder):
    if name not in _NC:
        _NC[name] = builder()
    return _NC[name]


def _run(name, builder, maps):
    return run_bass_kernel_spmd(builder(), maps, core_ids=list(range(8))).results


def _rows(a, c):
    return np.ascontiguousarray(a[NT * c:NT * (c + 1)])


def _bc(v):
    return np.ascontiguousarray(np.broadcast_to(np.asarray(v, np.float32), (128, v.shape[-1])))


def kernel(**inputs):
    f = np.float32
    x = np.asarray(inputs["x"], f)[0]
    idn = np.eye(128, dtype=f)
    CC, SS = _rope_tables_np()
    for l in range(DEPTH):
        P = {k: np.asarray(v[l], f) for k, v in inputs.items() if k != "x"}
        maps = [dict(x=_rows(x, c), wg=P["ffn1_gate"], wu=P["ffn1_up"], wd=P["ffn1_down"],
                     lng=_bc(P["ln_g"][0]), lnb=_bc(P["ln_b"][0]), idn=idn) for c in range(8)]
        x1 = np.concatenate([r["y"] for r in _run("ffn", build_ffn, maps)], 0)
        win = np.ascontiguousarray(P["w_in"][:, :ZC])
        maps = [dict(x=_rows(x1, c), win=win, idn=idn, cc=_rows(CC, c), ss=_rows(SS, c),
                     sg=_bc(P["sgu_ln_g"]), sb=_bc(P["sgu_ln_b"])) for c in range(8)]
        res = _run("win", build_win, maps)
        zb = np.concatenate([r["zb"] for r in res], 0)
        zf = np.concatenate([r["zf"] for r in res], 0)
        res = _run("attn", build_attn, prep_attn(zb, zf, P))
        o_nsa, o_loc = gather_attn(res)
        wgm = np.ascontiguousarray(P["w_in"][:, ZC:])
        maps = [dict(x=_rows(x1, c), o_nsa=_rows(o_nsa, c), o_loc=_rows(o_loc, c), wgm=wgm, wa=P["w_branch_a"], wb=P["w_branch_b"],
                     wc=P["w_branch_c"], wo=P["w_out"], lng=_bc(P["ln_g"][1]), lnb=_bc(P["ln_b"][1]), idn=idn) for c in range(8)]
        x2 = np.concatenate([r["y"] for r in _run("merge", build_merge, maps)], 0)
        maps = [dict(x=_rows(x2, c), wg=P["ffn2_gate"], wu=P["ffn2_up"], wd=P["ffn2_down"],
                     lng=_bc(P["ln_g"][2]), lnb=_bc(P["ln_b"][2]), idn=idn) for c in range(8)]
        x = np.concatenate([r["y"] for r in _run("ffn", build_ffn, maps)], 0)
    return x[None].astype(np.float32)
```
